# Optimizing a Trainium2 kernel written in Bass

```python
import jax, jax.numpy as jnp
from jax import lax
import numpy as np

D_MODEL = 1024
BATCH = 4
SEQ = 8192
DEPTH = 1

N_MEM = 256
HEAD_DIM = 64
N_ATTN_HEADS = 8
N_MEM_HEADS = 4
POOL_WINDOWS = (2, 4, 8, 16)
N_POOL_GROUPS = 4
POOL_GROUP_DIM = 64
D_ATTN = N_ATTN_HEADS * HEAD_DIM
D_POOL = N_POOL_GROUPS * POOL_GROUP_DIM
D_XMEM = N_MEM_HEADS * HEAD_DIM
D_MIX = D_ATTN + D_POOL + D_XMEM
D_IN = 3 * D_ATTN + D_POOL + D_XMEM
DILATED = ((128, 1), (512, 4), (2048, 16))
BLK = 128
D_FF = 4 * D_MODEL
EPS = 1e-6

kernel_name = 'hymba_dilated_pool_memory_layer'


def rmsnorm(x, g):
    xf = x.astype(jnp.float32)
    xf = xf * lax.rsqrt(jnp.mean(xf * xf, axis=-1, keepdims=True) + EPS)
    return xf.astype(x.dtype) * g


def alibi_slopes(n_heads):
    return 2.0 ** (-8.0 * jnp.arange(1, n_heads + 1, dtype=jnp.float32) / n_heads)


def band_blocks(t, n_prev, nb):
    N, H, _, Dh = t.shape
    tp = jnp.pad(t, ((0, 0), (0, 0), (n_prev * BLK, 0), (0, 0)))
    tb = tp.reshape(N, H, nb + n_prev, BLK, Dh)
    return jnp.concatenate([tb[:, :, o:o + nb] for o in range(n_prev + 1)], axis=3)


def dilated_window_attention(q, k, v, window, dilation, slopes):
    B, S, H, Dh = q.shape
    L = S // dilation
    Lp = -(-L // BLK) * BLK
    nb = Lp // BLK
    span = window // dilation
    n_prev = -(-span // BLK)
    N = B * dilation

    def to_sub(t):
        t = t.reshape(B, L, dilation, H, Dh).transpose(0, 2, 3, 1, 4).reshape(N, H, L, Dh)
        return jnp.pad(t, ((0, 0), (0, 0), (0, Lp - L), (0, 0)))

    qb = to_sub(q).reshape(N, H, nb, BLK, Dh)
    kb = band_blocks(to_sub(k), n_prev, nb)
    vb = band_blocks(to_sub(v), n_prev, nb)
    KB = (n_prev + 1) * BLK
    s = jnp.einsum('nhbqd,nhbkd->nhbqk', qb, kb).astype(jnp.float32) * (Dh ** -0.5)
    qi = jnp.arange(BLK)[:, None]
    ki = jnp.arange(KB)[None, :]
    rel = n_prev * BLK + qi - ki
    key_idx = (jnp.arange(nb)[:, None, None] - n_prev) * BLK + ki[None]
    valid = (rel >= 0) & (rel <= span) & (key_idx >= 0)
    bias = -slopes[:, None, None, None] * (dilation * rel).astype(jnp.float32)
    s = jnp.where(valid, s + bias, -jnp.inf)
    m = jnp.max(s, axis=-1, keepdims=True)
    p = jnp.exp(s - m)
    den = jnp.sum(p, axis=-1, keepdims=True)
    o = jnp.einsum('nhbqk,nhbkd->nhbqd', (p / den).astype(v.dtype), vb)
    lse = (m + jnp.log(den))[..., 0]
    o = o.reshape(N, H, Lp, Dh)[:, :, :L].reshape(B, dilation, H, L, Dh)
    o = o.transpose(0, 3, 1, 2, 4).reshape(B, S, H, Dh)
    lse = lse.reshape(N, H, Lp)[:, :, :L].reshape(B, dilation, H, L)
    lse = lse.transpose(0, 3, 1, 2).reshape(B, S, H)
    return o, lse


def causal_multiscale_pool(u):
    B, S, _ = u.shape
    ug = u.astype(jnp.float32).reshape(B, S, N_POOL_GROUPS, POOL_GROUP_DIM)
    cs = jnp.pad(jnp.cumsum(ug, axis=1), ((0, 0), (1, 0), (0, 0), (0, 0)))
    t = jnp.arange(S)
    outs = []
    for g, w in enumerate(POOL_WINDOWS):
        lo = jnp.maximum(t + 1 - w, 0)
        total = cs[:, 1:, g] - cs[:, lo, g]
        cnt = (t + 1 - lo).astype(jnp.float32)
        outs.append(total / cnt[None, :, None])
    pooled = jnp.stack(outs, axis=2)
    return (pooled - ug).astype(u.dtype)


def setup_inputs(seed: int = 0) -> dict:
    key = jax.random.key(seed)
    ks = jax.random.split(key, 13)

    def nrm(k, shape, fan_in):
        return jax.random.normal(k, shape, jnp.float32) * (fan_in ** -0.5)

    def gain(k, shape):
        return 1.0 + 0.1 * jax.random.normal(k, shape, jnp.float32)

    return {
        'x': jax.random.normal(ks[0], (BATCH, SEQ, D_MODEL), jnp.float32),
        'mem': jax.random.normal(ks[1], (BATCH, N_MEM, D_MODEL), jnp.float32),
        'g_mix': gain(ks[2], (DEPTH, D_MODEL)),
        'w_in': nrm(ks[3], (DEPTH, D_MODEL, D_IN), D_MODEL),
        'g_mem': gain(ks[4], (DEPTH, D_MODEL)),
        'w_mem_kv': nrm(ks[5], (DEPTH, D_MODEL, 2 * D_XMEM), D_MODEL),
        'w_pool': nrm(ks[6], (DEPTH, N_POOL_GROUPS, POOL_GROUP_DIM, POOL_GROUP_DIM), POOL_GROUP_DIM),
        'pool_scale': gain(ks[7], (DEPTH, D_POOL)),
        'w_out': nrm(ks[8], (DEPTH, D_MIX, D_MODEL), D_MIX),
        'g_ffn': gain(ks[9], (DEPTH, D_MODEL)),
        'w_ff1': nrm(ks[10], (DEPTH, D_MODEL, D_FF), D_MODEL),
        'w_ff2': nrm(ks[11], (DEPTH, D_FF, D_MODEL), D_FF),
        'g_final': gain(ks[12], (D_MODEL,)),
    }


def reference(x, mem, g_mix, w_in, g_mem, w_mem_kv, w_pool, pool_scale, w_out,
              g_ffn, w_ff1, w_ff2, g_final):
    B, S, _ = x.shape
    slopes = alibi_slopes(N_ATTN_HEADS)
    for i in range(DEPTH):
        h = rmsnorm(x, g_mix[i])
        proj = h @ w_in[i]
        q, k, v, u, qm = jnp.split(
            proj, [D_ATTN, 2 * D_ATTN, 3 * D_ATTN, 3 * D_ATTN + D_POOL], axis=-1)
        q = q.reshape(B, S, N_ATTN_HEADS, HEAD_DIM)
        k = k.reshape(B, S, N_ATTN_HEADS, HEAD_DIM)
        v = v.reshape(B, S, N_ATTN_HEADS, HEAD_DIM)

        outs, lses = [], []
        for window, dilation in DILATED:
            o, l = dilated_window_attention(q, k, v, window, dilation, slopes)
            outs.append(o)
            lses.append(l)
        wts = jax.nn.softmax(jnp.stack(lses, axis=0), axis=0)
        y_attn = jnp.sum(wts[..., None] * jnp.stack(outs, axis=0).astype(jnp.float32), axis=0)
        y_attn = y_attn.astype(x.dtype).reshape(B, S, D_ATTN)

        d = causal_multiscale_pool(u)
        y_pool = jnp.einsum('bsgc,gce->bsge', d, w_pool[i]).reshape(B, S, D_POOL) * pool_scale[i]

        mn = rmsnorm(mem, g_mem[i])
        km, vm = jnp.split(mn @ w_mem_kv[i], 2, axis=-1)
        km = km.reshape(B, N_MEM, N_MEM_HEADS, HEAD_DIM)
        vm = vm.reshape(B, N_MEM, N_MEM_HEADS, HEAD_DIM)
        qm = qm.reshape(B, S, N_MEM_HEADS, HEAD_DIM)
        sm = jnp.einsum('bshd,bmhd->bhsm', qm, km).astype(jnp.float32) * (HEAD_DIM ** -0.5)
        pm = jax.nn.softmax(sm, axis=-1).astype(vm.dtype)
        y_mem = jnp.einsum('bhsm,bmhd->bshd', pm, vm).reshape(B, S, D_XMEM)

        y = jnp.concatenate([y_attn, y_pool, y_mem], axis=-1)
        x = x + y @ w_out[i]

        h2 = rmsnorm(x, g_ffn[i])
        a = jax.nn.relu(h2 @ w_ff1[i])
        x = x + (a * a) @ w_ff2[i]
    return rmsnorm(x, g_final)
```

```python
import numpy as np
from contextlib import ExitStack

import concourse.bass as bass
import concourse.mybir as mybir
from concourse.bass_utils import run_bass_kernel_spmd

F32 = mybir.dt.float32
BF16 = mybir.dt.bfloat16
AF = mybir.ActivationFunctionType
ALU = mybir.AluOpType

D = 1024
SEQ = 8192
NB = 4
TOK = 512
NHALO = 4
NOWN = 8
EPS = 1e-6
NRING = 3
KSLOTS = 5

P_KVM = 0
P_WIN = 1
P_WOUT = 5
P_FF1 = 7
P_FF2 = 15
NPIECES = 23
FFN_ORDER = [("ff1", 0), ("ff1", 1), ("ff2", 0), ("ff1", 2), ("ff2", 1), ("ff1", 3), ("ff2", 2), ("ff2", 3)]


class Buf:
    __slots__ = ("name", "w", "r")

    def __init__(self, name):
        self.name = name
        self.w = None
        self.r = []


class DmaSem:
    __slots__ = ("sem", "val")

    def __init__(self, sem):
        self.sem = sem
        self.val = 0


class Tracker:
    def __init__(self, nc, es):
        self.nc = nc
        self.eng = {"pe": nc.tensor, "act": nc.scalar, "dve": nc.vector,
                    "pool": nc.gpsimd, "sp": nc.sync}
        self.sem = {k: es.enter_context(nc.semaphore("sem_" + k)) for k in self.eng}
        self.cnt = {k: 0 for k in self.eng}
        self.waited = {}
        self.nwaits = 0
        self.ninst = {k: 0 for k in self.eng}

    def wait(self, e, tok):
        if tok is None:
            return
        sem, val, prod = tok
        if prod == "pe" and e == "pe":
            return
        key = (e, id(sem))
        if self.waited.get(key, 0) >= val:
            return
        self.waited[key] = val
        self.eng[e].wait_ge(sem, val)
        self.nwaits += 1

    def _pre(self, e, reads, writes):
        for b in reads:
            self.wait(e, b.w)
        for b in writes:
            self.wait(e, b.w)
            for t in b.r:
                self.wait(e, t)

    def _post(self, tok, reads, writes):
        for b in reads:
            b.r = [t for t in b.r if t[0] is not tok[0]] + [tok]
        for b in writes:
            b.w = tok
            b.r = []

    def op(self, e, fn, reads=(), writes=(), mark=True):
        self._pre(e, reads, writes)
        ins = fn()
        self.ninst[e] += 1
        if mark:
            self.cnt[e] += 1
            ins.then_inc(self.sem[e], 1)
            tok = (self.sem[e], self.cnt[e], e)
        else:
            tok = (self.sem[e], self.cnt[e] + 1, e)
        self._post(tok, reads, writes)
        return tok

    def dma(self, e, dsem, out, in_, reads=(), writes=(), **kw):
        self._pre(e, reads, writes)
        ins = self.eng[e].dma_start(out=out, in_=in_, **kw)
        self.ninst[e] += 1
        dsem.val += 16
        ins.then_inc(dsem.sem, 16)
        tok = (dsem.sem, dsem.val, "dma")
        self._post(tok, reads, writes)
        return tok


def build_nc(n_own=NOWN, debug=False):
    nc = bass.Bass("TRN2", target_bir_lowering=False)
    NT = NHALO + n_own

    def din(name, shape, dtype=F32):
        return nc.dram_tensor(name, shape, dtype, kind="ExternalInput").ap()

    xh = din("xh", [(NHALO + NOWN) * TOK, D])
    memd = din("mem", [256, D])
    w_in = din("w_in", [D, 2048])
    w_kvm = din("w_mem_kv", [D, 512])
    w_out = din("w_out", [D, D])
    w_ff1 = din("w_ff1", [D, 4096])
    w_ff2 = din("w_ff2", [4096, D])
    gbs = din("gbs", [4, 128, D])
    cE1 = din("cE1", [128, 8, 2, 128])
    cE16 = din("cE16", [128, 8, 5, 128])
    cwpool = din("cwpool", [128, 2, 128])
    csm = din("csm", [128, 8])
    cpc = din("cpc", [128, 2, 16])
    outd = nc.dram_tensor("out", [NOWN * TOK, D], F32, kind="ExternalOutput").ap()
    sc = nc.dram_tensor("wscratch", [NPIECES, 128, 4096], BF16).ap()
    dbg = {}
    if debug:
        for nm, shp in debug.items():
            dbg[nm] = nc.dram_tensor("dbg_" + nm, shp, F32, kind="ExternalOutput").ap()

    with ExitStack() as es:
        T = Tracker(nc, es)

        def sb(name, shape, dtype):
            return es.enter_context(nc.sbuf_tensor(name, shape, dtype))

        def psum(name, shape, dtype):
            return es.enter_context(nc.psum_tensor(name, shape, dtype))

        def dsem(name):
            return DmaSem(es.enter_context(nc.semaphore(name)))

        ring = [sb(f"ring{i}", [128, 4096], BF16) for i in range(NRING)]
        b_ring = [Buf(f"ring{i}") for i in range(NRING)]
        ring_sem = [dsem(f"ringsem{i}") for i in range(NRING)]
        KT = sb("KT", [128, 4, KSLOTS * TOK], BF16)
        b_KT = [Buf(f"KT{i}") for i in range(KSLOTS)]
        Vd1 = sb("Vd1", [128, 5, 4 * 192], BF16)
        b_Vd1 = [Buf(f"Vd1_{i}") for i in range(5)]
        Vd4 = sb("Vd4", [128, KSLOTS, 4, 4 * 192], BF16)
        b_Vd4 = [Buf(f"Vd4_{i}") for i in range(KSLOTS)]
        big = sb("big", [128, 8192], BF16)
        b_big = [Buf("bigA"), Buf("bigB")]
        aT = [big[:, 0:4096].rearrange("p (f t) -> p f t", f=8),
              big[:, 4096:8192].rearrange("p (f t) -> p f t", f=8)]
        QA = big[:, 0:2048].rearrange("p (c t) -> p c t", c=4)
        QB = big[:, 2048:4096].rearrange("p (c t) -> p c t", c=4)
        qmA = big[:, 4096:5120].rearrange("p (c t) -> p c t", c=2)
        qmB = big[:, 5120:6144].rearrange("p (c t) -> p c t", c=2)
        VT = big[:, 6144:8192].rearrange("p (c t) -> p c t", c=4)
        U = sb("U", [128, 2, 16 + TOK], F32)
        b_U = Buf("U")
        S1 = sb("S1", [128, 16 + TOK], F32)
        S2 = sb("S2", [128, 16 + TOK], F32)
        b_S1, b_S2 = Buf("S1"), Buf("S2")
        dT = sb("dT", [128, 2, TOK], BF16)
        b_dT = Buf("dT")
        yT = sb("yT", [128, 8, TOK], BF16)
        b_yT = [Buf(f"yT{i}") for i in range(8)]
        E1 = sb("E1", [128, 8, 2, 128], BF16)
        E16 = sb("E16", [128, 8, 5, 128], BF16)
        b_E = Buf("E")
        xa = [sb(f"xa{i}", [128, D], F32) for i in range(2)]
        b_xa = [Buf(f"xa{i}") for i in range(2)]
        xa_sem = [dsem(f"xasem{i}") for i in range(2)]
        x1 = [sb(f"x1_{i}", [128, D], F32) for i in range(4)]
        b_x1 = [Buf(f"x1_{i}") for i in range(4)]
        x1_sem = [dsem(f"x1sem{i}") for i in range(4)]
        out_sem = [dsem(f"outsem{i}") for i in range(4)]
        hbs = [sb(f"hb{i}", [128, D], BF16) for i in range(2)]
        b_hbs = [Buf(f"hb{i}") for i in range(2)]
        junk = sb("junk", [128, D], BF16)
        hT = sb("hT", [128, 8, TOK], BF16)
        b_hT = Buf("hT")
        h2T = sb("h2T", [128, 8, TOK], BF16)
        b_h2T = Buf("h2T")
        NPB = 5
        Pbuf = [sb(f"P{i}", [128, 4, 128], BF16) for i in range(NPB)]
        b_P = [Buf(f"P{i}") for i in range(NPB)]
        rec = [sb(f"rec{i}", [128, TOK], F32) for i in range(2)]
        b_rec = [Buf(f"rec{i}") for i in range(2)]
        gb_mix = sb("gb_mix", [128, D], F32)
        gb_ffn = sb("gb_ffn", [128, D], F32)
        gb_fin = sb("gb_fin", [128, D], F32)
        b_g = Buf("g")
        kmT = sb("kmT", [128, 2, 256], BF16)
        vmA = sb("vmA", [128, 2, 2 * 192], BF16)
        b_km, b_vm = Buf("km"), Buf("vm")
        wpool = sb("wpool", [128, 2, 128], BF16)
        csm_t = sb("csm_t", [128, 8], F32)
        cpc_t = sb("cpc_t", [128, 2, 16], F32)
        b_c = Buf("consts")
        ident = sb("ident", [128, 128], BF16)
        identf = sb("identf", [128, 128], F32)
        b_id = Buf("ident")
        stat = sb("stat", [128, 3, 8], F32)
        b_stat = [Buf(f"stat{i}") for i in range(8)]

        mm = [psum(f"mm{i}", [128, 512], F32) for i in range(2)]
        b_mm = [Buf(f"mm{i}") for i in range(2)]
        scb = [psum(f"sc{i}", [128, 4, 128], F32) for i in range(3)]
        b_sc = [Buf(f"sc{i}") for i in range(3)]
        scb = [t_[:] for t_ in scb] + [m_[:].rearrange("p (a b) -> p a b", a=4) for m_ in mm]
        b_sc = b_sc + b_mm
        NSC = 5
        ob = [psum(f"ob{i}", [128, 4, 128], F32) for i in range(2)]
        b_ob = [Buf(f"ob{i}") for i in range(2)]
        tp = psum("tp", [128, 1024], BF16)
        b_tp = Buf("tp")
        b_tph = [b_tp, b_tp]

        b_sc_piece = [Buf(f"scp{i}") for i in range(NPIECES)]
        conv_sem = [dsem(f"cv{i}") for i in range(NPIECES)]

        def conv(idx, src, pat_out, kw):
            T.dma("pool", conv_sem[idx], sc[idx].rearrange(pat_out, **kw), src, writes=[b_sc_piece[idx]])

        def conv_k(idx, wsrc, c0):
            conv(idx, wsrc[:, c0:c0 + 512].rearrange("(k p) c -> p k c", p=128), "p (k c) -> p k c", dict(k=8))

        hv_col = csm_t[:, 0:1]

        T.op("pool", lambda: nc.gpsimd.memset(identf[:], 1.0), writes=[b_id])
        T.op("pool", lambda: nc.gpsimd.affine_select(out=identf[:], in_=identf[:], pattern=[[-1, 128]],
                                                     compare_op=ALU.is_equal, fill=0.0, base=0,
                                                     channel_multiplier=1), writes=[b_id])
        T.op("pool", lambda: nc.gpsimd.tensor_copy(out=ident[:], in_=identf[:]), reads=[b_id], writes=[b_id])
        T.op("pool", lambda: nc.gpsimd.memset(U[:], 0.0), writes=[b_U])
        T.op("pool", lambda: nc.gpsimd.memset(vmA[:], 1.0), writes=[b_vm])

        g_sem = dsem("gsem")
        T.dma("sp", g_sem, gb_mix[:], gbs[0], writes=[b_g])
        T.dma("sp", g_sem, csm_t[:], csm[:, :], writes=[b_g])
        xa_state = dict(n=0)

        def load_xa_rows(src_rows):
            s_ = xa_state["n"] % 2
            xa_state["n"] += 1
            T.dma("sp", xa_sem[s_], xa[s_][:], src_rows, writes=[b_xa[s_]])
            return s_

        def load_xa(j, c):
            r0 = j * TOK + c * 128
            return load_xa_rows(xh[r0:r0 + 128, :])

        a_slots = {}

        def stage_A_begin(j):
            a_slots[j] = [load_xa(j, 0), load_xa(j, 1)]

        stage_A_begin(0)
        T.dma("sp", g_sem, cpc_t[:], cpc[:, :, :], writes=[b_g])
        T.dma("sp", g_sem, x1[0][:], gbs[3], writes=[b_x1[0]])
        T.dma("sp", g_sem, gb_ffn[:], gbs[1], writes=[b_g])
        T.dma("sp", g_sem, gb_fin[:], gbs[2], writes=[b_g])
        totg = (g_sem.sem, g_sem.val, "dma")
        b_g.w = totg
        b_x1[0].w = totg

        def direct_piece(dst2d, b_dst, dsem_, wsrc, c0, ncols=512):
            T.dma("pool", dsem_, dst2d.rearrange("p (k c) -> p k c", k=8),
                  wsrc[:, c0:c0 + ncols].rearrange("(k p) c -> p k c", p=128), writes=b_dst)

        halo_pieces = {}
        for pi, slot in ((1, 1), (2, 2), (3, 0)):
            direct_piece(ring[slot][:], [b_ring[slot]], ring_sem[slot], w_in, 512 * pi)
            halo_pieces[pi] = (ring[slot], b_ring[slot])
        kvm_sem = dsem("kvmsem")
        kvm_piece = yT[:].rearrange("p a t -> p (a t)")
        direct_piece(kvm_piece, b_yT, kvm_sem, w_kvm, 0)
        c_sem = dsem("csem")
        T.dma("pool", c_sem, E1[:], cE1[:, :, :, :], writes=[b_E])
        T.dma("pool", c_sem, E16[:], cE16[:, :, :, :], writes=[b_E])
        T.dma("pool", c_sem, wpool[:], cwpool[:, :, :], writes=[b_c])
        tot = (c_sem.sem, c_sem.val, "dma")
        b_E.w = tot
        b_c.w = tot
        for i in (0, 1, 2, 3):
            conv_k(P_WIN + i, w_in, 512 * i)
        for i in range(2):
            conv_k(P_WOUT + i, w_out, 512 * i)
        for kind, q in FFN_ORDER:
            for i in (2 * q, 2 * q + 1):
                if kind == "ff1":
                    conv_k(P_FF1 + i, w_ff1, 512 * i)
                else:
                    conv(P_FF2 + i, w_ff2[512 * i:512 * i + 512, :].rearrange("(f p) c -> p f c", p=128),
                         "p (f c) -> p f c", dict(f=4))

        seq = []
        for j in range(NT):
            if j < NHALO:
                pass
            else:
                seq += [P_WIN + 0, P_WIN + 1, P_WIN + 2, P_WIN + 3, P_WOUT, P_WOUT + 1]
                for kind, q in FFN_ORDER:
                    base = P_FF1 if kind == "ff1" else P_FF2
                    seq += [base + 2 * q, base + 2 * q + 1]
        rs = dict(issued=0, taken=0)

        def ring_issue():
            n = rs["issued"]
            if n >= len(seq):
                return
            s = n % NRING
            idx = seq[n]
            T.dma("sp", ring_sem[s], ring[s][:], sc[idx], reads=[b_sc_piece[idx]], writes=[b_ring[s]])
            rs["issued"] = n + 1

        def ring_take(expect):
            n = rs["taken"]
            assert seq[n] == expect, (n, seq[n], expect)
            while rs["issued"] <= n:
                ring_issue()
            rs["taken"] = n + 1
            s = n % NRING
            return ring[s], b_ring[s]


        cnt = dict(stat=0, mm=0, sc=0, P=0, tp=0, hb=0, rec=0)

        def rms_hb(src, b_src, gb, extra_reads=()):
            hi = cnt["hb"] % 2
            cnt["hb"] += 1
            dst_hb, b_dst = hbs[hi][:], b_hbs[hi]
            i = cnt["stat"] % 8
            cnt["stat"] += 1
            bs = b_stat[i]
            ss, ln, rs_ = stat[:, 0, i:i + 1], stat[:, 1, i:i + 1], stat[:, 2, i:i + 1]
            T.op("act", lambda: nc.scalar.activation(out=junk[:], in_=src, func=AF.Square, accum_out=ss),
                 reads=[b_src], writes=[bs])
            T.op("act", lambda: nc.scalar.activation(out=ln, in_=ss, func=AF.Ln, scale=1.0 / D, bias=EPS),
                 reads=[bs], writes=[bs])
            T.op("act", lambda: nc.scalar.activation(out=rs_, in_=ln, func=AF.Exp, scale=-0.5),
                 reads=[bs], writes=[bs])
            T.op("dve", lambda: nc.vector.scalar_tensor_tensor(out=dst_hb, in0=src, scalar=rs_, in1=gb,
                                                               op0=ALU.mult, op1=ALU.mult),
                 reads=[b_src, bs, b_g] + list(extra_reads), writes=[b_dst])
            return hbs[hi], b_dst

        def transpose_to(hb, b_hb, dst3, b_dst):
            for k in range(8):
                T.op("pe", lambda: nc.tensor.transpose(tp[:, k * 128:(k + 1) * 128], hb[:, k * 128:(k + 1) * 128],
                                                       ident[:]),
                     reads=[b_hb, b_id], writes=[b_tp], mark=(k == 7))
            T.op("dve", lambda: nc.vector.tensor_copy(out=dst3, in_=tp[:].rearrange("p (k t) -> p k t", k=8)),
                 reads=[b_tp], writes=[b_dst])

        def next_mm():
            i = cnt["mm"] % 2
            cnt["mm"] += 1
            return mm[i], b_mm[i]

        def mm_group(out_ap, b_out, pairs, reads):
            n = len(pairs)
            for i, (l, r) in enumerate(pairs):
                T.op("pe", lambda: nc.tensor.matmul(out_ap, lhsT=l, rhs=r, start=(i == 0), stop=(i == n - 1)),
                     reads=reads, writes=[b_out], mark=(i == n - 1))

        def mem_phase():
            pk = kvm_piece.rearrange("p (k c) -> p k c", k=8)
            mT, b_mT = aT[0], b_big[0]
            for c in range(2):
                s_ = load_xa_rows(memd[c * 128:(c + 1) * 128, :])
                h_, bh_ = rms_hb(xa[s_][:], b_xa[s_], x1[0][:], extra_reads=[b_x1[0]])
                transpose_to(h_, bh_, mT[:, :, c * 128:(c + 1) * 128], b_mT)
            for mp in range(2):
                m_, bm_ = next_mm()
                mm_group(m_[:, 0:256], bm_, [(pk[:, k, mp * 128:(mp + 1) * 128], mT[:, k, 0:256]) for k in range(8)],
                         b_yT + [b_mT])
                T.op("dve", lambda: nc.vector.tensor_copy(out=kmT[:, mp, :], in_=m_[:, 0:256]), reads=[bm_], writes=[b_km])
            for kc in range(2):
                m_, bm_ = next_mm()
                mm_group(m_[:, 0:256], bm_, [(mT[:, k, kc * 128:(kc + 1) * 128], pk[:, k, 256:512]) for k in range(8)],
                         b_yT + [b_mT])
                T.op("dve", lambda: nc.vector.tensor_copy(
                    out=vmA[:, kc, :].rearrange("p (c s d) -> p c s d", c=2, s=3)[:, :, 0::2, :],
                    in_=m_[:, 0:256].rearrange("p (c s d) -> p c s d", c=2, s=2)), reads=[bm_], writes=[b_vm])

        a_hb = {}

        def stage_A_elem(j, c):
            slots = a_slots[j]
            s = slots[c]
            a_hb[(j, c)] = rms_hb(xa[s][:], b_xa[s], gb_mix[:])
            if c + 2 < 4:
                slots.append(load_xa(j, c + 2))

        def stage_A_pe(j, c):
            h_, bh_ = a_hb.pop((j, c))
            hd, b_hd = hbuf(j)
            transpose_to(h_, bh_, hd[:, :, c * 128:(c + 1) * 128], b_hd)

        def stage_A_chunk(j, c):
            stage_A_elem(j, c)
            stage_A_pe(j, c)

        def stage_A(j):
            stage_A_begin(j)
            for c in range(4):
                stage_A_chunk(j, c)

        def hbuf(j):
            if j < NHALO and j % 2 == 1:
                return h2T, b_h2T
            return hT, b_hT

        def stage_B(j, hooks=None):
            own = j >= NHALO
            kslot = j % KSLOTS
            hsrc, b_hsrc = hbuf(j)
            plist = [1, 2] if j < NHALO - 1 else ([1, 2, 3] if j == NHALO - 1 else [0, 1, 2, 3])
            for ip, pi in enumerate(plist):
                if own:
                    piece, b_piece = ring_take(P_WIN + pi)
                else:
                    piece, b_piece = halo_pieces[pi]
                pk = piece[:].rearrange("p (k c) -> p k c", k=8)
                for ol in range(4):
                    if pi == 3 and (not own) and ol >= 2:
                        continue
                    m_, bm_ = next_mm()
                    mm_group(m_[:], bm_, [(pk[:, k, ol * 128:(ol + 1) * 128], hsrc[:, k, :]) for k in range(8)],
                             [b_piece, b_hsrc])
                    if pi == 0:
                        T.op("dve", lambda: nc.vector.tensor_scalar(
                            out=QA[:, ol, :].rearrange("p (r c) -> p r c", r=4),
                            in0=m_[:, :].rearrange("p (c r) -> p r c", r=4), scalar1=csm_t[:, 5:6], scalar2=None,
                            op0=ALU.mult),
                            reads=[bm_, b_g], writes=[b_big[0]])
                        T.op("dve", lambda: nc.vector.tensor_scalar(
                            out=QB[:, ol, :].rearrange("p (r c) -> p r c", r=4),
                            in0=m_[:, :].rearrange("p (c r) -> p r c", r=4), scalar1=csm_t[:, 6:7], scalar2=None,
                            op0=ALU.mult),
                            reads=[bm_, b_g], writes=[b_big[0]])
                    elif pi == 1:
                        T.op("dve", lambda: nc.vector.tensor_copy(out=KT[:, ol, kslot * TOK:(kslot + 1) * TOK], in_=m_[:]),
                             reads=[bm_], writes=[b_KT[kslot]])
                    elif pi == 2:
                        T.op("act", lambda: nc.scalar.copy(out=VT[:, ol, :], in_=m_[:]), reads=[bm_], writes=[b_big[1]])
                    elif ol < 2:
                        T.op("dve", lambda: nc.vector.tensor_copy(out=U[:, ol, 16:16 + TOK], in_=m_[:]),
                             reads=[bm_], writes=[b_U])
                    else:
                        mp = ol - 2
                        T.op("dve", lambda: nc.vector.tensor_scalar(
                            out=qmA[:, mp, :].rearrange("p (r c) -> p r c", r=4),
                            in0=m_[:, :].rearrange("p (c r) -> p r c", r=4), scalar1=csm_t[:, 5:6], scalar2=None,
                            op0=ALU.mult),
                            reads=[bm_, b_g], writes=[b_big[1]])
                        T.op("dve", lambda: nc.vector.tensor_scalar(
                            out=qmB[:, mp, :].rearrange("p (r c) -> p r c", r=4),
                            in0=m_[:, :].rearrange("p (c r) -> p r c", r=4), scalar1=csm_t[:, 6:7], scalar2=None,
                            op0=ALU.mult),
                            reads=[bm_, b_g], writes=[b_big[1]])
                if own:
                    ring_issue()
                if hooks is not None and ip < len(hooks):
                    hooks[ip]()

        vb_state = dict(n=0)

        def vblock(src_of_pair, dst, b_dst, ones_from_hv):
            hf = vb_state["n"] % 2
            vb_state["n"] += 1
            t_ = tp[:, hf * 512:(hf + 1) * 512]
            bt_ = b_tph[hf]
            for p_ in range(4):
                T.op("pe", lambda: nc.tensor.transpose(t_[:, p_ * 128:(p_ + 1) * 128], src_of_pair(p_), ident[:]),
                     reads=[b_big[1], b_id], writes=[bt_], mark=(p_ == 3))
            d4 = dst.rearrange("p (c s d) -> p c s d", c=4, s=3)
            if hf == 0:
                T.op("act", lambda: nc.scalar.copy(out=d4[:, :, 0::2, :],
                                                   in_=t_.rearrange("p (c s d) -> p c s d", c=4, s=2)),
                     reads=[bt_], writes=[b_dst])
            else:
                T.op("dve", lambda: nc.vector.tensor_copy(out=d4[:, :, 0::2, :],
                                                          in_=t_.rearrange("p (c s d) -> p c s d", c=4, s=2)),
                     reads=[bt_], writes=[b_dst])
            if ones_from_hv:
                T.op("pool", lambda: nc.gpsimd.tensor_copy(out=d4[:, :, 1, :],
                                                           in_=hv_col.unsqueeze(1).to_broadcast([128, 4, 64])),
                     reads=[b_g], writes=[b_dst])
            else:
                T.op("pool", lambda: nc.gpsimd.memset(d4[:, :, 1, :], 1.0), writes=[b_dst])

        def stage_C(j):
            own = j >= NHALO
            s4 = j % KSLOTS
            for r in range(4):
                vblock(lambda p_: VT[:, p_, :].rearrange("p (a s) -> p a s", s=4)[:, :, r],
                       Vd4[:, s4, r, :], b_Vd4[s4], not own)
            if j >= NHALO - 1:
                for b in range(4):
                    g = (4 * j + b) % 5
                    if j == NHALO - 1 and b < 3:
                        continue
                    vblock(lambda p_: VT[:, p_, b * 128:(b + 1) * 128], Vd1[:, g, :], b_Vd1[g], not own)

        def stage_attn(j):
            batches = []
            cur = j % KSLOTS
            prv = (j - 1) % KSLOTS
            for h in range(8):
                pr, hh = h // 2, h % 2
                Qz = (QA if hh == 0 else QB)[:, pr, :]
                v0 = pr * 192 + 64 * hh
                o_ = ob[h % 2]
                for bt in range(2):
                    qk, pv, pvr = [], [], []
                    for qi in range(2):
                        qb = 2 * bt + qi
                        qv = Qz.rearrange("p (r c) -> p r c", r=4)[:, :, 32 * qb:32 * qb + 32]
                        gs, gp = (4 * j + qb) % 5, (4 * j + qb - 1) % 5
                        if qb == 0:
                            kprev = KT[:, pr, prv * TOK + 384: prv * TOK + 512]
                        else:
                            kprev = KT[:, pr, cur * TOK + (qb - 1) * 128: cur * TOK + qb * 128]
                        ksame = KT[:, pr, cur * TOK + qb * 128: cur * TOK + (qb + 1) * 128]
                        oo = o_[:, :, 32 * qb:32 * qb + 32]
                        qk.append((2 * qi, kprev, qv))
                        qk.append((2 * qi + 1, ksame, qv))
                        pv.append((2 * qi, Vd1[:, gp, v0:v0 + 128], oo))
                        pv.append((2 * qi + 1, Vd1[:, gs, v0:v0 + 128], oo))
                        pvr += [b_Vd1[gp], b_Vd1[gs]]
                    batches.append(dict(qk=qk, qk_reads=[b_KT[cur], b_KT[prv], b_big[0]],
                                        mask=E1[:, h, :, :].unsqueeze(1).to_broadcast([128, 2, 2, 128]), meng="dve",
                                        pv=pv, pv_reads=pvr, obank=h % 2,
                                        first=(bt == 0), last=False, post=None))
                for dl in range(5):
                    slot = (j - dl) % KSLOTS
                    qk, pv = [], []
                    for r in range(4):
                        qv = Qz[:, r * 128:(r + 1) * 128]
                        kv = KT[:, pr, slot * TOK:(slot + 1) * TOK].rearrange("p (a s) -> p a s", s=4)[:, :, r]
                        qk.append((r, kv, qv))
                        pv.append((r, Vd4[:, slot, r, v0:v0 + 128], o_[:, r, :]))
                    batches.append(dict(qk=qk, qk_reads=[b_KT[slot], b_big[0]],
                                        mask=E16[:, h, dl, :].unsqueeze(1).to_broadcast([128, 4, 128]),
                                        meng="dve",
                                        pv=pv, pv_reads=[b_Vd4[slot]], obank=h % 2,
                                        first=False, last=(dl == 4),
                                        post=(h // 2, hh) if dl == 4 else None))
            for mh in range(4):
                mp, hh = mh // 2, mh % 2
                Qz = (qmA if hh == 0 else qmB)[:, mp, :]
                v0 = mp * 192 + 64 * hh
                for kc in range(2):
                    batches.append(dict(qk=[(None, kmT[:, mp, kc * 128:(kc + 1) * 128], Qz)],
                                        qk_reads=[b_km, b_big[1]], mask=None, meng=None,
                                        pv=[(None, vmA[:, kc, v0:v0 + 128], None)],
                                        pv_reads=[b_vm], obank=mh % 2, first=(kc == 0), last=(kc == 1),
                                        post=(6 + mh // 2, hh) if kc == 1 else None))

            def emit_front(bt):
                si = cnt["sc"] % NSC
                cnt["sc"] += 1
                pi = cnt["P"] % NPB
                cnt["P"] += 1
                s_, bs_ = scb[si], b_sc[si]
                P_, bP_ = Pbuf[pi], b_P[pi]
                n = len(bt["qk"])
                for i, (sec, l, r) in enumerate(bt["qk"]):
                    o2 = s_[:, sec, :] if sec is not None else s_
                    T.op("pe", lambda: nc.tensor.matmul(o2, lhsT=l, rhs=r, start=True, stop=True),
                         reads=bt["qk_reads"], writes=[bs_], mark=(i == n - 1))
                T.op("act", lambda: nc.scalar.activation(out=P_[:], in_=s_, func=AF.Exp), reads=[bs_], writes=[bP_])
                if bt["mask"] is not None:
                    m = bt["mask"]
                    Pv = P_[:].rearrange("p (a b) q -> p a b q", a=2) if len(m.shape) == 4 else P_[:]
                    if bt["meng"] == "pool":
                        T.op("pool", lambda: nc.gpsimd.tensor_tensor(out=Pv, in0=Pv, in1=m, op=ALU.mult),
                             reads=[bP_, b_E], writes=[bP_])
                    else:
                        T.op("dve", lambda: nc.vector.tensor_tensor(out=Pv, in0=Pv, in1=m, op=ALU.mult),
                             reads=[bP_, b_E], writes=[bP_])
                bt["P"] = (P_, bP_)

            def emit_back(bt):
                P_, bP_ = bt["P"]
                o_b = ob[bt["obank"]]
                bo_ = b_ob[bt["obank"]]
                n = len(bt["pv"])
                for i, (sec, l, oo) in enumerate(bt["pv"]):
                    if sec is None:
                        r_ = P_[:].rearrange("p a b -> p (a b)")
                        oo = o_b[:].rearrange("p a b -> p (a b)")
                    else:
                        r_ = P_[:, sec, :]
                    T.op("pe", lambda: nc.tensor.matmul(oo, lhsT=l, rhs=r_, start=(bt["first"] and i == 0),
                                                        stop=(bt["last"] and i == n - 1), skip_group_check=True),
                         reads=bt["pv_reads"] + [bP_], writes=[bo_], mark=(i == n - 1))
                if bt["post"] is None:
                    return
                ychunk, hh = bt["post"]
                lo, o2_ = 64 * hh, 64 - 64 * hh
                of = o_b[:].rearrange("p a b -> p (a b)")
                ri = cnt["rec"] % 2
                cnt["rec"] += 1
                T.op("act", lambda: nc.scalar.activation(out=rec[ri][lo:lo + 64, :], in_=of[o2_:o2_ + 64, :], func=AF.Ln),
                     reads=[bo_], writes=[b_rec[ri]])
                T.op("act", lambda: nc.scalar.activation(out=rec[ri][lo:lo + 64, :], in_=rec[ri][lo:lo + 64, :], func=AF.Exp,
                                                         scale=-1.0),
                     reads=[b_rec[ri]], writes=[b_rec[ri]])
                T.op("dve", lambda: nc.vector.tensor_tensor(out=yT[lo:lo + 64, ychunk, :], in0=of[lo:lo + 64, :],
                                                            in1=rec[ri][lo:lo + 64, :], op=ALU.mult),
                     reads=[bo_, b_rec[ri]], writes=[b_yT[ychunk]])

            LAG = 4
            for i, bt in enumerate(batches):
                emit_front(bt)
                if i >= LAG:
                    emit_back(batches[i - LAG])
            for bt in batches[-LAG:]:
                emit_back(bt)

        def stage_pool(j):
            first_own = (j == NHALO)
            W = 16 + TOK
            for ch in range(2):
                Uc = U[:, ch, :]
                T.op("pool", lambda: nc.gpsimd.tensor_tensor(out=S1[:, 1:W], in0=Uc[:, 1:W], in1=Uc[:, 0:W - 1], op=ALU.add),
                     reads=[b_U], writes=[b_S1])
                if ch == 0:
                    T.op("pool", lambda: nc.gpsimd.tensor_tensor(out=S2[64:128, 3:W], in0=S1[64:128, 3:W],
                                                                 in1=S1[64:128, 1:W - 2], op=ALU.add),
                         reads=[b_S1], writes=[b_S2])
                else:
                    T.op("pool", lambda: nc.gpsimd.tensor_tensor(out=S2[:, 3:W], in0=S1[:, 3:W], in1=S1[:, 1:W - 2], op=ALU.add),
                         reads=[b_S1], writes=[b_S2])
                    T.op("pool", lambda: nc.gpsimd.tensor_tensor(out=S1[:, 7:W], in0=S2[:, 7:W], in1=S2[:, 3:W - 4], op=ALU.add),
                         reads=[b_S2], writes=[b_S1])
                    T.op("pool", lambda: nc.gpsimd.tensor_tensor(out=S2[64:128, 15:W], in0=S1[64:128, 15:W],
                                                                 in1=S1[64:128, 7:W - 8], op=ALU.add),
                         reads=[b_S1], writes=[b_S2])
                lo_src, hi_src = S1, S2
                if first_own:
                    T.op("pool", lambda: nc.gpsimd.tensor_tensor(out=lo_src[0:64, 16:32], in0=lo_src[0:64, 16:32],
                                                                 in1=cpc_t[0:64, ch, :], op=ALU.mult),
                         reads=[b_g], writes=[b_S1])
                    T.op("pool", lambda: nc.gpsimd.tensor_tensor(out=hi_src[64:128, 16:32], in0=hi_src[64:128, 16:32],
                                                                 in1=cpc_t[64:128, ch, :], op=ALU.mult),
                         reads=[b_g], writes=[b_S2])
                inv = csm_t[:, 3 + ch:4 + ch]
                for src_, bsrc_, p0 in ((lo_src, b_S1, 0), (hi_src, b_S2, 64)):
                    T.op("pool", lambda: nc.gpsimd.tensor_scalar(out=src_[p0:p0 + 64, 16:W], in0=src_[p0:p0 + 64, 16:W],
                                                                 scalar1=inv[p0:p0 + 64, :], scalar2=None, op0=ALU.mult),
                         reads=[b_g], writes=[bsrc_])
                    T.op("pool", lambda: nc.gpsimd.tensor_tensor(out=dT[p0:p0 + 64, ch, :], in0=src_[p0:p0 + 64, 16:W],
                                                                 in1=Uc[p0:p0 + 64, 16:W], op=ALU.subtract),
                         reads=[bsrc_, b_U], writes=[b_dT])

        def stage_pool_mm(j):
            for ch in range(2):
                m_, bm_ = next_mm()
                mm_group(m_[:], bm_, [(wpool[:, ch, :], dT[:, ch, :].rearrange("p (c r) -> p r c", r=4))], [b_c, b_dT])
                T.op("act", lambda: nc.scalar.activation(out=yT[:, 4 + ch, :], in_=m_[:], func=AF.Copy,
                                                         scale=csm_t[:, 1 + ch:2 + ch]),
                     reads=[bm_, b_g], writes=[b_yT[4 + ch]])

        def u_margin():
            T.op("pool", lambda: nc.gpsimd.tensor_copy(out=U[:, :, 0:16], in_=U[:, :, TOK:TOK + 16]),
                 reads=[b_U], writes=[b_U])

        def stage_J_chunk(tc):
            h_, bh_ = rms_hb(x1[tc][:], b_x1[tc], gb_ffn[:])
            transpose_to(h_, bh_, h2T[:, :, tc * 128:(tc + 1) * 128], b_h2T)

        def stage_IJ(j):
            pcs = [ring_take(P_WOUT), ring_take(P_WOUT + 1)]
            for tc in range(4):
                for ch in range(2):
                    piece, b_piece = pcs[ch]
                    pk = piece[:].rearrange("p (k c) -> p k c", k=8)
                    m_, bm_ = next_mm()
                    for k in range(8):
                        T.op("pe", lambda: nc.tensor.matmul(m_[:], lhsT=yT[:, k, tc * 128:(tc + 1) * 128], rhs=pk[:, k, :],
                                                            start=(k == 0), stop=(k == 7)),
                             reads=[b_piece, b_yT[k]], writes=[bm_], mark=(k == 7))
                    xs = x1[tc][:, ch * 512:(ch + 1) * 512]
                    T.op("dve", lambda: nc.vector.tensor_tensor(out=xs, in0=xs, in1=m_[:], op=ALU.add),
                         reads=[bm_], writes=[b_x1[tc]])
                if tc >= 2:
                    stage_J_chunk(tc - 2)
            ring_issue()
            ring_issue()
            stage_J_chunk(2)
            stage_J_chunk(3)

        def ff1_quarter(q):
            a_, ba_ = aT[q % 2], b_big[q % 2]
            pcs = [ring_take(P_FF1 + 2 * q), ring_take(P_FF1 + 2 * q + 1)]
            for fl in range(8):
                piece, b_piece = pcs[fl // 4]
                pk = piece[:].rearrange("p (k c) -> p k c", k=8)
                c0 = (fl % 4) * 128
                m_, bm_ = next_mm()
                mm_group(m_[:], bm_, [(pk[:, k, c0:c0 + 128], h2T[:, k, :]) for k in range(8)], [b_piece, b_h2T])
                rt, brt = (S1, b_S1) if fl % 2 == 0 else (S2, b_S2)
                T.op("act", lambda: nc.scalar.activation(out=rt[:, 0:TOK], in_=m_[:], func=AF.Relu),
                     reads=[bm_], writes=[brt])
                T.op("pool", lambda: nc.gpsimd.tensor_tensor(out=a_[:, fl, :], in0=rt[:, 0:TOK], in1=rt[:, 0:TOK],
                                                             op=ALU.mult),
                     reads=[brt], writes=[ba_])
                if fl % 4 == 3:
                    ring_issue()

        def ff2_quarter(q):
            a_, ba_ = aT[q % 2], b_big[q % 2]
            pcs = [ring_take(P_FF2 + 2 * q), ring_take(P_FF2 + 2 * q + 1)]
            for tc in range(4):
                for ch in range(2):
                    m_, bm_ = next_mm()
                    prs = []
                    for fl in range(8):
                        piece, b_piece = pcs[fl // 4]
                        pf = piece[:].rearrange("p (f c) -> p f c", f=4)
                        prs.append((a_[:, fl, tc * 128:(tc + 1) * 128], pf[:, fl % 4, ch * 512:(ch + 1) * 512]))
                    mm_group(m_[:], bm_, prs, [pcs[0][1], pcs[1][1], ba_])
                    xs = x1[tc][:, ch * 512:(ch + 1) * 512]
                    T.op("dve", lambda: nc.vector.tensor_tensor(out=xs, in0=xs, in1=m_[:], op=ALU.add),
                         reads=[bm_], writes=[b_x1[tc]])
            ring_issue()
            ring_issue()

        def stage_K(j, nxt):
            if nxt is not None:
                stage_A_begin(nxt)
            for kind, q in FFN_ORDER:
                if kind == "ff1":
                    if nxt is not None:
                        stage_A_elem(nxt, q)
                    ff1_quarter(q)
                    if nxt is not None:
                        stage_A_pe(nxt, q)
                else:
                    ff2_quarter(q)

        def stage_L(j):
            jo = j - NHALO
            ov = outd[jo * TOK:(jo + 1) * TOK, :].rearrange("(c r) d -> r c d", r=4)
            for tc in range(4):
                i = cnt["stat"] % 8
                cnt["stat"] += 1
                bs = b_stat[i]
                ss, ln, rs_ = stat[:, 0, i:i + 1], stat[:, 1, i:i + 1], stat[:, 2, i:i + 1]
                T.op("act", lambda: nc.scalar.activation(out=junk[:], in_=x1[tc][:], func=AF.Square, accum_out=ss),
                     reads=[b_x1[tc]], writes=[bs])
                T.op("act", lambda: nc.scalar.activation(out=ln, in_=ss, func=AF.Ln, scale=1.0 / D, bias=EPS),
                     reads=[bs], writes=[bs])
                T.op("act", lambda: nc.scalar.activation(out=rs_, in_=ln, func=AF.Exp, scale=-0.5),
                     reads=[bs], writes=[bs])
                T.op("dve", lambda: nc.vector.scalar_tensor_tensor(out=x1[tc][:], in0=x1[tc][:], scalar=rs_, in1=gb_fin[:],
                                                                   op0=ALU.mult, op1=ALU.mult),
                     reads=[bs, b_g], writes=[b_x1[tc]])
                T.dma("sp", out_sem[tc], ov[tc], x1[tc][:], reads=[b_x1[tc]])

        def load_x1(j):
            xv = xh[j * TOK:(j + 1) * TOK, :].rearrange("(c r) d -> r c d", r=4)
            for tc in range(4):
                T.dma("sp", x1_sem[tc], x1[tc][:], xv[tc], writes=[b_x1[tc]])

        for c in range(4):
            stage_A_chunk(0, c)
        for j in range(NT):
            own = j >= NHALO
            nxt = j + 1 if j + 1 < NT else None
            if own:
                load_x1(j)
                stage_B(j)
                stage_C(j)
                stage_pool(j)
                u_margin()
                stage_attn(j)
                stage_pool_mm(j)
                stage_IJ(j)
                stage_K(j, nxt)
                stage_L(j)
            else:
                if nxt is not None:
                    stage_A_begin(nxt)
                    stage_A_elem(nxt, 0)
                    stage_A_elem(nxt, 1)
                    hooks = [lambda: (stage_A_pe(nxt, 0), stage_A_elem(nxt, 2)),
                             lambda: (stage_A_pe(nxt, 1), stage_A_elem(nxt, 3))]
                    stage_B(j, hooks)
                else:
                    stage_B(j)
                stage_C(j)
                if nxt is not None:
                    stage_A_pe(nxt, 2)
                    stage_A_pe(nxt, 3)
                if j == NHALO - 1:
                    u_margin()
                if j == 1:
                    mem_phase()

        for tc in range(4):
            nc.sync.wait_ge(out_sem[tc].sem, out_sem[tc].val)
        build_nc.stats = dict(ninst=dict(T.ninst), nwaits=T.nwaits, sbuf_left=nc.sbuf_bytes_remaining)
    return nc


def _consts():
    slopes = 2.0 ** (-8.0 * np.arange(1, 9, dtype=np.float64) / 8.0)
    a = np.arange(128)[:, None].astype(np.float64)
    c = np.arange(128)[None, :].astype(np.float64)
    E1 = np.zeros((128, 8, 2, 128), np.float64)
    E4 = np.zeros((128, 8, 2, 128), np.float64)
    E16 = np.zeros((128, 8, 5, 128), np.float64)
    for h in range(8):
        for E, d in ((E1, 1), (E4, 4)):
            sd = slopes[h] * d
            E[:, h, 0, :] = np.where(c <= a, np.exp(-sd * (128 + c - a)), 0.0)
            E[:, h, 1, :] = np.where(c >= a, np.exp(-sd * (c - a)), 0.0)
        sd = slopes[h] * 16
        ka, km_ = a // 4, a % 4
        cq, cm = c // 4, c % 4
        for dl in range(5):
            rel = 32 * dl + cq - ka
            ok = (km_ == cm) & (rel >= 0) & (rel <= 128)
            E16[:, h, dl, :] = np.where(ok, np.exp(-sd * np.where(ok, rel, 0.0)), 0.0)
    E16[:, :, 0, :] += E4[:, :, 1, :]
    E16[:, :, 1, :] += E4[:, :, 0, :]
    E1 = E1.reshape(128, 8, 2, 32, 4).transpose(0, 1, 2, 4, 3).reshape(128, 8, 2, 128)
    return np.ascontiguousarray(E1.astype(np.float32)), E4.astype(np.float32), E16.astype(np.float32)


def _prepare_inputs(x, mem, g_mix, w_in, g_mem, w_mem_kv, w_pool, pool_scale, w_out, g_ffn, w_ff1, w_ff2, g_final):
    f = lambda a: np.ascontiguousarray(np.asarray(a, dtype=np.float32))
    x, mem = f(x), f(mem)
    E1, E4, E16 = _consts()
    gbs = np.stack([np.broadcast_to(f(g).reshape(1, D), (128, D)) for g in (g_mix[0], g_ffn[0], g_final, g_mem[0])])
    gbs = np.ascontiguousarray(gbs)
    wp = f(w_pool)[0]
    wpool = np.zeros((128, 2, 128), np.float32)
    for ch in range(2):
        wpool[0:64, ch, 0:64] = wp[2 * ch]
        wpool[64:128, ch, 64:128] = wp[2 * ch + 1]
    ps = f(pool_scale)[0]
    wins = (2, 4, 8, 16)
    shared = dict(w_in=f(w_in)[0], w_mem_kv=f(w_mem_kv)[0], w_out=f(w_out)[0], w_ff1=f(w_ff1)[0], w_ff2=f(w_ff2)[0],
                  gbs=gbs, cE1=E1, cE16=E16, cwpool=wpool)
    in_maps = []
    for core in range(8):
        b, half = core // 2, core % 2
        xh = np.zeros(((NHALO + NOWN) * TOK, D), np.float32)
        if half == 1:
            xh[:] = x[b, SEQ // 2 - NHALO * TOK:]
        else:
            xh[NHALO * TOK:] = x[b, :SEQ // 2]
        csm = np.zeros((128, 8), np.float32)
        csm[:, 0] = float(half)
        csm[0:64, 5] = 0.125
        csm[64:128, 6] = 0.125
        cpc = np.ones((128, 2, 16), np.float32)
        for ch in range(2):
            csm[:, 1 + ch] = ps[ch * 128:(ch + 1) * 128]
            for hf in range(2):
                w = wins[2 * ch + hf]
                csm[64 * hf:64 * hf + 64, 3 + ch] = 1.0 / w
                if half == 0:
                    t = np.arange(16)
                    cpc[64 * hf:64 * hf + 64, ch, :] = (w / np.minimum(t + 1, w))[None, :]
        m = dict(shared)
        m.update(xh=xh, mem=np.ascontiguousarray(mem[b]), csm=csm, cpc=cpc)
        in_maps.append(m)
    return in_maps


_NC_CACHE = {}


def kernel(x, mem, g_mix, w_in, g_mem, w_mem_kv, w_pool, pool_scale, w_out, g_ffn, w_ff1, w_ff2, g_final):
    in_maps = _prepare_inputs(x, mem, g_mix, w_in, g_mem, w_mem_kv, w_pool, pool_scale, w_out, g_ffn, w_ff1, w_ff2,
                              g_final)
    if "nc" not in _NC_CACHE:
        _NC_CACHE["nc"] = build_nc()
    nc = _NC_CACHE["nc"]
    res = run_bass_kernel_spmd(nc, in_maps, core_ids=list(range(8)))
    out = np.zeros((NB, SEQ, D), np.float32)
    for core in range(8):
        b, half = core // 2, core % 2
        out[b, half * (SEQ // 2):(half + 1) * (SEQ // 2)] = res.results[core]["out"]
    return out
```

```python
import numpy as np
from contextlib import ExitStack

import concourse.bass as bass
import concourse.mybir as mybir
from concourse.bass_utils import run_bass_kernel_spmd

F32 = mybir.dt.float32
BF16 = mybir.dt.bfloat16
AF = mybir.ActivationFunctionType
ALU = mybir.AluOpType

D = 1024
SEQ = 8192
NB = 4
TOK = 512
NHALO = 4
NOWN = 8
EPS = 1e-6
NRING = 3
KSLOTS = 5

P_KVM = 0
P_WIN = 1
P_WOUT = 5
P_FF1 = 7
P_FF2 = 15
NPIECES = 23
FFN_ORDER = [("ff1", 0), ("ff1", 1), ("ff2", 0), ("ff1", 2), ("ff2", 1), ("ff1", 3), ("ff2", 2), ("ff2", 3)]


class Buf:
    __slots__ = ("name", "w", "r")

    def __init__(self, name):
        self.name = name
        self.w = None
        self.r = []


class DmaSem:
    __slots__ = ("sem", "val")

    def __init__(self, sem):
        self.sem = sem
        self.val = 0


class Tracker:
    def __init__(self, nc, es):
        self.nc = nc
        self.eng = {"pe": nc.tensor, "act": nc.scalar, "dve": nc.vector,
                    "pool": nc.gpsimd, "sp": nc.sync}
        self.sem = {k: es.enter_context(nc.semaphore("sem_" + k)) for k in self.eng}
        self.cnt = {k: 0 for k in self.eng}
        self.waited = {}
        self.nwaits = 0
        self.ninst = {k: 0 for k in self.eng}

    def wait(self, e, tok):
        if tok is None:
            return
        sem, val, prod = tok
        if prod == "pe" and e == "pe":
            return
        key = (e, id(sem))
        if self.waited.get(key, 0) >= val:
            return
        self.waited[key] = val
        self.eng[e].wait_ge(sem, val)
        self.nwaits += 1

    def _pre(self, e, reads, writes):
        for b in reads:
            self.wait(e, b.w)
        for b in writes:
            self.wait(e, b.w)
            for t in b.r:
                self.wait(e, t)

    def _post(self, tok, reads, writes):
        for b in reads:
            b.r = [t for t in b.r if t[0] is not tok[0]] + [tok]
        for b in writes:
            b.w = tok
            b.r = []

    def op(self, e, fn, reads=(), writes=(), mark=True):
        self._pre(e, reads, writes)
        ins = fn()
        self.ninst[e] += 1
        if mark:
            self.cnt[e] += 1
            ins.then_inc(self.sem[e], 1)
            tok = (self.sem[e], self.cnt[e], e)
        else:
            tok = (self.sem[e], self.cnt[e] + 1, e)
        self._post(tok, reads, writes)
        return tok

    def dma(self, e, dsem, out, in_, reads=(), writes=(), **kw):
        self._pre(e, reads, writes)
        ins = self.eng[e].dma_start(out=out, in_=in_, **kw)
        self.ninst[e] += 1
        dsem.val += 16
        ins.then_inc(dsem.sem, 16)
        tok = (dsem.sem, dsem.val, "dma")
        self._post(tok, reads, writes)
        return tok


def build_nc(n_own=NOWN, debug=False):
    nc = bass.Bass("TRN2", target_bir_lowering=False)
    NT = NHALO + n_own

    def din(name, shape, dtype=F32):
        return nc.dram_tensor(name, shape, dtype, kind="ExternalInput").ap()

    xh = din("xh", [(NHALO + NOWN) * TOK, D])
    memd = din("mem", [256, D])
    w_in = din("w_in", [D, 2048])
    w_kvm = din("w_mem_kv", [D, 512])
    w_out = din("w_out", [D, D])
    w_ff1 = din("w_ff1", [D, 4096])
    w_ff2 = din("w_ff2", [4096, D])
    gbs = din("gbs", [4, 128, D])
    cE1 = din("cE1", [128, 8, 2, 128])
    cE16 = din("cE16", [128, 8, 5, 128])
    cwpool = din("cwpool", [128, 2, 128])
    csm = din("csm", [128, 8])
    cpc = din("cpc", [128, 2, 16])
    cident = din("cident", [128, 128])
    outd = nc.dram_tensor("out", [NOWN * TOK, D], F32, kind="ExternalOutput").ap()
    sc = nc.dram_tensor("wscratch", [NPIECES, 128, 4096], BF16).ap()
    dbg = {}
    if debug:
        for nm, shp in debug.items():
            dbg[nm] = nc.dram_tensor("dbg_" + nm, shp, F32, kind="ExternalOutput").ap()

    with ExitStack() as es:
        T = Tracker(nc, es)

        def sb(name, shape, dtype):
            return es.enter_context(nc.sbuf_tensor(name, shape, dtype))

        def psum(name, shape, dtype):
            return es.enter_context(nc.psum_tensor(name, shape, dtype))

        def dsem(name):
            return DmaSem(es.enter_context(nc.semaphore(name)))

        ring = [sb(f"ring{i}", [128, 4096], BF16) for i in range(NRING)]
        b_ring = [Buf(f"ring{i}") for i in range(NRING)]
        ring_sem = [dsem(f"ringsem{i}") for i in range(NRING)]
        KT = sb("KT", [128, 4, KSLOTS * TOK], BF16)
        b_KT = [Buf(f"KT{i}") for i in range(KSLOTS)]
        Vd1 = sb("Vd1", [128, 5, 4 * 192], BF16)
        b_Vd1 = [Buf(f"Vd1_{i}") for i in range(5)]
        Vd4 = sb("Vd4", [128, KSLOTS, 4, 4 * 192], BF16)
        b_Vd4 = [Buf(f"Vd4_{i}") for i in range(KSLOTS)]
        big = sb("big", [128, 8192], BF16)
        b_big = [Buf("bigA"), Buf("bigB")]
        aT = [big[:, 0:4096].rearrange("p (f t) -> p f t", f=8),
              big[:, 4096:8192].rearrange("p (f t) -> p f t", f=8)]
        QA = big[:, 0:2048].rearrange("p (c t) -> p c t", c=4)
        QB = big[:, 2048:4096].rearrange("p (c t) -> p c t", c=4)
        qmA = big[:, 4096:5120].rearrange("p (c t) -> p c t", c=2)
        qmB = big[:, 5120:6144].rearrange("p (c t) -> p c t", c=2)
        VT = big[:, 6144:8192].rearrange("p (c t) -> p c t", c=4)
        U = sb("U", [128, 2, 16 + TOK], F32)
        b_U = Buf("U")
        S1 = sb("S1", [128, 16 + TOK], F32)
        S2 = sb("S2", [128, 16 + TOK], F32)
        b_S1, b_S2 = Buf("S1"), Buf("S2")
        dT = sb("dT", [128, 2, TOK], BF16)
        b_dT = Buf("dT")
        yT = sb("yT", [128, 8, TOK], BF16)
        b_yT = [Buf(f"yT{i}") for i in range(8)]
        E1 = sb("E1", [128, 8, 2, 128], BF16)
        E16 = sb("E16", [128, 8, 5, 128], BF16)
        b_E = Buf("E")
        xa = [sb(f"xa{i}", [128, D], F32) for i in range(2)]
        b_xa = [Buf(f"xa{i}") for i in range(2)]
        xa_sem = [dsem(f"xasem{i}") for i in range(2)]
        x1 = [sb(f"x1_{i}", [128, D], F32) for i in range(4)]
        b_x1 = [Buf(f"x1_{i}") for i in range(4)]
        x1_sem = [dsem(f"x1sem{i}") for i in range(4)]
        out_sem = [dsem(f"outsem{i}") for i in range(4)]
        hbs = [sb(f"hb{i}", [128, D], BF16) for i in range(2)]
        b_hbs = [Buf(f"hb{i}") for i in range(2)]
        junk = sb("junk", [128, D], BF16)
        hT = sb("hT", [128, 8, TOK], BF16)
        b_hT = Buf("hT")
        h2T = sb("h2T", [128, 8, TOK], BF16)
        b_h2T = Buf("h2T")
        NPB = 5
        Pbuf = [sb(f"P{i}", [128, 4, 128], BF16) for i in range(NPB)]
        b_P = [Buf(f"P{i}") for i in range(NPB)]
        rec = [sb(f"rec{i}", [128, TOK], F32) for i in range(2)]
        b_rec = [Buf(f"rec{i}") for i in range(2)]
        gb_mix = sb("gb_mix", [128, D], F32)
        gb_ffn = sb("gb_ffn", [128, D], F32)
        gb_fin = sb("gb_fin", [128, D], F32)
        b_g = Buf("g")
        kmT = sb("kmT", [128, 2, 256], BF16)
        vmA = sb("vmA", [128, 2, 2 * 192], BF16)
        b_km, b_vm = Buf("km"), Buf("vm")
        wpool = sb("wpool", [128, 2, 128], BF16)
        csm_t = sb("csm_t", [128, 8], F32)
        cpc_t = sb("cpc_t", [128, 2, 16], F32)
        b_c = Buf("consts")
        ident = sb("ident", [128, 128], BF16)
        b_id = Buf("ident")
        stat = sb("stat", [128, 3, 8], F32)
        b_stat = [Buf(f"stat{i}") for i in range(8)]

        mm = [psum(f"mm{i}", [128, 512], F32) for i in range(2)]
        b_mm = [Buf(f"mm{i}") for i in range(2)]
        scb = [psum(f"sc{i}", [128, 4, 128], F32) for i in range(3)]
        b_sc = [Buf(f"sc{i}") for i in range(3)]
        scb = [t_[:] for t_ in scb] + [m_[:].rearrange("p (a b) -> p a b", a=4) for m_ in mm]
        b_sc = b_sc + b_mm
        NSC = 5
        ob = [psum(f"ob{i}", [128, 4, 128], F32) for i in range(2)]
        b_ob = [Buf(f"ob{i}") for i in range(2)]
        tp = psum("tp", [128, 1024], BF16)
        b_tp = Buf("tp")
        b_tph = [b_tp, b_tp]

        b_sc_piece = [Buf(f"scp{i}") for i in range(NPIECES)]
        conv_sem = [dsem(f"cv{i}") for i in range(NPIECES)]

        def conv(idx, src, pat_out, kw):
            T.dma("pool", conv_sem[idx], sc[idx].rearrange(pat_out, **kw), src, writes=[b_sc_piece[idx]])

        def conv_k(idx, wsrc, c0):
            conv(idx, wsrc[:, c0:c0 + 512].rearrange("(k p) c -> p k c", p=128), "p (k c) -> p k c", dict(k=8))

        hv_col = csm_t[:, 0:1]

        id_sem = dsem("idsem")
        T.dma("pool", id_sem, ident[:], cident[:, :], writes=[b_id])
        T.op("pool", lambda: nc.gpsimd.memset(U[:], 0.0), writes=[b_U])
        T.op("pool", lambda: nc.gpsimd.memset(vmA[:], 1.0), writes=[b_vm])

        g_sem = dsem("gsem")
        T.dma("sp", g_sem, gb_mix[:], gbs[0])
        T.dma("sp", g_sem, csm_t[:], csm[:, :])
        xa_state = dict(n=0)

        def load_xa_rows(src_rows):
            s_ = xa_state["n"] % 2
            xa_state["n"] += 1
            T.dma("sp", xa_sem[s_], xa[s_][:], src_rows, writes=[b_xa[s_]])
            return s_

        def load_xa(j, c):
            r0 = j * TOK + c * 128
            return load_xa_rows(xh[r0:r0 + 128, :])

        a_slots = {}

        def stage_A_begin(j):
            a_slots[j] = [load_xa(j, 0), load_xa(j, 1)]

        stage_A_begin(0)
        T.dma("sp", g_sem, cpc_t[:], cpc[:, :, :])
        T.dma("sp", g_sem, x1[0][:], gbs[3])
        T.dma("sp", g_sem, gb_ffn[:], gbs[1])
        T.dma("sp", g_sem, gb_fin[:], gbs[2])
        totg = (g_sem.sem, g_sem.val, "dma")
        b_g.w = totg
        b_x1[0].w = totg

        def direct_piece(dst2d, b_dst, dsem_, wsrc, c0, ncols=512):
            T.dma("pool", dsem_, dst2d.rearrange("p (k c) -> p k c", k=8),
                  wsrc[:, c0:c0 + ncols].rearrange("(k p) c -> p k c", p=128), writes=b_dst)

        halo_pieces = {}
        halo_sem = [dsem(f"halosem{i}") for i in range(NRING)]
        for pi, slot in ((1, 1), (2, 2), (3, 0)):
            direct_piece(ring[slot][:], [b_ring[slot]], halo_sem[slot], w_in, 512 * pi)
            halo_pieces[pi] = (ring[slot], b_ring[slot])
        kvm_sem = dsem("kvmsem")
        kvm_piece = yT[:].rearrange("p a t -> p (a t)")
        direct_piece(kvm_piece, b_yT, kvm_sem, w_kvm, 0)
        c_sem = dsem("csem")
        T.dma("pool", c_sem, E1[:], cE1[:, :, :, :])
        T.dma("pool", c_sem, E16[:], cE16[:, :, :, :])
        T.dma("pool", c_sem, wpool[:], cwpool[:, :, :])
        tot = (c_sem.sem, c_sem.val, "dma")
        b_E.w = tot
        b_c.w = tot
        for i in (0, 1, 2, 3):
            conv_k(P_WIN + i, w_in, 512 * i)
        for i in range(2):
            conv_k(P_WOUT + i, w_out, 512 * i)
        for kind, q in FFN_ORDER:
            for i in (2 * q, 2 * q + 1):
                if kind == "ff1":
                    conv_k(P_FF1 + i, w_ff1, 512 * i)
                else:
                    conv(P_FF2 + i, w_ff2[512 * i:512 * i + 512, :].rearrange("(f p) c -> p f c", p=128),
                         "p (f c) -> p f c", dict(f=4))

        seq = []
        for j in range(NT):
            if j < NHALO:
                pass
            else:
                seq += [P_WIN + 0, P_WIN + 1, P_WIN + 2, P_WIN + 3, P_WOUT, P_WOUT + 1]
                for kind, q in FFN_ORDER:
                    base = P_FF1 if kind == "ff1" else P_FF2
                    seq += [base + 2 * q, base + 2 * q + 1]
        rs = dict(issued=0, taken=0)

        def ring_issue():
            n = rs["issued"]
            if n >= len(seq):
                return
            s = n % NRING
            idx = seq[n]
            T.dma("sp", ring_sem[s], ring[s][:], sc[idx], reads=[b_sc_piece[idx]], writes=[b_ring[s]])
            rs["issued"] = n + 1

        def ring_take(expect):
            n = rs["taken"]
            assert seq[n] == expect, (n, seq[n], expect)
            while rs["issued"] <= n:
                ring_issue()
            rs["taken"] = n + 1
            s = n % NRING
            return ring[s], b_ring[s]


        cnt = dict(stat=0, mm=0, sc=0, P=0, tp=0, hb=0, rec=0)

        def rms_hb(src, b_src, gb, extra_reads=()):
            hi = cnt["hb"] % 2
            cnt["hb"] += 1
            dst_hb, b_dst = hbs[hi][:], b_hbs[hi]
            i = cnt["stat"] % 8
            cnt["stat"] += 1
            bs = b_stat[i]
            ss, ln, rs_ = stat[:, 0, i:i + 1], stat[:, 1, i:i + 1], stat[:, 2, i:i + 1]
            T.op("act", lambda: nc.scalar.activation(out=junk[:], in_=src, func=AF.Square, accum_out=ss),
                 reads=[b_src], writes=[bs])
            T.op("act", lambda: nc.scalar.activation(out=ln, in_=ss, func=AF.Ln, scale=1.0 / D, bias=EPS),
                 reads=[bs], writes=[bs])
            T.op("act", lambda: nc.scalar.activation(out=rs_, in_=ln, func=AF.Exp, scale=-0.5),
                 reads=[bs], writes=[bs])
            T.op("dve", lambda: nc.vector.scalar_tensor_tensor(out=dst_hb, in0=src, scalar=rs_, in1=gb,
                                                               op0=ALU.mult, op1=ALU.mult),
                 reads=[b_src, bs, b_g] + list(extra_reads), writes=[b_dst])
            return hbs[hi], b_dst

        def transpose_to(hb, b_hb, dst3, b_dst):
            for k in range(8):
                T.op("pe", lambda: nc.tensor.transpose(tp[:, k * 128:(k + 1) * 128], hb[:, k * 128:(k + 1) * 128],
                                                       ident[:]),
                     reads=[b_hb, b_id], writes=[b_tp], mark=(k == 7))
            T.op("dve", lambda: nc.vector.tensor_copy(out=dst3, in_=tp[:].rearrange("p (k t) -> p k t", k=8)),
                 reads=[b_tp], writes=[b_dst])

        def next_mm():
            i = cnt["mm"] % 2
            cnt["mm"] += 1
            return mm[i], b_mm[i]

        def mm_group(out_ap, b_out, pairs, reads):
            n = len(pairs)
            for i, (l, r) in enumerate(pairs):
                T.op("pe", lambda: nc.tensor.matmul(out_ap, lhsT=l, rhs=r, start=(i == 0), stop=(i == n - 1)),
                     reads=reads, writes=[b_out], mark=(i == n - 1))

        def mem_phase():
            pk = kvm_piece.rearrange("p (k c) -> p k c", k=8)
            mT, b_mT = aT[0], b_big[0]
            for c in range(2):
                s_ = load_xa_rows(memd[c * 128:(c + 1) * 128, :])
                h_, bh_ = rms_hb(xa[s_][:], b_xa[s_], x1[0][:], extra_reads=[b_x1[0]])
                transpose_to(h_, bh_, mT[:, :, c * 128:(c + 1) * 128], b_mT)
            for mp in range(2):
                m_, bm_ = next_mm()
                mm_group(m_[:, 0:256], bm_, [(pk[:, k, mp * 128:(mp + 1) * 128], mT[:, k, 0:256]) for k in range(8)],
                         b_yT + [b_mT])
                T.op("dve", lambda: nc.vector.tensor_copy(out=kmT[:, mp, :], in_=m_[:, 0:256]), reads=[bm_], writes=[b_km])
            for kc in range(2):
                m_, bm_ = next_mm()
                mm_group(m_[:, 0:256], bm_, [(mT[:, k, kc * 128:(kc + 1) * 128], pk[:, k, 256:512]) for k in range(8)],
                         b_yT + [b_mT])
                T.op("dve", lambda: nc.vector.tensor_copy(
                    out=vmA[:, kc, :].rearrange("p (c s d) -> p c s d", c=2, s=3)[:, :, 0::2, :],
                    in_=m_[:, 0:256].rearrange("p (c s d) -> p c s d", c=2, s=2)), reads=[bm_], writes=[b_vm])

        a_hb = {}

        def stage_A_elem(j, c):
            slots = a_slots[j]
            s = slots[c]
            a_hb[(j, c)] = rms_hb(xa[s][:], b_xa[s], gb_mix[:])
            if c + 2 < 4:
                slots.append(load_xa(j, c + 2))

        def stage_A_pe(j, c):
            h_, bh_ = a_hb.pop((j, c))
            hd, b_hd = hbuf(j)
            transpose_to(h_, bh_, hd[:, :, c * 128:(c + 1) * 128], b_hd)

        def stage_A_chunk(j, c):
            stage_A_elem(j, c)
            stage_A_pe(j, c)

        def stage_A(j):
            stage_A_begin(j)
            for c in range(4):
                stage_A_chunk(j, c)

        def hbuf(j):
            return hT, b_hT

        def stage_B(j, hooks=None):
            own = j >= NHALO
            kslot = j % KSLOTS
            hsrc, b_hsrc = hbuf(j)
            plist = [1, 2] if j < NHALO - 1 else ([1, 2, 3] if j == NHALO - 1 else [0, 1, 2, 3])
            for ip, pi in enumerate(plist):
                if own:
                    piece, b_piece = ring_take(P_WIN + pi)
                else:
                    piece, b_piece = halo_pieces[pi]
                pk = piece[:].rearrange("p (k c) -> p k c", k=8)
                for ol in range(4):
                    if pi == 3 and (not own) and ol >= 2:
                        continue
                    m_, bm_ = next_mm()
                    mm_group(m_[:], bm_, [(pk[:, k, ol * 128:(ol + 1) * 128], hsrc[:, k, :]) for k in range(8)],
                             [b_piece, b_hsrc])
                    if pi == 0:
                        T.op("dve", lambda: nc.vector.tensor_scalar(
                            out=QA[:, ol, :].rearrange("p (r c) -> p r c", r=4),
                            in0=m_[:, :].rearrange("p (c r) -> p r c", r=4), scalar1=csm_t[:, 5:6], scalar2=None,
                            op0=ALU.mult),
                            reads=[bm_, b_g], writes=[b_big[0]])
                        T.op("dve", lambda: nc.vector.tensor_scalar(
                            out=QB[:, ol, :].rearrange("p (r c) -> p r c", r=4),
                            in0=m_[:, :].rearrange("p (c r) -> p r c", r=4), scalar1=csm_t[:, 6:7], scalar2=None,
                            op0=ALU.mult),
                            reads=[bm_, b_g], writes=[b_big[0]])
                    elif pi == 1:
                        T.op("dve", lambda: nc.vector.tensor_copy(out=KT[:, ol, kslot * TOK:(kslot + 1) * TOK], in_=m_[:]),
                             reads=[bm_], writes=[b_KT[kslot]])
                    elif pi == 2:
                        T.op("act", lambda: nc.scalar.copy(out=VT[:, ol, :], in_=m_[:]), reads=[bm_], writes=[b_big[1]])
                    elif ol < 2:
                        T.op("dve", lambda: nc.vector.tensor_copy(out=U[:, ol, 16:16 + TOK], in_=m_[:]),
                             reads=[bm_], writes=[b_U])
                    else:
                        mp = ol - 2
                        T.op("dve", lambda: nc.vector.tensor_scalar(
                            out=qmA[:, mp, :].rearrange("p (r c) -> p r c", r=4),
                            in0=m_[:, :].rearrange("p (c r) -> p r c", r=4), scalar1=csm_t[:, 5:6], scalar2=None,
                            op0=ALU.mult),
                            reads=[bm_, b_g], writes=[b_big[1]])
                        T.op("dve", lambda: nc.vector.tensor_scalar(
                            out=qmB[:, mp, :].rearrange("p (r c) -> p r c", r=4),
                            in0=m_[:, :].rearrange("p (c r) -> p r c", r=4), scalar1=csm_t[:, 6:7], scalar2=None,
                            op0=ALU.mult),
                            reads=[bm_, b_g], writes=[b_big[1]])
                if own:
                    ring_issue()
                if hooks is not None and ip < len(hooks):
                    hooks[ip]()

        vb_state = dict(n=0)

        def vblock(src_of_pair, dst, b_dst, ones_from_hv):
            hf = vb_state["n"] % 2
            vb_state["n"] += 1
            t_ = tp[:, hf * 512:(hf + 1) * 512]
            bt_ = b_tph[hf]
            for p_ in range(4):
                T.op("pe", lambda: nc.tensor.transpose(t_[:, p_ * 128:(p_ + 1) * 128], src_of_pair(p_), ident[:]),
                     reads=[b_big[1], b_id], writes=[bt_], mark=(p_ == 3))
            d4 = dst.rearrange("p (c s d) -> p c s d", c=4, s=3)
            if hf == 0:
                T.op("act", lambda: nc.scalar.copy(out=d4[:, :, 0::2, :],
                                                   in_=t_.rearrange("p (c s d) -> p c s d", c=4, s=2)),
                     reads=[bt_], writes=[b_dst])
            else:
                T.op("dve", lambda: nc.vector.tensor_copy(out=d4[:, :, 0::2, :],
                                                          in_=t_.rearrange("p (c s d) -> p c s d", c=4, s=2)),
                     reads=[bt_], writes=[b_dst])
            if ones_from_hv:
                T.op("pool", lambda: nc.gpsimd.tensor_copy(out=d4[:, :, 1, :],
                                                           in_=hv_col.unsqueeze(1).to_broadcast([128, 4, 64])),
                     reads=[b_g], writes=[b_dst])
            else:
                T.op("pool", lambda: nc.gpsimd.memset(d4[:, :, 1, :], 1.0), writes=[b_dst])

        def stage_C(j):
            own = j >= NHALO
            s4 = j % KSLOTS
            for r in range(4):
                vblock(lambda p_: VT[:, p_, :].rearrange("p (a s) -> p a s", s=4)[:, :, r],
                       Vd4[:, s4, r, :], b_Vd4[s4], not own)
            if j >= NHALO - 1:
                for b in range(4):
                    g = (4 * j + b) % 5
                    if j == NHALO - 1 and b < 3:
                        continue
                    vblock(lambda p_: VT[:, p_, b * 128:(b + 1) * 128], Vd1[:, g, :], b_Vd1[g], not own)

        def stage_attn(j, side_ops=()):
            batches = []
            cur = j % KSLOTS
            prv = (j - 1) % KSLOTS
            for h in range(8):
                pr, hh = h // 2, h % 2
                Qz = (QA if hh == 0 else QB)[:, pr, :]
                v0 = pr * 192 + 64 * hh
                o_ = ob[h % 2]
                for bt in range(2):
                    qk, pv, pvr = [], [], []
                    for qi in range(2):
                        qb = 2 * bt + qi
                        qv = Qz.rearrange("p (r c) -> p r c", r=4)[:, :, 32 * qb:32 * qb + 32]
                        gs, gp = (4 * j + qb) % 5, (4 * j + qb - 1) % 5
                        if qb == 0:
                            kprev = KT[:, pr, prv * TOK + 384: prv * TOK + 512]
                        else:
                            kprev = KT[:, pr, cur * TOK + (qb - 1) * 128: cur * TOK + qb * 128]
                        ksame = KT[:, pr, cur * TOK + qb * 128: cur * TOK + (qb + 1) * 128]
                        oo = o_[:, :, 32 * qb:32 * qb + 32]
                        qk.append((2 * qi, kprev, qv))
                        qk.append((2 * qi + 1, ksame, qv))
                        pv.append((2 * qi, Vd1[:, gp, v0:v0 + 128], oo))
                        pv.append((2 * qi + 1, Vd1[:, gs, v0:v0 + 128], oo))
                        pvr += [b_Vd1[gp], b_Vd1[gs]]
                    batches.append(dict(qk=qk, qk_reads=[b_KT[cur], b_KT[prv], b_big[0]],
                                        mask=E1[:, h, :, :].unsqueeze(1).to_broadcast([128, 2, 2, 128]), meng="dve",
                                        pv=pv, pv_reads=pvr, obank=h % 2,
                                        first=(bt == 0), last=False, post=None))
                for dl in range(5):
                    slot = (j - dl) % KSLOTS
                    qk, pv = [], []
                    for r in range(4):
                        qv = Qz[:, r * 128:(r + 1) * 128]
                        kv = KT[:, pr, slot * TOK:(slot + 1) * TOK].rearrange("p (a s) -> p a s", s=4)[:, :, r]
                        qk.append((r, kv, qv))
                        pv.append((r, Vd4[:, slot, r, v0:v0 + 128], o_[:, r, :]))
                    batches.append(dict(qk=qk, qk_reads=[b_KT[slot], b_big[0]],
                                        mask=E16[:, h, dl, :].unsqueeze(1).to_broadcast([128, 4, 128]),
                                        meng="dve",
                                        pv=pv, pv_reads=[b_Vd4[slot]], obank=h % 2,
                                        first=False, last=(dl == 4),
                                        post=(h // 2, hh) if dl == 4 else None))
            for mh in range(4):
                mp, hh = mh // 2, mh % 2
                Qz = (qmA if hh == 0 else qmB)[:, mp, :]
                v0 = mp * 192 + 64 * hh
                for kc in range(2):
                    batches.append(dict(qk=[(None, kmT[:, mp, kc * 128:(kc + 1) * 128], Qz)],
                                        qk_reads=[b_km, b_big[1]], mask=None, meng=None,
                                        pv=[(None, vmA[:, kc, v0:v0 + 128], None)],
                                        pv_reads=[b_vm], obank=mh % 2, first=(kc == 0), last=(kc == 1),
                                        post=(6 + mh // 2, hh) if kc == 1 else None))

            def emit_front(bt):
                si = cnt["sc"] % NSC
                cnt["sc"] += 1
                pi = cnt["P"] % NPB
                cnt["P"] += 1
                s_, bs_ = scb[si], b_sc[si]
                P_, bP_ = Pbuf[pi], b_P[pi]
                n = len(bt["qk"])
                for i, (sec, l, r) in enumerate(bt["qk"]):
                    o2 = s_[:, sec, :] if sec is not None else s_.rearrange("p a b -> p (a b)")
                    if len(r.shape) == 3:
                        o2 = o2.rearrange("p (a b) -> p a b", a=r.shape[1])
                    T.op("pe", lambda: nc.tensor.matmul(o2, lhsT=l, rhs=r, start=True, stop=True),
                         reads=bt["qk_reads"], writes=[bs_], mark=(i == n - 1))
                T.op("act", lambda: nc.scalar.activation(out=P_[:], in_=s_, func=AF.Exp), reads=[bs_], writes=[bP_])
                if bt["mask"] is not None:
                    m = bt["mask"]
                    Pv = P_[:].rearrange("p (a b) q -> p a b q", a=2) if len(m.shape) == 4 else P_[:]
                    if bt["meng"] == "pool":
                        T.op("pool", lambda: nc.gpsimd.tensor_tensor(out=Pv, in0=Pv, in1=m, op=ALU.mult),
                             reads=[bP_, b_E], writes=[bP_])
                    else:
                        T.op("dve", lambda: nc.vector.tensor_tensor(out=Pv, in0=Pv, in1=m, op=ALU.mult),
                             reads=[bP_, b_E], writes=[bP_])
                bt["P"] = (P_, bP_)

            def emit_back(bt):
                P_, bP_ = bt["P"]
                o_b = ob[bt["obank"]]
                bo_ = b_ob[bt["obank"]]
                n = len(bt["pv"])
                for i, (sec, l, oo) in enumerate(bt["pv"]):
                    if sec is None:
                        r_ = P_[:].rearrange("p a b -> p (a b)")
                        oo = o_b[:].rearrange("p a b -> p (a b)")
                    else:
                        r_ = P_[:, sec, :]
                        if len(oo.shape) == 3:
                            r_ = r_.rearrange("p (a b) -> p a b", a=oo.shape[1])
                    T.op("pe", lambda: nc.tensor.matmul(oo, lhsT=l, rhs=r_, start=(bt["first"] and i == 0),
                                                        stop=(bt["last"] and i == n - 1), skip_group_check=True),
                         reads=bt["pv_reads"] + [bP_], writes=[bo_], mark=(i == n - 1))
                if bt["post"] is None:
                    return
                ychunk, hh = bt["post"]
                lo, o2_ = 64 * hh, 64 - 64 * hh
                of = o_b[:].rearrange("p a b -> p (a b)")
                ri = cnt["rec"] % 2
                cnt["rec"] += 1
                T.op("act", lambda: nc.scalar.activation(out=rec[ri][lo:lo + 64, :], in_=of[o2_:o2_ + 64, :], func=AF.Ln),
                     reads=[bo_], writes=[b_rec[ri]])
                T.op("act", lambda: nc.scalar.activation(out=rec[ri][lo:lo + 64, :], in_=rec[ri][lo:lo + 64, :], func=AF.Exp,
                                                         scale=-1.0),
                     reads=[b_rec[ri]], writes=[b_rec[ri]])
                T.op("dve", lambda: nc.vector.tensor_tensor(out=yT[lo:lo + 64, ychunk, :], in0=of[lo:lo + 64, :],
                                                            in1=rec[ri][lo:lo + 64, :], op=ALU.mult),
                     reads=[bo_, b_rec[ri]], writes=[b_yT[ychunk]])

            LAG = 4
            side = list(side_ops)
            for i, bt in enumerate(batches):
                emit_front(bt)
                if i >= LAG:
                    emit_back(batches[i - LAG])
                if side and i % 3 == 2:
                    side.pop(0)()
            for bt in batches[-LAG:]:
                emit_back(bt)
            for op_ in side:
                op_()

        def stage_pool(j):
            first_own = (j == NHALO)
            W = 16 + TOK
            ops = []

            def add(out, in0, in1, rd, wr):
                ops.append(lambda out=out, in0=in0, in1=in1, rd=rd, wr=wr: T.op(
                    "dve", lambda: nc.vector.tensor_tensor(out=out, in0=in0, in1=in1, op=ALU.add), reads=rd, writes=wr))

            for ch in range(2):
                Uc = U[:, ch, :]
                add(S1[:, 1:W], Uc[:, 1:W], Uc[:, 0:W - 1], [b_U], [b_S1])
                if ch == 0:
                    add(S2[64:128, 3:W], S1[64:128, 3:W], S1[64:128, 1:W - 2], [b_S1], [b_S2])
                else:
                    add(S2[:, 3:W], S1[:, 3:W], S1[:, 1:W - 2], [b_S1], [b_S2])
                    add(S1[:, 7:W], S2[:, 7:W], S2[:, 3:W - 4], [b_S2], [b_S1])
                    add(S2[64:128, 15:W], S1[64:128, 15:W], S1[64:128, 7:W - 8], [b_S1], [b_S2])
                if first_own:
                    for src_, bsrc_, p0 in ((S1, b_S1, 0), (S2, b_S2, 64)):
                        ops.append(lambda src_=src_, bsrc_=bsrc_, p0=p0, ch=ch: T.op(
                            "dve", lambda: nc.vector.tensor_tensor(out=src_[p0:p0 + 64, 16:32], in0=src_[p0:p0 + 64, 16:32],
                                                                   in1=cpc_t[p0:p0 + 64, ch, :], op=ALU.mult),
                            reads=[b_g], writes=[bsrc_]))
                inv = csm_t[:, 3 + ch:4 + ch]
                for src_, bsrc_, p0 in ((S1, b_S1, 0), (S2, b_S2, 64)):
                    ops.append(lambda src_=src_, bsrc_=bsrc_, p0=p0, ch=ch, Uc=Uc, inv=inv: T.op(
                        "dve", lambda: nc.vector.scalar_tensor_tensor(out=dT[p0:p0 + 64, ch, :], in0=src_[p0:p0 + 64, 16:W],
                                                                      scalar=inv[p0:p0 + 64, :], in1=Uc[p0:p0 + 64, 16:W],
                                                                      op0=ALU.mult, op1=ALU.subtract),
                        reads=[bsrc_, b_U, b_g], writes=[b_dT]))
            ops.append(lambda: T.op("dve", lambda: nc.vector.tensor_copy(out=U[:, :, 0:16], in_=U[:, :, TOK:TOK + 16]),
                                    reads=[b_U], writes=[b_U]))
            return ops

        def stage_pool_mm(j):
            for ch in range(2):
                m_, bm_ = next_mm()
                mm_group(m_[:].rearrange("p (r c) -> p r c", r=4), bm_,
                         [(wpool[:, ch, :], dT[:, ch, :].rearrange("p (c r) -> p r c", r=4))], [b_c, b_dT])
                T.op("act", lambda: nc.scalar.activation(out=yT[:, 4 + ch, :], in_=m_[:], func=AF.Copy,
                                                         scale=csm_t[:, 1 + ch:2 + ch]),
                     reads=[bm_, b_g], writes=[b_yT[4 + ch]])

        def u_margin():
            T.op("pool", lambda: nc.gpsimd.tensor_copy(out=U[:, :, 0:16], in_=U[:, :, TOK:TOK + 16]),
                 reads=[b_U], writes=[b_U])

        def stage_J_chunk(tc):
            h_, bh_ = rms_hb(x1[tc][:], b_x1[tc], gb_ffn[:])
            transpose_to(h_, bh_, h2T[:, :, tc * 128:(tc + 1) * 128], b_h2T)

        def stage_IJ(j):
            pcs = [ring_take(P_WOUT), ring_take(P_WOUT + 1)]
            for tc in range(4):
                for ch in range(2):
                    piece, b_piece = pcs[ch]
                    pk = piece[:].rearrange("p (k c) -> p k c", k=8)
                    m_, bm_ = next_mm()
                    for k in range(8):
                        T.op("pe", lambda: nc.tensor.matmul(m_[:], lhsT=yT[:, k, tc * 128:(tc + 1) * 128], rhs=pk[:, k, :],
                                                            start=(k == 0), stop=(k == 7)),
                             reads=[b_piece, b_yT[k]], writes=[bm_], mark=(k == 7))
                    xs = x1[tc][:, ch * 512:(ch + 1) * 512]
                    T.op("dve", lambda: nc.vector.tensor_tensor(out=xs, in0=xs, in1=m_[:], op=ALU.add),
                         reads=[bm_], writes=[b_x1[tc]])
                if tc >= 2:
                    stage_J_chunk(tc - 2)
            ring_issue()
            ring_issue()
            stage_J_chunk(2)
            stage_J_chunk(3)

        def ff1_quarter(q):
            a_, ba_ = aT[q % 2], b_big[q % 2]
            pcs = [ring_take(P_FF1 + 2 * q), ring_take(P_FF1 + 2 * q + 1)]
            for fl in range(8):
                piece, b_piece = pcs[fl // 4]
                pk = piece[:].rearrange("p (k c) -> p k c", k=8)
                c0 = (fl % 4) * 128
                m_, bm_ = next_mm()
                mm_group(m_[:], bm_, [(pk[:, k, c0:c0 + 128], h2T[:, k, :]) for k in range(8)], [b_piece, b_h2T])
                rt, brt = (S1, b_S1) if fl % 2 == 0 else (S2, b_S2)
                T.op("act", lambda: nc.scalar.activation(out=rt[:, 0:TOK], in_=m_[:], func=AF.Relu),
                     reads=[bm_], writes=[brt])
                T.op("pool", lambda: nc.gpsimd.tensor_tensor(out=a_[:, fl, :], in0=rt[:, 0:TOK], in1=rt[:, 0:TOK],
                                                             op=ALU.mult),
                     reads=[brt], writes=[ba_])
                if fl % 4 == 3:
                    ring_issue()

        def ff2_quarter(q):
            a_, ba_ = aT[q % 2], b_big[q % 2]
            pcs = [ring_take(P_FF2 + 2 * q), ring_take(P_FF2 + 2 * q + 1)]
            for tc in range(4):
                for ch in range(2):
                    m_, bm_ = next_mm()
                    prs = []
                    for fl in range(8):
                        piece, b_piece = pcs[fl // 4]
                        pf = piece[:].rearrange("p (f c) -> p f c", f=4)
                        prs.append((a_[:, fl, tc * 128:(tc + 1) * 128], pf[:, fl % 4, ch * 512:(ch + 1) * 512]))
                    mm_group(m_[:], bm_, prs, [pcs[0][1], pcs[1][1], ba_])
                    xs = x1[tc][:, ch * 512:(ch + 1) * 512]
                    T.op("dve", lambda: nc.vector.tensor_tensor(out=xs, in0=xs, in1=m_[:], op=ALU.add),
                         reads=[bm_], writes=[b_x1[tc]])
            ring_issue()
            ring_issue()

        def stage_K(j, nxt):
            if nxt is not None:
                stage_A_begin(nxt)
            for kind, q in FFN_ORDER:
                if kind == "ff1":
                    if nxt is not None:
                        stage_A_elem(nxt, q)
                    ff1_quarter(q)
                    if nxt is not None:
                        stage_A_pe(nxt, q)
                else:
                    ff2_quarter(q)

        def stage_L(j):
            jo = j - NHALO
            ov = outd[jo * TOK:(jo + 1) * TOK, :].rearrange("(c r) d -> r c d", r=4)
            for tc in range(4):
                i = cnt["stat"] % 8
                cnt["stat"] += 1
                bs = b_stat[i]
                ss, ln, rs_ = stat[:, 0, i:i + 1], stat[:, 1, i:i + 1], stat[:, 2, i:i + 1]
                T.op("act", lambda: nc.scalar.activation(out=junk[:], in_=x1[tc][:], func=AF.Square, accum_out=ss),
                     reads=[b_x1[tc]], writes=[bs])
                T.op("act", lambda: nc.scalar.activation(out=ln, in_=ss, func=AF.Ln, scale=1.0 / D, bias=EPS),
                     reads=[bs], writes=[bs])
                T.op("act", lambda: nc.scalar.activation(out=rs_, in_=ln, func=AF.Exp, scale=-0.5),
                     reads=[bs], writes=[bs])
                T.op("dve", lambda: nc.vector.scalar_tensor_tensor(out=x1[tc][:], in0=x1[tc][:], scalar=rs_, in1=gb_fin[:],
                                                                   op0=ALU.mult, op1=ALU.mult),
                     reads=[bs, b_g], writes=[b_x1[tc]])
                T.dma("sp", out_sem[tc], ov[tc], x1[tc][:], reads=[b_x1[tc]])

        def load_x1(j):
            xv = xh[j * TOK:(j + 1) * TOK, :].rearrange("(c r) d -> r c d", r=4)
            for tc in range(4):
                T.dma("sp", x1_sem[tc], x1[tc][:], xv[tc], writes=[b_x1[tc]])

        for c in range(4):
            stage_A_chunk(0, c)
        for j in range(NT):
            own = j >= NHALO
            nxt = j + 1 if j + 1 < NT else None
            if own:
                load_x1(j)
                stage_B(j)
                stage_C(j)
                stage_attn(j, stage_pool(j))
                stage_pool_mm(j)
                stage_IJ(j)
                stage_K(j, nxt)
                stage_L(j)
            else:
                stage_B(j)
                stage_C(j)
                if nxt is not None:
                    stage_A(nxt)
                if j == NHALO - 1:
                    u_margin()
                if j == 1:
                    mem_phase()

        for tc in range(4):
            nc.sync.wait_ge(out_sem[tc].sem, out_sem[tc].val)
        build_nc.stats = dict(ninst=dict(T.ninst), nwaits=T.nwaits, sbuf_left=nc.sbuf_bytes_remaining)
    return nc


def _consts():
    slopes = 2.0 ** (-8.0 * np.arange(1, 9, dtype=np.float64) / 8.0)
    a = np.arange(128)[:, None].astype(np.float64)
    c = np.arange(128)[None, :].astype(np.float64)
    E1 = np.zeros((128, 8, 2, 128), np.float64)
    E4 = np.zeros((128, 8, 2, 128), np.float64)
    E16 = np.zeros((128, 8, 5, 128), np.float64)
    for h in range(8):
        for E, d in ((E1, 1), (E4, 4)):
            sd = slopes[h] * d
            E[:, h, 0, :] = np.where(c <= a, np.exp(-sd * (128 + c - a)), 0.0)
            E[:, h, 1, :] = np.where(c >= a, np.exp(-sd * (c - a)), 0.0)
        sd = slopes[h] * 16
        ka, km_ = a // 4, a % 4
        cq, cm = c // 4, c % 4
        for dl in range(5):
            rel = 32 * dl + cq - ka
            ok = (km_ == cm) & (rel >= 0) & (rel <= 128)
            E16[:, h, dl, :] = np.where(ok, np.exp(-sd * np.where(ok, rel, 0.0)), 0.0)
    E16[:, :, 0, :] += E4[:, :, 1, :]
    E16[:, :, 1, :] += E4[:, :, 0, :]
    E1 = E1.reshape(128, 8, 2, 32, 4).transpose(0, 1, 2, 4, 3).reshape(128, 8, 2, 128)
    return np.ascontiguousarray(E1.astype(np.float32)), E4.astype(np.float32), E16.astype(np.float32)


def _prepare_inputs(x, mem, g_mix, w_in, g_mem, w_mem_kv, w_pool, pool_scale, w_out, g_ffn, w_ff1, w_ff2, g_final):
    f = lambda a: np.ascontiguousarray(np.asarray(a, dtype=np.float32))
    x, mem = f(x), f(mem)
    E1, E4, E16 = _consts()
    gbs = np.stack([np.broadcast_to(f(g).reshape(1, D), (128, D)) for g in (g_mix[0], g_ffn[0], g_final, g_mem[0])])
    gbs = np.ascontiguousarray(gbs)
    wp = f(w_pool)[0]
    wpool = np.zeros((128, 2, 128), np.float32)
    for ch in range(2):
        wpool[0:64, ch, 0:64] = wp[2 * ch]
        wpool[64:128, ch, 64:128] = wp[2 * ch + 1]
    ps = f(pool_scale)[0]
    wins = (2, 4, 8, 16)
    shared = dict(w_in=f(w_in)[0], w_mem_kv=f(w_mem_kv)[0], w_out=f(w_out)[0], w_ff1=f(w_ff1)[0], w_ff2=f(w_ff2)[0],
                  gbs=gbs, cE1=E1, cE16=E16, cwpool=wpool, cident=np.eye(128, dtype=np.float32))
    in_maps = []
    for core in range(8):
        b, half = core // 2, core % 2
        xh = np.zeros(((NHALO + NOWN) * TOK, D), np.float32)
        if half == 1:
            xh[:] = x[b, SEQ // 2 - NHALO * TOK:]
        else:
            xh[NHALO * TOK:] = x[b, :SEQ // 2]
        csm = np.zeros((128, 8), np.float32)
        csm[:, 0] = float(half)
        csm[0:64, 5] = 0.125
        csm[64:128, 6] = 0.125
        cpc = np.ones((128, 2, 16), np.float32)
        for ch in range(2):
            csm[:, 1 + ch] = ps[ch * 128:(ch + 1) * 128]
            for hf in range(2):
                w = wins[2 * ch + hf]
                csm[64 * hf:64 * hf + 64, 3 + ch] = 1.0 / w
                if half == 0:
                    t = np.arange(16)
                    cpc[64 * hf:64 * hf + 64, ch, :] = (w / np.minimum(t + 1, w))[None, :]
        m = dict(shared)
        m.update(xh=xh, mem=np.ascontiguousarray(mem[b]), csm=csm, cpc=cpc)
        in_maps.append(m)
    return in_maps


_NC_CACHE = {}


def kernel(x, mem, g_mix, w_in, g_mem, w_mem_kv, w_pool, pool_scale, w_out, g_ffn, w_ff1, w_ff2, g_final):
    in_maps = _prepare_inputs(x, mem, g_mix, w_in, g_mem, w_mem_kv, w_pool, pool_scale, w_out, g_ffn, w_ff1, w_ff2,
                              g_final)
    if "nc" not in _NC_CACHE:
        _NC_CACHE["nc"] = build_nc()
    nc = _NC_CACHE["nc"]
    res = run_bass_kernel_spmd(nc, in_maps, core_ids=list(range(8)))
    out = np.zeros((NB, SEQ, D), np.float32)
    for core in range(8):
        b, half = core // 2, core % 2
        out[b, half * (SEQ // 2):(half + 1) * (SEQ // 2)] = res.results[core]["out"]
    return out
```

```python
import numpy as np
from contextlib import ExitStack

import concourse.bass as bass
import concourse.mybir as mybir
from concourse.bass_utils import run_bass_kernel_spmd

F32 = mybir.dt.float32
BF16 = mybir.dt.bfloat16
AF = mybir.ActivationFunctionType
ALU = mybir.AluOpType

D = 1024
SEQ = 8192
NB = 4
TOK = 512
NHALO = 4
NOWN = 8
EPS = 1e-6
NRING = 3
KSLOTS = 5

P_KVM = 0
P_WIN = 1
P_WOUT = 5
P_FF1 = 7
P_FF2 = 15
NPIECES = 23
FFN_ORDER = [("ff1", 0), ("ff1", 1), ("ff2", 0), ("ff1", 2), ("ff2", 1), ("ff1", 3), ("ff2", 2), ("ff2", 3)]


class Buf:
    __slots__ = ("name", "w", "r")

    def __init__(self, name):
        self.name = name
        self.w = None
        self.r = []


class DmaSem:
    __slots__ = ("sem", "val")

    def __init__(self, sem):
        self.sem = sem
        self.val = 0


class Tracker:
    def __init__(self, nc, es):
        self.nc = nc
        self.eng = {"pe": nc.tensor, "act": nc.scalar, "dve": nc.vector,
                    "pool": nc.gpsimd, "sp": nc.sync}
        self.sem = {k: es.enter_context(nc.semaphore("sem_" + k)) for k in self.eng}
        self.cnt = {k: 0 for k in self.eng}
        self.waited = {}
        self.nwaits = 0
        self.ninst = {k: 0 for k in self.eng}

    def wait(self, e, tok):
        if tok is None:
            return
        sem, val, prod = tok
        if prod == "pe" and e == "pe":
            return
        key = (e, id(sem))
        if self.waited.get(key, 0) >= val:
            return
        self.waited[key] = val
        self.eng[e].wait_ge(sem, val)
        self.nwaits += 1

    def _pre(self, e, reads, writes):
        for b in reads:
            self.wait(e, b.w)
        for b in writes:
            self.wait(e, b.w)
            for t in b.r:
                self.wait(e, t)

    def _post(self, tok, reads, writes):
        for b in reads:
            b.r = [t for t in b.r if t[0] is not tok[0]] + [tok]
        for b in writes:
            b.w = tok
            b.r = []

    def op(self, e, fn, reads=(), writes=(), mark=True):
        self._pre(e, reads, writes)
        ins = fn()
        self.ninst[e] += 1
        if mark:
            self.cnt[e] += 1
            ins.then_inc(self.sem[e], 1)
            tok = (self.sem[e], self.cnt[e], e)
        else:
            tok = (self.sem[e], self.cnt[e] + 1, e)
        self._post(tok, reads, writes)
        return tok

    def dma(self, e, dsem, out, in_, reads=(), writes=(), **kw):
        self._pre(e, reads, writes)
        ins = self.eng[e].dma_start(out=out, in_=in_, **kw)
        self.ninst[e] += 1
        dsem.val += 16
        ins.then_inc(dsem.sem, 16)
        tok = (dsem.sem, dsem.val, "dma")
        self._post(tok, reads, writes)
        return tok


def build_nc(n_own=NOWN, debug=False):
    nc = bass.Bass("TRN2", target_bir_lowering=False)
    NT = NHALO + n_own

    def din(name, shape, dtype=F32):
        return nc.dram_tensor(name, shape, dtype, kind="ExternalInput").ap()

    xh = din("xh", [(NHALO + NOWN) * TOK, D])
    memd = din("mem", [256, D])
    w_in = din("w_in", [D, 2048])
    w_kvm = din("w_mem_kv", [D, 512])
    w_out = din("w_out", [D, D])
    w_ff1 = din("w_ff1", [D, 4096])
    w_ff2 = din("w_ff2", [4096, D])
    gbs = din("gbs", [4, 128, D])
    cE1 = din("cE1", [128, 8, 2, 128])
    cE16 = din("cE16", [128, 8, 5, 128])
    cwpool = din("cwpool", [128, 2, 128])
    csm = din("csm", [128, 8])
    cpc = din("cpc", [128, 2, 16])
    cident = din("cident", [128, 128])
    outd = nc.dram_tensor("out", [NOWN * TOK, D], F32, kind="ExternalOutput").ap()
    sc = nc.dram_tensor("wscratch", [NPIECES, 128, 4096], BF16).ap()
    dbg = {}
    if debug:
        for nm, shp in debug.items():
            dbg[nm] = nc.dram_tensor("dbg_" + nm, shp, F32, kind="ExternalOutput").ap()

    with ExitStack() as es:
        T = Tracker(nc, es)

        def sb(name, shape, dtype):
            return es.enter_context(nc.sbuf_tensor(name, shape, dtype))

        def psum(name, shape, dtype):
            return es.enter_context(nc.psum_tensor(name, shape, dtype))

        def dsem(name):
            return DmaSem(es.enter_context(nc.semaphore(name)))

        ring = [sb(f"ring{i}", [128, 4096], BF16) for i in range(NRING)]
        b_ring = [Buf(f"ring{i}") for i in range(NRING)]
        ring_sem = [dsem(f"ringsem{i}") for i in range(NRING)]
        KT = sb("KT", [128, 4, KSLOTS * TOK], BF16)
        b_KT = [Buf(f"KT{i}") for i in range(KSLOTS)]
        Vd1 = sb("Vd1", [128, 5, 4 * 192], BF16)
        b_Vd1 = [Buf(f"Vd1_{i}") for i in range(5)]
        Vd4 = sb("Vd4", [128, KSLOTS, 4, 4 * 192], BF16)
        b_Vd4 = [Buf(f"Vd4_{i}") for i in range(KSLOTS)]
        big = sb("big", [128, 8192], BF16)
        b_big = [Buf("bigA"), Buf("bigB")]
        aT = [big[:, 0:4096].rearrange("p (f t) -> p f t", f=8),
              big[:, 4096:8192].rearrange("p (f t) -> p f t", f=8)]
        QA = big[:, 0:2048].rearrange("p (c t) -> p c t", c=4)
        QB = big[:, 2048:4096].rearrange("p (c t) -> p c t", c=4)
        qmA = big[:, 4096:5120].rearrange("p (c t) -> p c t", c=2)
        qmB = big[:, 5120:6144].rearrange("p (c t) -> p c t", c=2)
        VT = big[:, 6144:8192].rearrange("p (c t) -> p c t", c=4)
        U = sb("U", [128, 2, 16 + TOK], F32)
        b_U = Buf("U")
        S1 = sb("S1", [128, 16 + TOK], F32)
        S2 = sb("S2", [128, 16 + TOK], F32)
        b_S1, b_S2 = Buf("S1"), Buf("S2")
        dT = sb("dT", [128, 2, TOK], BF16)
        b_dT = Buf("dT")
        yT = sb("yT", [128, 8, TOK], BF16)
        b_yT = [Buf(f"yT{i}") for i in range(8)]
        E1 = sb("E1", [128, 8, 2, 128], BF16)
        E16 = sb("E16", [128, 8, 5, 128], BF16)
        b_E = Buf("E")
        xa = [sb(f"xa{i}", [128, D], F32) for i in range(2)]
        b_xa = [Buf(f"xa{i}") for i in range(2)]
        xa_sem = [dsem(f"xasem{i}") for i in range(2)]
        x1 = [sb(f"x1_{i}", [128, D], F32) for i in range(4)]
        b_x1 = [Buf(f"x1_{i}") for i in range(4)]
        x1_sem = [dsem(f"x1sem{i}") for i in range(4)]
        out_sem = [dsem(f"outsem{i}") for i in range(4)]
        hbs = [sb(f"hb{i}", [128, D], BF16) for i in range(2)]
        b_hbs = [Buf(f"hb{i}") for i in range(2)]
        junk = sb("junk", [128, D], BF16)
        hT = sb("hT", [128, 8, TOK], BF16)
        b_hT = Buf("hT")
        h2T = sb("h2T", [128, 8, TOK], BF16)
        b_h2T = Buf("h2T")
        NPB = 5
        Pbuf = [sb(f"P{i}", [128, 4, 128], BF16) for i in range(NPB)]
        b_P = [Buf(f"P{i}") for i in range(NPB)]
        rec = [sb(f"rec{i}", [128, TOK], F32) for i in range(2)]
        b_rec = [Buf(f"rec{i}") for i in range(2)]
        gb_mix = sb("gb_mix", [128, D], F32)
        gb_ffn = sb("gb_ffn", [128, D], F32)
        gb_fin = sb("gb_fin", [128, D], F32)
        b_g = Buf("g")
        kmT = sb("kmT", [128, 2, 256], BF16)
        vmA = sb("vmA", [128, 2, 2 * 192], BF16)
        b_km, b_vm = Buf("km"), Buf("vm")
        wpool = sb("wpool", [128, 2, 128], BF16)
        csm_t = sb("csm_t", [128, 8], F32)
        cpc_t = sb("cpc_t", [128, 2, 16], F32)
        b_c = Buf("consts")
        ident = sb("ident", [128, 128], BF16)
        b_id = Buf("ident")
        stat = sb("stat", [128, 3, 8], F32)
        b_stat = [Buf(f"stat{i}") for i in range(8)]

        mm = [psum(f"mm{i}", [128, 512], F32) for i in range(2)]
        b_mm = [Buf(f"mm{i}") for i in range(2)]
        scb = [psum(f"sc{i}", [128, 4, 128], F32) for i in range(3)]
        b_sc = [Buf(f"sc{i}") for i in range(3)]
        scb = [t_[:] for t_ in scb] + [m_[:].rearrange("p (a b) -> p a b", a=4) for m_ in mm]
        b_sc = b_sc + b_mm
        NSC = 5
        ob = [psum(f"ob{i}", [128, 4, 128], F32) for i in range(2)]
        b_ob = [Buf(f"ob{i}") for i in range(2)]
        tp = psum("tp", [128, 1024], BF16)
        b_tp = Buf("tp")
        b_tph = [b_tp, b_tp]

        b_sc_piece = [Buf(f"scp{i}") for i in range(NPIECES)]
        conv_sem = [dsem(f"cv{i}") for i in range(NPIECES)]

        def conv(idx, src, pat_out, kw):
            T.dma("pool", conv_sem[idx], sc[idx].rearrange(pat_out, **kw), src, writes=[b_sc_piece[idx]])

        def conv_k(idx, wsrc, c0):
            conv(idx, wsrc[:, c0:c0 + 512].rearrange("(k p) c -> p k c", p=128), "p (k c) -> p k c", dict(k=8))

        hv_col = csm_t[:, 0:1]

        id_sem = dsem("idsem")
        T.dma("pool", id_sem, ident[:], cident[:, :], writes=[b_id])
        T.op("dve", lambda: nc.vector.memset(U[:], 0.0), writes=[b_U])
        T.op("dve", lambda: nc.vector.memset(vmA[:], 1.0), writes=[b_vm])

        g_sem = dsem("gsem")
        T.dma("sp", g_sem, gb_mix[:], gbs[0])
        T.dma("sp", g_sem, csm_t[:], csm[:, :])
        xa_state = dict(n=0)

        def load_xa_rows(src_rows):
            s_ = xa_state["n"] % 2
            xa_state["n"] += 1
            T.dma("sp", xa_sem[s_], xa[s_][:], src_rows, writes=[b_xa[s_]])
            return s_

        def load_xa(j, c):
            r0 = j * TOK + c * 128
            return load_xa_rows(xh[r0:r0 + 128, :])

        a_slots = {}

        def stage_A_begin(j):
            a_slots[j] = [load_xa(j, 0), load_xa(j, 1)]

        stage_A_begin(0)
        T.dma("sp", g_sem, cpc_t[:], cpc[:, :, :])
        T.dma("sp", g_sem, x1[0][:], gbs[3])
        T.dma("sp", g_sem, gb_ffn[:], gbs[1])
        T.dma("sp", g_sem, gb_fin[:], gbs[2])
        totg = (g_sem.sem, g_sem.val, "dma")
        b_g.w = totg
        b_x1[0].w = totg

        def direct_piece(dst2d, b_dst, dsem_, wsrc, c0, ncols=512):
            T.dma("pool", dsem_, dst2d.rearrange("p (k c) -> p k c", k=8),
                  wsrc[:, c0:c0 + ncols].rearrange("(k p) c -> p k c", p=128), writes=b_dst)

        halo_pieces = {}
        halo_sem = [dsem(f"halosem{i}") for i in range(NRING)]
        for pi, slot in ((1, 1), (2, 2), (3, 0)):
            direct_piece(ring[slot][:], [b_ring[slot]], halo_sem[slot], w_in, 512 * pi)
            halo_pieces[pi] = (ring[slot], b_ring[slot])
        kvm_sem = dsem("kvmsem")
        kvm_piece = yT[:].rearrange("p a t -> p (a t)")
        direct_piece(kvm_piece, b_yT, kvm_sem, w_kvm, 0)
        c_sem = dsem("csem")
        T.dma("pool", c_sem, E1[:], cE1[:, :, :, :])
        T.dma("pool", c_sem, E16[:], cE16[:, :, :, :])
        T.dma("pool", c_sem, wpool[:], cwpool[:, :, :])
        tot = (c_sem.sem, c_sem.val, "dma")
        b_E.w = tot
        b_c.w = tot
        for i in (0, 1, 2, 3):
            conv_k(P_WIN + i, w_in, 512 * i)
        for i in range(2):
            conv_k(P_WOUT + i, w_out, 512 * i)
        for kind, q in FFN_ORDER:
            for i in (2 * q, 2 * q + 1):
                if kind == "ff1":
                    conv_k(P_FF1 + i, w_ff1, 512 * i)
                else:
                    conv(P_FF2 + i, w_ff2[512 * i:512 * i + 512, :].rearrange("(f p) c -> p f c", p=128),
                         "p (f c) -> p f c", dict(f=4))

        seq = []
        for j in range(NT):
            if j < NHALO:
                pass
            else:
                seq += [P_WIN + 0, P_WIN + 1, P_WIN + 2, P_WIN + 3, P_WOUT, P_WOUT + 1]
                for kind, q in FFN_ORDER:
                    base = P_FF1 if kind == "ff1" else P_FF2
                    seq += [base + 2 * q, base + 2 * q + 1]
        rs = dict(issued=0, taken=0)

        def ring_issue():
            n = rs["issued"]
            if n >= len(seq):
                return
            s = n % NRING
            idx = seq[n]
            T.dma("sp", ring_sem[s], ring[s][:], sc[idx], reads=[b_sc_piece[idx]], writes=[b_ring[s]])
            rs["issued"] = n + 1

        def ring_take(expect):
            n = rs["taken"]
            assert seq[n] == expect, (n, seq[n], expect)
            while rs["issued"] <= n:
                ring_issue()
            rs["taken"] = n + 1
            s = n % NRING
            return ring[s], b_ring[s]


        cnt = dict(stat=0, mm=0, sc=0, P=0, tp=0, hb=0, rec=0)

        def rms_hb(src, b_src, gb, extra_reads=()):
            hi = cnt["hb"] % 2
            cnt["hb"] += 1
            dst_hb, b_dst = hbs[hi][:], b_hbs[hi]
            i = cnt["stat"] % 8
            cnt["stat"] += 1
            bs = b_stat[i]
            ss, ln, rs_ = stat[:, 0, i:i + 1], stat[:, 1, i:i + 1], stat[:, 2, i:i + 1]
            T.op("act", lambda: nc.scalar.activation(out=junk[:], in_=src, func=AF.Square, accum_out=ss),
                 reads=[b_src], writes=[bs])
            T.op("act", lambda: nc.scalar.activation(out=ln, in_=ss, func=AF.Ln, scale=1.0 / D, bias=EPS),
                 reads=[bs], writes=[bs])
            T.op("act", lambda: nc.scalar.activation(out=rs_, in_=ln, func=AF.Exp, scale=-0.5),
                 reads=[bs], writes=[bs])
            T.op("dve", lambda: nc.vector.scalar_tensor_tensor(out=dst_hb, in0=src, scalar=rs_, in1=gb,
                                                               op0=ALU.mult, op1=ALU.mult),
                 reads=[b_src, bs, b_g] + list(extra_reads), writes=[b_dst])
            return hbs[hi], b_dst

        def transpose_to(hb, b_hb, dst3, b_dst):
            for k in range(8):
                T.op("pe", lambda: nc.tensor.transpose(tp[:, k * 128:(k + 1) * 128], hb[:, k * 128:(k + 1) * 128],
                                                       ident[:]),
                     reads=[b_hb, b_id], writes=[b_tp], mark=(k == 7))
            T.op("dve", lambda: nc.vector.tensor_copy(out=dst3, in_=tp[:].rearrange("p (k t) -> p k t", k=8)),
                 reads=[b_tp], writes=[b_dst])

        def next_mm():
            i = cnt["mm"] % 2
            cnt["mm"] += 1
            return mm[i], b_mm[i]

        def mm_group(out_ap, b_out, pairs, reads):
            n = len(pairs)
            for i, (l, r) in enumerate(pairs):
                T.op("pe", lambda: nc.tensor.matmul(out_ap, lhsT=l, rhs=r, start=(i == 0), stop=(i == n - 1)),
                     reads=reads, writes=[b_out], mark=(i == n - 1))

        def mem_phase():
            pk = kvm_piece.rearrange("p (k c) -> p k c", k=8)
            mT, b_mT = aT[0], b_big[0]
            for c in range(2):
                s_ = load_xa_rows(memd[c * 128:(c + 1) * 128, :])
                h_, bh_ = rms_hb(xa[s_][:], b_xa[s_], x1[0][:], extra_reads=[b_x1[0]])
                transpose_to(h_, bh_, mT[:, :, c * 128:(c + 1) * 128], b_mT)
            for mp in range(2):
                m_, bm_ = next_mm()
                mm_group(m_[:, 0:256], bm_, [(pk[:, k, mp * 128:(mp + 1) * 128], mT[:, k, 0:256]) for k in range(8)],
                         b_yT + [b_mT])
                T.op("dve", lambda: nc.vector.tensor_copy(out=kmT[:, mp, :], in_=m_[:, 0:256]), reads=[bm_], writes=[b_km])
            for kc in range(2):
                m_, bm_ = next_mm()
                mm_group(m_[:, 0:256], bm_, [(mT[:, k, kc * 128:(kc + 1) * 128], pk[:, k, 256:512]) for k in range(8)],
                         b_yT + [b_mT])
                T.op("dve", lambda: nc.vector.tensor_copy(
                    out=vmA[:, kc, :].rearrange("p (c s d) -> p c s d", c=2, s=3)[:, :, 0::2, :],
                    in_=m_[:, 0:256].rearrange("p (c s d) -> p c s d", c=2, s=2)), reads=[bm_], writes=[b_vm])

        a_hb = {}

        def stage_A_elem(j, c):
            slots = a_slots[j]
            s = slots[c]
            a_hb[(j, c)] = rms_hb(xa[s][:], b_xa[s], gb_mix[:])
            if c + 2 < 4:
                slots.append(load_xa(j, c + 2))

        def stage_A_pe(j, c):
            h_, bh_ = a_hb.pop((j, c))
            hd, b_hd = hbuf(j)
            transpose_to(h_, bh_, hd[:, :, c * 128:(c + 1) * 128], b_hd)

        def stage_A_chunk(j, c):
            stage_A_elem(j, c)
            stage_A_pe(j, c)

        def stage_A(j):
            stage_A_begin(j)
            for c in range(4):
                stage_A_chunk(j, c)

        def hbuf(j):
            if j < NHALO and j % 2 == 1:
                return h2T, b_h2T
            return hT, b_hT

        def stage_B(j, hooks=None):
            own = j >= NHALO
            kslot = j % KSLOTS
            hsrc, b_hsrc = hbuf(j)
            plist = [1, 2] if j < NHALO - 1 else ([1, 2, 3] if j == NHALO - 1 else [0, 1, 2, 3])
            for ip, pi in enumerate(plist):
                if own:
                    piece, b_piece = ring_take(P_WIN + pi)
                else:
                    piece, b_piece = halo_pieces[pi]
                pk = piece[:].rearrange("p (k c) -> p k c", k=8)
                for ol in range(4):
                    if pi == 3 and (not own) and ol >= 2:
                        continue
                    m_, bm_ = next_mm()
                    mm_group(m_[:], bm_, [(pk[:, k, ol * 128:(ol + 1) * 128], hsrc[:, k, :]) for k in range(8)],
                             [b_piece, b_hsrc])
                    if pi == 0:
                        T.op("dve", lambda: nc.vector.tensor_scalar(
                            out=QA[:, ol, :].rearrange("p (r c) -> p r c", r=4),
                            in0=m_[:, :].rearrange("p (c r) -> p r c", r=4), scalar1=csm_t[:, 5:6], scalar2=None,
                            op0=ALU.mult),
                            reads=[bm_, b_g], writes=[b_big[0]])
                        T.op("dve", lambda: nc.vector.tensor_scalar(
                            out=QB[:, ol, :].rearrange("p (r c) -> p r c", r=4),
                            in0=m_[:, :].rearrange("p (c r) -> p r c", r=4), scalar1=csm_t[:, 6:7], scalar2=None,
                            op0=ALU.mult),
                            reads=[bm_, b_g], writes=[b_big[0]])
                    elif pi == 1:
                        T.op("dve", lambda: nc.vector.tensor_copy(out=KT[:, ol, kslot * TOK:(kslot + 1) * TOK], in_=m_[:]),
                             reads=[bm_], writes=[b_KT[kslot]])
                    elif pi == 2:
                        T.op("act", lambda: nc.scalar.copy(out=VT[:, ol, :], in_=m_[:]), reads=[bm_], writes=[b_big[1]])
                    elif ol < 2:
                        T.op("dve", lambda: nc.vector.tensor_copy(out=U[:, ol, 16:16 + TOK], in_=m_[:]),
                             reads=[bm_], writes=[b_U])
                    else:
                        mp = ol - 2
                        T.op("dve", lambda: nc.vector.tensor_scalar(
                            out=qmA[:, mp, :].rearrange("p (r c) -> p r c", r=4),
                            in0=m_[:, :].rearrange("p (c r) -> p r c", r=4), scalar1=csm_t[:, 5:6], scalar2=None,
                            op0=ALU.mult),
                            reads=[bm_, b_g], writes=[b_big[1]])
                        T.op("dve", lambda: nc.vector.tensor_scalar(
                            out=qmB[:, mp, :].rearrange("p (r c) -> p r c", r=4),
                            in0=m_[:, :].rearrange("p (c r) -> p r c", r=4), scalar1=csm_t[:, 6:7], scalar2=None,
                            op0=ALU.mult),
                            reads=[bm_, b_g], writes=[b_big[1]])
                if own:
                    ring_issue()
                if hooks is not None and ip < len(hooks):
                    hooks[ip]()

        vb_state = dict(n=0)

        def vblock(src_of_pair, dst, b_dst, ones_from_hv):
            hf = vb_state["n"] % 2
            vb_state["n"] += 1
            t_ = tp[:, hf * 512:(hf + 1) * 512]
            bt_ = b_tph[hf]
            for p_ in range(4):
                T.op("pe", lambda: nc.tensor.transpose(t_[:, p_ * 128:(p_ + 1) * 128], src_of_pair(p_), ident[:]),
                     reads=[b_big[1], b_id], writes=[bt_], mark=(p_ == 3))
            d4 = dst.rearrange("p (c s d) -> p c s d", c=4, s=3)
            if hf == 0:
                T.op("act", lambda: nc.scalar.copy(out=d4[:, :, 0::2, :],
                                                   in_=t_.rearrange("p (c s d) -> p c s d", c=4, s=2)),
                     reads=[bt_], writes=[b_dst])
            else:
                T.op("dve", lambda: nc.vector.tensor_copy(out=d4[:, :, 0::2, :],
                                                          in_=t_.rearrange("p (c s d) -> p c s d", c=4, s=2)),
                     reads=[bt_], writes=[b_dst])
            if ones_from_hv:
                T.op("dve", lambda: nc.vector.tensor_copy(out=d4[:, :, 1, :],
                                                          in_=hv_col.unsqueeze(1).to_broadcast([128, 4, 64])),
                     reads=[b_g], writes=[b_dst])
            else:
                T.op("dve", lambda: nc.vector.memset(d4[:, :, 1, :], 1.0), writes=[b_dst])

        def stage_C(j):
            own = j >= NHALO
            s4 = j % KSLOTS
            for r in range(4):
                vblock(lambda p_: VT[:, p_, :].rearrange("p (a s) -> p a s", s=4)[:, :, r],
                       Vd4[:, s4, r, :], b_Vd4[s4], not own)
            if j >= NHALO - 1:
                for b in range(4):
                    g = (4 * j + b) % 5
                    if j == NHALO - 1 and b < 3:
                        continue
                    vblock(lambda p_: VT[:, p_, b * 128:(b + 1) * 128], Vd1[:, g, :], b_Vd1[g], not own)

        def stage_attn(j, side_ops=()):
            batches = []
            cur = j % KSLOTS
            prv = (j - 1) % KSLOTS
            for h in range(8):
                pr, hh = h // 2, h % 2
                Qz = (QA if hh == 0 else QB)[:, pr, :]
                v0 = pr * 192 + 64 * hh
                o_ = ob[h % 2]
                for bt in range(2):
                    qk, pv, pvr = [], [], []
                    for qi in range(2):
                        qb = 2 * bt + qi
                        qv = Qz.rearrange("p (r c) -> p r c", r=4)[:, :, 32 * qb:32 * qb + 32]
                        gs, gp = (4 * j + qb) % 5, (4 * j + qb - 1) % 5
                        if qb == 0:
                            kprev = KT[:, pr, prv * TOK + 384: prv * TOK + 512]
                        else:
                            kprev = KT[:, pr, cur * TOK + (qb - 1) * 128: cur * TOK + qb * 128]
                        ksame = KT[:, pr, cur * TOK + qb * 128: cur * TOK + (qb + 1) * 128]
                        oo = o_[:, :, 32 * qb:32 * qb + 32]
                        qk.append((2 * qi, kprev, qv))
                        qk.append((2 * qi + 1, ksame, qv))
                        pv.append((2 * qi, Vd1[:, gp, v0:v0 + 128], oo))
                        pv.append((2 * qi + 1, Vd1[:, gs, v0:v0 + 128], oo))
                        pvr += [b_Vd1[gp], b_Vd1[gs]]
                    batches.append(dict(qk=qk, qk_reads=[b_KT[cur], b_KT[prv], b_big[0]],
                                        mask=E1[:, h, :, :].unsqueeze(1).to_broadcast([128, 2, 2, 128]), meng="dve",
                                        pv=pv, pv_reads=pvr, obank=h % 2,
                                        first=(bt == 0), last=False, post=None))
                for dl in range(5):
                    slot = (j - dl) % KSLOTS
                    qk, pv = [], []
                    for r in range(4):
                        qv = Qz[:, r * 128:(r + 1) * 128]
                        kv = KT[:, pr, slot * TOK:(slot + 1) * TOK].rearrange("p (a s) -> p a s", s=4)[:, :, r]
                        qk.append((r, kv, qv))
                        pv.append((r, Vd4[:, slot, r, v0:v0 + 128], o_[:, r, :]))
                    batches.append(dict(qk=qk, qk_reads=[b_KT[slot], b_big[0]],
                                        mask=E16[:, h, dl, :].unsqueeze(1).to_broadcast([128, 4, 128]),
                                        meng="dve",
                                        pv=pv, pv_reads=[b_Vd4[slot]], obank=h % 2,
                                        first=False, last=(dl == 4),
                                        post=(h // 2, hh) if dl == 4 else None))
            for mh in range(4):
                mp, hh = mh // 2, mh % 2
                Qz = (qmA if hh == 0 else qmB)[:, mp, :]
                v0 = mp * 192 + 64 * hh
                for kc in range(2):
                    batches.append(dict(qk=[(None, kmT[:, mp, kc * 128:(kc + 1) * 128], Qz)],
                                        qk_reads=[b_km, b_big[1]], mask=None, meng=None,
                                        pv=[(None, vmA[:, kc, v0:v0 + 128], None)],
                                        pv_reads=[b_vm], obank=mh % 2, first=(kc == 0), last=(kc == 1),
                                        post=(6 + mh // 2, hh) if kc == 1 else None))

            def emit_front(bt):
                si = cnt["sc"] % NSC
                cnt["sc"] += 1
                pi = cnt["P"] % NPB
                cnt["P"] += 1
                s_, bs_ = scb[si], b_sc[si]
                P_, bP_ = Pbuf[pi], b_P[pi]
                n = len(bt["qk"])
                for i, (sec, l, r) in enumerate(bt["qk"]):
                    o2 = s_[:, sec, :] if sec is not None else s_.rearrange("p a b -> p (a b)")
                    if len(r.shape) == 3:
                        o2 = o2.rearrange("p (a b) -> p a b", a=r.shape[1])
                    T.op("pe", lambda: nc.tensor.matmul(o2, lhsT=l, rhs=r, start=True, stop=True),
                         reads=bt["qk_reads"], writes=[bs_], mark=(i == n - 1))
                T.op("act", lambda: nc.scalar.activation(out=P_[:], in_=s_, func=AF.Exp), reads=[bs_], writes=[bP_])
                if bt["mask"] is not None:
                    m = bt["mask"]
                    Pv = P_[:].rearrange("p (a b) q -> p a b q", a=2) if len(m.shape) == 4 else P_[:]
                    if bt["meng"] == "pool":
                        T.op("pool", lambda: nc.gpsimd.tensor_tensor(out=Pv, in0=Pv, in1=m, op=ALU.mult),
                             reads=[bP_, b_E], writes=[bP_])
                    else:
                        T.op("dve", lambda: nc.vector.tensor_tensor(out=Pv, in0=Pv, in1=m, op=ALU.mult),
                             reads=[bP_, b_E], writes=[bP_])
                bt["P"] = (P_, bP_)

            def emit_back(bt):
                P_, bP_ = bt["P"]
                o_b = ob[bt["obank"]]
                bo_ = b_ob[bt["obank"]]
                n = len(bt["pv"])
                for i, (sec, l, oo) in enumerate(bt["pv"]):
                    if sec is None:
                        r_ = P_[:].rearrange("p a b -> p (a b)")
                        oo = o_b[:].rearrange("p a b -> p (a b)")
                    else:
                        r_ = P_[:, sec, :]
                        if len(oo.shape) == 3:
                            r_ = r_.rearrange("p (a b) -> p a b", a=oo.shape[1])
                    T.op("pe", lambda: nc.tensor.matmul(oo, lhsT=l, rhs=r_, start=(bt["first"] and i == 0),
                                                        stop=(bt["last"] and i == n - 1), skip_group_check=True),
                         reads=bt["pv_reads"] + [bP_], writes=[bo_], mark=(i == n - 1))
                if bt["post"] is None:
                    return
                ychunk, hh = bt["post"]
                lo, o2_ = 64 * hh, 64 - 64 * hh
                of = o_b[:].rearrange("p a b -> p (a b)")
                ri = cnt["rec"] % 2
                cnt["rec"] += 1
                T.op("act", lambda: nc.scalar.activation(out=rec[ri][lo:lo + 64, :], in_=of[o2_:o2_ + 64, :], func=AF.Ln),
                     reads=[bo_], writes=[b_rec[ri]])
                T.op("act", lambda: nc.scalar.activation(out=rec[ri][lo:lo + 64, :], in_=rec[ri][lo:lo + 64, :], func=AF.Exp,
                                                         scale=-1.0),
                     reads=[b_rec[ri]], writes=[b_rec[ri]])
                T.op("dve", lambda: nc.vector.tensor_tensor(out=yT[lo:lo + 64, ychunk, :], in0=of[lo:lo + 64, :],
                                                            in1=rec[ri][lo:lo + 64, :], op=ALU.mult),
                     reads=[bo_, b_rec[ri]], writes=[b_yT[ychunk]])

            LAG = 4
            side = list(side_ops)
            for i, bt in enumerate(batches):
                emit_front(bt)
                if i >= LAG:
                    emit_back(batches[i - LAG])
                if side and i % 3 == 2:
                    side.pop(0)()
            for bt in batches[-LAG:]:
                emit_back(bt)
            for op_ in side:
                op_()

        def stage_pool(j):
            first_own = (j == NHALO)
            W = 16 + TOK
            ops = []

            def add(out, in0, in1, rd, wr):
                ops.append(lambda out=out, in0=in0, in1=in1, rd=rd, wr=wr: T.op(
                    "dve", lambda: nc.vector.tensor_tensor(out=out, in0=in0, in1=in1, op=ALU.add), reads=rd, writes=wr))

            for ch in range(2):
                Uc = U[:, ch, :]
                add(S1[:, 1:W], Uc[:, 1:W], Uc[:, 0:W - 1], [b_U], [b_S1])
                if ch == 0:
                    add(S2[64:128, 3:W], S1[64:128, 3:W], S1[64:128, 1:W - 2], [b_S1], [b_S2])
                else:
                    add(S2[:, 3:W], S1[:, 3:W], S1[:, 1:W - 2], [b_S1], [b_S2])
                    add(S1[:, 7:W], S2[:, 7:W], S2[:, 3:W - 4], [b_S2], [b_S1])
                    add(S2[64:128, 15:W], S1[64:128, 15:W], S1[64:128, 7:W - 8], [b_S1], [b_S2])
                if first_own:
                    for src_, bsrc_, p0 in ((S1, b_S1, 0), (S2, b_S2, 64)):
                        ops.append(lambda src_=src_, bsrc_=bsrc_, p0=p0, ch=ch: T.op(
                            "dve", lambda: nc.vector.tensor_tensor(out=src_[p0:p0 + 64, 16:32], in0=src_[p0:p0 + 64, 16:32],
                                                                   in1=cpc_t[p0:p0 + 64, ch, :], op=ALU.mult),
                            reads=[b_g], writes=[bsrc_]))
                inv = csm_t[:, 3 + ch:4 + ch]
                for src_, bsrc_, p0 in ((S1, b_S1, 0), (S2, b_S2, 64)):
                    ops.append(lambda src_=src_, bsrc_=bsrc_, p0=p0, ch=ch, Uc=Uc, inv=inv: T.op(
                        "dve", lambda: nc.vector.scalar_tensor_tensor(out=dT[p0:p0 + 64, ch, :], in0=src_[p0:p0 + 64, 16:W],
                                                                      scalar=inv[p0:p0 + 64, :], in1=Uc[p0:p0 + 64, 16:W],
                                                                      op0=ALU.mult, op1=ALU.subtract),
                        reads=[bsrc_, b_U, b_g], writes=[b_dT]))
            ops.append(lambda: T.op("dve", lambda: nc.vector.tensor_copy(out=U[:, :, 0:16], in_=U[:, :, TOK:TOK + 16]),
                                    reads=[b_U], writes=[b_U]))
            return ops

        def stage_pool_mm(j):
            for ch in range(2):
                m_, bm_ = next_mm()
                mm_group(m_[:].rearrange("p (r c) -> p r c", r=4), bm_,
                         [(wpool[:, ch, :], dT[:, ch, :].rearrange("p (c r) -> p r c", r=4))], [b_c, b_dT])
                T.op("act", lambda: nc.scalar.activation(out=yT[:, 4 + ch, :], in_=m_[:], func=AF.Copy,
                                                         scale=csm_t[:, 1 + ch:2 + ch]),
                     reads=[bm_, b_g], writes=[b_yT[4 + ch]])

        def u_margin():
            T.op("dve", lambda: nc.vector.tensor_copy(out=U[:, :, 0:16], in_=U[:, :, TOK:TOK + 16]),
                 reads=[b_U], writes=[b_U])

        def stage_J_chunk(tc):
            h_, bh_ = rms_hb(x1[tc][:], b_x1[tc], gb_ffn[:])
            transpose_to(h_, bh_, h2T[:, :, tc * 128:(tc + 1) * 128], b_h2T)

        def stage_IJ(j):
            pcs = [ring_take(P_WOUT), ring_take(P_WOUT + 1)]
            for tc in range(4):
                for ch in range(2):
                    piece, b_piece = pcs[ch]
                    pk = piece[:].rearrange("p (k c) -> p k c", k=8)
                    m_, bm_ = next_mm()
                    for k in range(8):
                        T.op("pe", lambda: nc.tensor.matmul(m_[:], lhsT=yT[:, k, tc * 128:(tc + 1) * 128], rhs=pk[:, k, :],
                                                            start=(k == 0), stop=(k == 7)),
                             reads=[b_piece, b_yT[k]], writes=[bm_], mark=(k == 7))
                    xs = x1[tc][:, ch * 512:(ch + 1) * 512]
                    T.op("dve", lambda: nc.vector.tensor_tensor(out=xs, in0=xs, in1=m_[:], op=ALU.add),
                         reads=[bm_], writes=[b_x1[tc]])
                if tc >= 2:
                    stage_J_chunk(tc - 2)
            ring_issue()
            ring_issue()
            stage_J_chunk(2)
            stage_J_chunk(3)

        def ff1_quarter(q):
            a_, ba_ = aT[q % 2], b_big[q % 2]
            pcs = [ring_take(P_FF1 + 2 * q), ring_take(P_FF1 + 2 * q + 1)]
            for fl in range(8):
                piece, b_piece = pcs[fl // 4]
                pk = piece[:].rearrange("p (k c) -> p k c", k=8)
                c0 = (fl % 4) * 128
                m_, bm_ = next_mm()
                mm_group(m_[:], bm_, [(pk[:, k, c0:c0 + 128], h2T[:, k, :]) for k in range(8)], [b_piece, b_h2T])
                rt, brt = (S1, b_S1) if fl % 2 == 0 else (S2, b_S2)
                T.op("act", lambda: nc.scalar.activation(out=rt[:, 0:TOK], in_=m_[:], func=AF.Relu),
                     reads=[bm_], writes=[brt])
                T.op("pool", lambda: nc.gpsimd.tensor_tensor(out=a_[:, fl, :], in0=rt[:, 0:TOK], in1=rt[:, 0:TOK],
                                                             op=ALU.mult),
                     reads=[brt], writes=[ba_])
                if fl % 4 == 3:
                    ring_issue()

        def ff2_quarter(q):
            a_, ba_ = aT[q % 2], b_big[q % 2]
            pcs = [ring_take(P_FF2 + 2 * q), ring_take(P_FF2 + 2 * q + 1)]
            for tc in range(4):
                for ch in range(2):
                    m_, bm_ = next_mm()
                    prs = []
                    for fl in range(8):
                        piece, b_piece = pcs[fl // 4]
                        pf = piece[:].rearrange("p (f c) -> p f c", f=4)
                        prs.append((a_[:, fl, tc * 128:(tc + 1) * 128], pf[:, fl % 4, ch * 512:(ch + 1) * 512]))
                    mm_group(m_[:], bm_, prs, [pcs[0][1], pcs[1][1], ba_])
                    xs = x1[tc][:, ch * 512:(ch + 1) * 512]
                    T.op("dve", lambda: nc.vector.tensor_tensor(out=xs, in0=xs, in1=m_[:], op=ALU.add),
                         reads=[bm_], writes=[b_x1[tc]])
            ring_issue()
            ring_issue()

        def stage_K(j, nxt):
            if nxt is not None:
                stage_A_begin(nxt)
            for kind, q in FFN_ORDER:
                if kind == "ff1":
                    if nxt is not None:
                        stage_A_elem(nxt, q)
                    ff1_quarter(q)
                    if nxt is not None:
                        stage_A_pe(nxt, q)
                else:
                    ff2_quarter(q)

        def stage_L(j):
            jo = j - NHALO
            ov = outd[jo * TOK:(jo + 1) * TOK, :].rearrange("(c r) d -> r c d", r=4)
            for tc in range(4):
                i = cnt["stat"] % 8
                cnt["stat"] += 1
                bs = b_stat[i]
                ss, ln, rs_ = stat[:, 0, i:i + 1], stat[:, 1, i:i + 1], stat[:, 2, i:i + 1]
                T.op("act", lambda: nc.scalar.activation(out=junk[:], in_=x1[tc][:], func=AF.Square, accum_out=ss),
                     reads=[b_x1[tc]], writes=[bs])
                T.op("act", lambda: nc.scalar.activation(out=ln, in_=ss, func=AF.Ln, scale=1.0 / D, bias=EPS),
                     reads=[bs], writes=[bs])
                T.op("act", lambda: nc.scalar.activation(out=rs_, in_=ln, func=AF.Exp, scale=-0.5),
                     reads=[bs], writes=[bs])
                T.op("dve", lambda: nc.vector.scalar_tensor_tensor(out=x1[tc][:], in0=x1[tc][:], scalar=rs_, in1=gb_fin[:],
                                                                   op0=ALU.mult, op1=ALU.mult),
                     reads=[bs, b_g], writes=[b_x1[tc]])
                T.dma("sp", out_sem[tc], ov[tc], x1[tc][:], reads=[b_x1[tc]])

        def load_x1(j):
            xv = xh[j * TOK:(j + 1) * TOK, :].rearrange("(c r) d -> r c d", r=4)
            for tc in range(4):
                T.dma("sp", x1_sem[tc], x1[tc][:], xv[tc], writes=[b_x1[tc]])

        for c in range(4):
            stage_A_chunk(0, c)
        for j in range(NT):
            own = j >= NHALO
            nxt = j + 1 if j + 1 < NT else None
            if own:
                load_x1(j)
                stage_B(j)
                stage_C(j)
                stage_attn(j, stage_pool(j))
                stage_pool_mm(j)
                stage_IJ(j)
                stage_K(j, nxt)
                stage_L(j)
            else:
                if nxt is not None:
                    stage_A_begin(nxt)
                    stage_A_elem(nxt, 0)
                    stage_A_elem(nxt, 1)
                    hooks = [lambda: (stage_A_pe(nxt, 0), stage_A_elem(nxt, 2)),
                             lambda: (stage_A_pe(nxt, 1), stage_A_elem(nxt, 3))]
                    stage_B(j, hooks)
                else:
                    stage_B(j)
                stage_C(j)
                if nxt is not None:
                    stage_A_pe(nxt, 2)
                    stage_A_pe(nxt, 3)
                if j == NHALO - 1:
                    u_margin()
                if j == 1:
                    mem_phase()

        for tc in range(4):
            nc.sync.wait_ge(out_sem[tc].sem, out_sem[tc].val)
        build_nc.stats = dict(ninst=dict(T.ninst), nwaits=T.nwaits, sbuf_left=nc.sbuf_bytes_remaining)
    return nc


def _consts():
    slopes = 2.0 ** (-8.0 * np.arange(1, 9, dtype=np.float64) / 8.0)
    a = np.arange(128)[:, None].astype(np.float64)
    c = np.arange(128)[None, :].astype(np.float64)
    E1 = np.zeros((128, 8, 2, 128), np.float64)
    E4 = np.zeros((128, 8, 2, 128), np.float64)
    E16 = np.zeros((128, 8, 5, 128), np.float64)
    for h in range(8):
        for E, d in ((E1, 1), (E4, 4)):
            sd = slopes[h] * d
            E[:, h, 0, :] = np.where(c <= a, np.exp(-sd * (128 + c - a)), 0.0)
            E[:, h, 1, :] = np.where(c >= a, np.exp(-sd * (c - a)), 0.0)
        sd = slopes[h] * 16
        ka, km_ = a // 4, a % 4
        cq, cm = c // 4, c % 4
        for dl in range(5):
            rel = 32 * dl + cq - ka
            ok = (km_ == cm) & (rel >= 0) & (rel <= 128)
            E16[:, h, dl, :] = np.where(ok, np.exp(-sd * np.where(ok, rel, 0.0)), 0.0)
    E16[:, :, 0, :] += E4[:, :, 1, :]
    E16[:, :, 1, :] += E4[:, :, 0, :]
    E1 = E1.reshape(128, 8, 2, 32, 4).transpose(0, 1, 2, 4, 3).reshape(128, 8, 2, 128)
    return np.ascontiguousarray(E1.astype(np.float32)), E4.astype(np.float32), E16.astype(np.float32)


def _prepare_inputs(x, mem, g_mix, w_in, g_mem, w_mem_kv, w_pool, pool_scale, w_out, g_ffn, w_ff1, w_ff2, g_final):
    f = lambda a: np.ascontiguousarray(np.asarray(a, dtype=np.float32))
    x, mem = f(x), f(mem)
    E1, E4, E16 = _consts()
    gbs = np.stack([np.broadcast_to(f(g).reshape(1, D), (128, D)) for g in (g_mix[0], g_ffn[0], g_final, g_mem[0])])
    gbs = np.ascontiguousarray(gbs)
    wp = f(w_pool)[0]
    wpool = np.zeros((128, 2, 128), np.float32)
    for ch in range(2):
        wpool[0:64, ch, 0:64] = wp[2 * ch]
        wpool[64:128, ch, 64:128] = wp[2 * ch + 1]
    ps = f(pool_scale)[0]
    wins = (2, 4, 8, 16)
    shared = dict(w_in=f(w_in)[0], w_mem_kv=f(w_mem_kv)[0], w_out=f(w_out)[0], w_ff1=f(w_ff1)[0], w_ff2=f(w_ff2)[0],
                  gbs=gbs, cE1=E1, cE16=E16, cwpool=wpool, cident=np.eye(128, dtype=np.float32))
    in_maps = []
    for core in range(8):
        b, half = core // 2, core % 2
        xh = np.zeros(((NHALO + NOWN) * TOK, D), np.float32)
        if half == 1:
            xh[:] = x[b, SEQ // 2 - NHALO * TOK:]
        else:
            xh[NHALO * TOK:] = x[b, :SEQ // 2]
        csm = np.zeros((128, 8), np.float32)
        csm[:, 0] = float(half)
        csm[0:64, 5] = 0.125
        csm[64:128, 6] = 0.125
        cpc = np.ones((128, 2, 16), np.float32)
        for ch in range(2):
            csm[:, 1 + ch] = ps[ch * 128:(ch + 1) * 128]
            for hf in range(2):
                w = wins[2 * ch + hf]
                csm[64 * hf:64 * hf + 64, 3 + ch] = 1.0 / w
                if half == 0:
                    t = np.arange(16)
                    cpc[64 * hf:64 * hf + 64, ch, :] = (w / np.minimum(t + 1, w))[None, :]
        m = dict(shared)
        m.update(xh=xh, mem=np.ascontiguousarray(mem[b]), csm=csm, cpc=cpc)
        in_maps.append(m)
    return in_maps


_NC_CACHE = {}


def kernel(x, mem, g_mix, w_in, g_mem, w_mem_kv, w_pool, pool_scale, w_out, g_ffn, w_ff1, w_ff2, g_final):
    in_maps = _prepare_inputs(x, mem, g_mix, w_in, g_mem, w_mem_kv, w_pool, pool_scale, w_out, g_ffn, w_ff1, w_ff2,
                              g_final)
    if "nc" not in _NC_CACHE:
        _NC_CACHE["nc"] = build_nc()
    nc = _NC_CACHE["nc"]
    res = run_bass_kernel_spmd(nc, in_maps, core_ids=list(range(8)))
    out = np.zeros((NB, SEQ, D), np.float32)
    for core in range(8):
        b, half = core // 2, core % 2
        out[b, half * (SEQ // 2):(half + 1) * (SEQ // 2)] = res.results[core]["out"]
    return out
```

```python
import numpy as np
from contextlib import ExitStack

import concourse.bass as bass
import concourse.mybir as mybir
from concourse.bass_utils import run_bass_kernel_spmd

F32 = mybir.dt.float32
BF16 = mybir.dt.bfloat16
AF = mybir.ActivationFunctionType
ALU = mybir.AluOpType

D = 1024
SEQ = 8192
NB = 4
TOK = 512
NHALO = 4
NOWN = 8
EPS = 1e-6
NRING = 3
KSLOTS = 5

P_KVM = 0
P_WIN = 1
P_WOUT = 5
P_FF1 = 7
P_FF2 = 15
NPIECES = 23
FFN_ORDER = [("ff1", 0), ("ff1", 1), ("ff2", 0), ("ff1", 2), ("ff2", 1), ("ff1", 3), ("ff2", 2), ("ff2", 3)]


class Buf:
    __slots__ = ("name", "w", "r")

    def __init__(self, name):
        self.name = name
        self.w = None
        self.r = []


class DmaSem:
    __slots__ = ("sem", "val")

    def __init__(self, sem):
        self.sem = sem
        self.val = 0


class Tracker:
    def __init__(self, nc, es):
        self.nc = nc
        self.eng = {"pe": nc.tensor, "act": nc.scalar, "dve": nc.vector,
                    "pool": nc.gpsimd, "sp": nc.sync}
        self.sem = {k: es.enter_context(nc.semaphore("sem_" + k)) for k in self.eng}
        self.cnt = {k: 0 for k in self.eng}
        self.waited = {}
        self.nwaits = 0
        self.ninst = {k: 0 for k in self.eng}

    def wait(self, e, tok):
        if tok is None:
            return
        sem, val, prod = tok
        if prod == "pe" and e == "pe":
            return
        key = (e, id(sem))
        if self.waited.get(key, 0) >= val:
            return
        self.waited[key] = val
        self.eng[e].wait_ge(sem, val)
        self.nwaits += 1

    def _pre(self, e, reads, writes):
        for b in reads:
            self.wait(e, b.w)
        for b in writes:
            self.wait(e, b.w)
            for t in b.r:
                self.wait(e, t)

    def _post(self, tok, reads, writes):
        for b in reads:
            b.r = [t for t in b.r if t[0] is not tok[0]] + [tok]
        for b in writes:
            b.w = tok
            b.r = []

    def op(self, e, fn, reads=(), writes=(), mark=True):
        self._pre(e, reads, writes)
        ins = fn()
        self.ninst[e] += 1
        if mark:
            self.cnt[e] += 1
            ins.then_inc(self.sem[e], 1)
            tok = (self.sem[e], self.cnt[e], e)
        else:
            tok = (self.sem[e], self.cnt[e] + 1, e)
        self._post(tok, reads, writes)
        return tok

    def dma(self, e, dsem, out, in_, reads=(), writes=(), **kw):
        self._pre(e, reads, writes)
        ins = self.eng[e].dma_start(out=out, in_=in_, **kw)
        self.ninst[e] += 1
        dsem.val += 16
        ins.then_inc(dsem.sem, 16)
        tok = (dsem.sem, dsem.val, "dma")
        self._post(tok, reads, writes)
        return tok


def build_nc(n_own=NOWN, debug=False):
    nc = bass.Bass("TRN2", target_bir_lowering=False)
    NT = NHALO + n_own

    def din(name, shape, dtype=F32):
        return nc.dram_tensor(name, shape, dtype, kind="ExternalInput").ap()

    xh = din("xh", [(NHALO + NOWN) * TOK, D])
    memd = din("mem", [256, D])
    w_in = din("w_in", [D, 2048])
    w_kvm = din("w_mem_kv", [D, 512])
    w_out = din("w_out", [D, D])
    w_ff1 = din("w_ff1", [D, 4096])
    w_ff2 = din("w_ff2", [4096, D])
    gbs = din("gbs", [4, 128, D])
    cE1 = din("cE1", [128, 8, 2, 128])
    cE16 = din("cE16", [128, 8, 5, 128])
    cwpool = din("cwpool", [128, 2, 128])
    csm = din("csm", [128, 8])
    cpc = din("cpc", [128, 2, 16])
    cident = din("cident", [128, 128])
    outd = nc.dram_tensor("out", [NOWN * TOK, D], F32, kind="ExternalOutput").ap()
    sc = nc.dram_tensor("wscratch", [NPIECES, 128, 4096], BF16).ap()
    dbg = {}
    if debug:
        for nm, shp in debug.items():
            dbg[nm] = nc.dram_tensor("dbg_" + nm, shp, F32, kind="ExternalOutput").ap()

    with ExitStack() as es:
        T = Tracker(nc, es)

        def sb(name, shape, dtype):
            return es.enter_context(nc.sbuf_tensor(name, shape, dtype))

        def psum(name, shape, dtype):
            return es.enter_context(nc.psum_tensor(name, shape, dtype))

        def dsem(name):
            return DmaSem(es.enter_context(nc.semaphore(name)))

        ring = [sb(f"ring{i}", [128, 4096], BF16) for i in range(NRING)]
        b_ring = [Buf(f"ring{i}") for i in range(NRING)]
        ring_sem = [dsem(f"ringsem{i}") for i in range(NRING)]
        KT = sb("KT", [128, 4, KSLOTS * TOK], BF16)
        b_KT = [Buf(f"KT{i}") for i in range(KSLOTS)]
        Vd1 = sb("Vd1", [128, 5, 4 * 192], BF16)
        b_Vd1 = [Buf(f"Vd1_{i}") for i in range(5)]
        Vd4 = sb("Vd4", [128, KSLOTS, 4, 4 * 192], BF16)
        b_Vd4 = [Buf(f"Vd4_{i}") for i in range(KSLOTS)]
        big = sb("big", [128, 8192], BF16)
        b_big = [Buf("bigA"), Buf("bigB")]
        aT = [big[:, 0:4096].rearrange("p (f t) -> p f t", f=8),
              big[:, 4096:8192].rearrange("p (f t) -> p f t", f=8)]
        QA = big[:, 0:2048].rearrange("p (c t) -> p c t", c=4)
        QB = big[:, 2048:4096].rearrange("p (c t) -> p c t", c=4)
        qmA = big[:, 4096:5120].rearrange("p (c t) -> p c t", c=2)
        qmB = big[:, 5120:6144].rearrange("p (c t) -> p c t", c=2)
        VT = big[:, 6144:8192].rearrange("p (c t) -> p c t", c=4)
        U = sb("U", [128, 2, 16 + TOK], F32)
        b_U = Buf("U")
        S1 = sb("S1", [128, 16 + TOK], F32)
        S2 = sb("S2", [128, 16 + TOK], F32)
        b_S1, b_S2 = Buf("S1"), Buf("S2")
        dT = sb("dT", [128, 2, TOK], BF16)
        b_dT = Buf("dT")
        yT = sb("yT", [128, 8, TOK], BF16)
        b_yT = [Buf(f"yT{i}") for i in range(8)]
        E1 = sb("E1", [128, 8, 2, 128], BF16)
        E16 = sb("E16", [128, 8, 5, 128], BF16)
        b_E = Buf("E")
        xa = [sb(f"xa{i}", [128, D], F32) for i in range(2)]
        b_xa = [Buf(f"xa{i}") for i in range(2)]
        xa_sem = [dsem(f"xasem{i}") for i in range(2)]
        x1 = [sb(f"x1_{i}", [128, D], F32) for i in range(4)]
        b_x1 = [Buf(f"x1_{i}") for i in range(4)]
        x1_sem = [dsem(f"x1sem{i}") for i in range(4)]
        out_sem = [dsem(f"outsem{i}") for i in range(4)]
        hbs = [sb(f"hb{i}", [128, D], BF16) for i in range(2)]
        b_hbs = [Buf(f"hb{i}") for i in range(2)]
        junk = sb("junk", [128, D], BF16)
        hT = sb("hT", [128, 8, TOK], BF16)
        b_hT = Buf("hT")
        h2T = sb("h2T", [128, 8, TOK], BF16)
        b_h2T = Buf("h2T")
        NPB = 5
        Pbuf = [sb(f"P{i}", [128, 4, 128], BF16) for i in range(NPB)]
        b_P = [Buf(f"P{i}") for i in range(NPB)]
        rec = [sb(f"rec{i}", [128, TOK], F32) for i in range(2)]
        b_rec = [Buf(f"rec{i}") for i in range(2)]
        gb_mix = sb("gb_mix", [128, D], F32)
        gb_ffn = sb("gb_ffn", [128, D], F32)
        gb_fin = sb("gb_fin", [128, D], F32)
        b_g = Buf("g")
        kmT = sb("kmT", [128, 2, 256], BF16)
        vmA = sb("vmA", [128, 2, 2 * 192], BF16)
        b_km, b_vm = Buf("km"), Buf("vm")
        wpool = sb("wpool", [128, 2, 128], BF16)
        csm_t = sb("csm_t", [128, 8], F32)
        cpc_t = sb("cpc_t", [128, 2, 16], F32)
        b_c = Buf("consts")
        ident = sb("ident", [128, 128], BF16)
        b_id = Buf("ident")
        stat = sb("stat", [128, 3, 8], F32)
        b_stat = [Buf(f"stat{i}") for i in range(8)]

        mm = [psum(f"mm{i}", [128, 512], F32) for i in range(2)]
        b_mm = [Buf(f"mm{i}") for i in range(2)]
        scb = [psum(f"sc{i}", [128, 4, 128], F32) for i in range(3)]
        b_sc = [Buf(f"sc{i}") for i in range(3)]
        scb = [t_[:] for t_ in scb] + [m_[:].rearrange("p (a b) -> p a b", a=4) for m_ in mm]
        b_sc = b_sc + b_mm
        NSC = 5
        ob = [psum(f"ob{i}", [128, 4, 128], F32) for i in range(2)]
        b_ob = [Buf(f"ob{i}") for i in range(2)]
        tp = psum("tp", [128, 1024], BF16)
        b_tp = Buf("tp")
        b_tph = [b_tp, b_tp]

        b_sc_piece = [Buf(f"scp{i}") for i in range(NPIECES)]
        conv_sem = [dsem(f"cv{i}") for i in range(NPIECES)]

        def conv(idx, src, pat_out, kw):
            T.dma("pool", conv_sem[idx], sc[idx].rearrange(pat_out, **kw), src, writes=[b_sc_piece[idx]])

        def conv_k(idx, wsrc, c0):
            conv(idx, wsrc[:, c0:c0 + 512].rearrange("(k p) c -> p k c", p=128), "p (k c) -> p k c", dict(k=8))

        hv_col = csm_t[:, 0:1]

        id_sem = dsem("idsem")
        T.dma("pool", id_sem, ident[:], cident[:, :], writes=[b_id])
        T.op("dve", lambda: nc.vector.memset(U[:], 0.0), writes=[b_U])
        T.op("dve", lambda: nc.vector.memset(vmA[:], 1.0), writes=[b_vm])

        g_sem = dsem("gsem")
        T.dma("sp", g_sem, gb_mix[:], gbs[0])
        T.dma("sp", g_sem, csm_t[:], csm[:, :])
        xa_state = dict(n=0)

        def load_xa_rows(src_rows):
            s_ = xa_state["n"] % 2
            xa_state["n"] += 1
            T.dma("sp", xa_sem[s_], xa[s_][:], src_rows, writes=[b_xa[s_]])
            return s_

        def load_xa(j, c):
            r0 = j * TOK + c * 128
            return load_xa_rows(xh[r0:r0 + 128, :])

        a_slots = {}

        def stage_A_begin(j):
            a_slots[j] = [load_xa(j, 0), load_xa(j, 1)]

        stage_A_begin(0)
        T.dma("sp", g_sem, cpc_t[:], cpc[:, :, :])
        T.dma("sp", g_sem, x1[0][:], gbs[3])
        T.dma("sp", g_sem, gb_ffn[:], gbs[1])
        T.dma("sp", g_sem, gb_fin[:], gbs[2])
        totg = (g_sem.sem, g_sem.val, "dma")
        b_g.w = totg
        b_x1[0].w = totg

        def direct_piece(dst2d, b_dst, dsem_, wsrc, c0, ncols=512):
            T.dma("pool", dsem_, dst2d.rearrange("p (k c) -> p k c", k=8),
                  wsrc[:, c0:c0 + ncols].rearrange("(k p) c -> p k c", p=128), writes=b_dst)

        halo_pieces = {}
        halo_sem = [dsem(f"halosem{i}") for i in range(NRING)]
        for pi, slot in ((1, 1), (2, 2), (3, 0)):
            direct_piece(ring[slot][:], [b_ring[slot]], halo_sem[slot], w_in, 512 * pi)
            halo_pieces[pi] = (ring[slot], b_ring[slot])
        kvm_sem = dsem("kvmsem")
        kvm_piece = yT[:].rearrange("p a t -> p (a t)")
        direct_piece(kvm_piece, b_yT, kvm_sem, w_kvm, 0)
        c_sem = dsem("csem")
        T.dma("pool", c_sem, E1[:], cE1[:, :, :, :])
        T.dma("pool", c_sem, E16[:], cE16[:, :, :, :])
        T.dma("pool", c_sem, wpool[:], cwpool[:, :, :])
        tot = (c_sem.sem, c_sem.val, "dma")
        b_E.w = tot
        b_c.w = tot
        for i in (0, 1, 2, 3):
            conv_k(P_WIN + i, w_in, 512 * i)
        for i in range(2):
            conv_k(P_WOUT + i, w_out, 512 * i)
        for kind, q in FFN_ORDER:
            for i in (2 * q, 2 * q + 1):
                if kind == "ff1":
                    conv_k(P_FF1 + i, w_ff1, 512 * i)
                else:
                    conv(P_FF2 + i, w_ff2[512 * i:512 * i + 512, :].rearrange("(f p) c -> p f c", p=128),
                         "p (f c) -> p f c", dict(f=4))

        seq = []
        for j in range(NT):
            if j < NHALO:
                pass
            else:
                seq += [P_WIN + 0, P_WIN + 1, P_WIN + 2, P_WIN + 3, P_WOUT, P_WOUT + 1]
                for kind, q in FFN_ORDER:
                    base = P_FF1 if kind == "ff1" else P_FF2
                    seq += [base + 2 * q, base + 2 * q + 1]
        rs = dict(issued=0, taken=0)

        def ring_issue():
            n = rs["issued"]
            if n >= len(seq):
                return
            s = n % NRING
            idx = seq[n]
            T.dma("sp", ring_sem[s], ring[s][:], sc[idx], reads=[b_sc_piece[idx]], writes=[b_ring[s]])
            rs["issued"] = n + 1

        def ring_take(expect):
            n = rs["taken"]
            assert seq[n] == expect, (n, seq[n], expect)
            while rs["issued"] <= n:
                ring_issue()
            rs["taken"] = n + 1
            s = n % NRING
            return ring[s], b_ring[s]


        cnt = dict(stat=0, mm=0, sc=0, P=0, tp=0, hb=0, rec=0)

        def rms_hb(src, b_src, gb, extra_reads=()):
            hi = cnt["hb"] % 2
            cnt["hb"] += 1
            dst_hb, b_dst = hbs[hi][:], b_hbs[hi]
            i = cnt["stat"] % 8
            cnt["stat"] += 1
            bs = b_stat[i]
            ss, ln, rs_ = stat[:, 0, i:i + 1], stat[:, 1, i:i + 1], stat[:, 2, i:i + 1]
            T.op("act", lambda: nc.scalar.activation(out=junk[:], in_=src, func=AF.Square, accum_out=ss),
                 reads=[b_src], writes=[bs])
            T.op("act", lambda: nc.scalar.activation(out=ln, in_=ss, func=AF.Ln, scale=1.0 / D, bias=EPS),
                 reads=[bs], writes=[bs])
            T.op("act", lambda: nc.scalar.activation(out=rs_, in_=ln, func=AF.Exp, scale=-0.5),
                 reads=[bs], writes=[bs])
            T.op("dve", lambda: nc.vector.scalar_tensor_tensor(out=dst_hb, in0=src, scalar=rs_, in1=gb,
                                                               op0=ALU.mult, op1=ALU.mult),
                 reads=[b_src, bs, b_g] + list(extra_reads), writes=[b_dst])
            return hbs[hi], b_dst

        def transpose_to(hb, b_hb, dst3, b_dst):
            for k in range(8):
                T.op("pe", lambda: nc.tensor.transpose(tp[:, k * 128:(k + 1) * 128], hb[:, k * 128:(k + 1) * 128],
                                                       ident[:]),
                     reads=[b_hb, b_id], writes=[b_tp], mark=(k == 7))
            T.op("dve", lambda: nc.vector.tensor_copy(out=dst3, in_=tp[:].rearrange("p (k t) -> p k t", k=8)),
                 reads=[b_tp], writes=[b_dst])

        def next_mm():
            i = cnt["mm"] % 2
            cnt["mm"] += 1
            return mm[i], b_mm[i]

        def mm_group(out_ap, b_out, pairs, reads):
            n = len(pairs)
            for i, (l, r) in enumerate(pairs):
                T.op("pe", lambda: nc.tensor.matmul(out_ap, lhsT=l, rhs=r, start=(i == 0), stop=(i == n - 1)),
                     reads=reads, writes=[b_out], mark=(i == n - 1))

        def mem_phase():
            pk = kvm_piece.rearrange("p (k c) -> p k c", k=8)
            mT, b_mT = aT[0], b_big[0]
            for c in range(2):
                s_ = load_xa_rows(memd[c * 128:(c + 1) * 128, :])
                h_, bh_ = rms_hb(xa[s_][:], b_xa[s_], x1[0][:], extra_reads=[b_x1[0]])
                transpose_to(h_, bh_, mT[:, :, c * 128:(c + 1) * 128], b_mT)
            for mp in range(2):
                m_, bm_ = next_mm()
                mm_group(m_[:, 0:256], bm_, [(pk[:, k, mp * 128:(mp + 1) * 128], mT[:, k, 0:256]) for k in range(8)],
                         b_yT + [b_mT])
                T.op("dve", lambda: nc.vector.tensor_copy(out=kmT[:, mp, :], in_=m_[:, 0:256]), reads=[bm_], writes=[b_km])
            for kc in range(2):
                m_, bm_ = next_mm()
                mm_group(m_[:, 0:256], bm_, [(mT[:, k, kc * 128:(kc + 1) * 128], pk[:, k, 256:512]) for k in range(8)],
                         b_yT + [b_mT])
                T.op("dve", lambda: nc.vector.tensor_copy(
                    out=vmA[:, kc, :].rearrange("p (c s d) -> p c s d", c=2, s=3)[:, :, 0::2, :],
                    in_=m_[:, 0:256].rearrange("p (c s d) -> p c s d", c=2, s=2)), reads=[bm_], writes=[b_vm])

        a_hb = {}

        def stage_A_elem(j, c):
            slots = a_slots[j]
            s = slots[c]
            a_hb[(j, c)] = rms_hb(xa[s][:], b_xa[s], gb_mix[:])
            if c + 2 < 4:
                slots.append(load_xa(j, c + 2))

        def stage_A_pe(j, c):
            h_, bh_ = a_hb.pop((j, c))
            hd, b_hd = hbuf(j)
            transpose_to(h_, bh_, hd[:, :, c * 128:(c + 1) * 128], b_hd)

        def stage_A_chunk(j, c):
            stage_A_elem(j, c)
            stage_A_pe(j, c)

        def stage_A(j):
            stage_A_begin(j)
            for c in range(4):
                stage_A_chunk(j, c)

        def hbuf(j):
            if j < NHALO and j % 2 == 1:
                return h2T, b_h2T
            return hT, b_hT

        def stage_B(j, hooks=None):
            own = j >= NHALO
            kslot = j % KSLOTS
            hsrc, b_hsrc = hbuf(j)
            plist = [1, 2] if j < NHALO - 1 else ([1, 2, 3] if j == NHALO - 1 else [0, 1, 2, 3])
            for ip, pi in enumerate(plist):
                if own:
                    piece, b_piece = ring_take(P_WIN + pi)
                else:
                    piece, b_piece = halo_pieces[pi]
                pk = piece[:].rearrange("p (k c) -> p k c", k=8)
                for ol in range(4):
                    if pi == 3 and (not own) and ol >= 2:
                        continue
                    m_, bm_ = next_mm()
                    mm_group(m_[:], bm_, [(pk[:, k, ol * 128:(ol + 1) * 128], hsrc[:, k, :]) for k in range(8)],
                             [b_piece, b_hsrc])
                    if pi == 0:
                        T.op("dve", lambda: nc.vector.tensor_scalar(
                            out=QA[:, ol, :].rearrange("p (r c) -> p r c", r=4),
                            in0=m_[:, :].rearrange("p (c r) -> p r c", r=4), scalar1=csm_t[:, 5:6], scalar2=None,
                            op0=ALU.mult),
                            reads=[bm_, b_g], writes=[b_big[0]])
                        T.op("dve", lambda: nc.vector.tensor_scalar(
                            out=QB[:, ol, :].rearrange("p (r c) -> p r c", r=4),
                            in0=m_[:, :].rearrange("p (c r) -> p r c", r=4), scalar1=csm_t[:, 6:7], scalar2=None,
                            op0=ALU.mult),
                            reads=[bm_, b_g], writes=[b_big[0]])
                    elif pi == 1:
                        T.op("dve", lambda: nc.vector.tensor_copy(out=KT[:, ol, kslot * TOK:(kslot + 1) * TOK], in_=m_[:]),
                             reads=[bm_], writes=[b_KT[kslot]])
                    elif pi == 2:
                        T.op("act", lambda: nc.scalar.copy(out=VT[:, ol, :], in_=m_[:]), reads=[bm_], writes=[b_big[1]])
                    elif ol < 2:
                        T.op("dve", lambda: nc.vector.tensor_copy(out=U[:, ol, 16:16 + TOK], in_=m_[:]),
                             reads=[bm_], writes=[b_U])
                    else:
                        mp = ol - 2
                        T.op("dve", lambda: nc.vector.tensor_scalar(
                            out=qmA[:, mp, :].rearrange("p (r c) -> p r c", r=4),
                            in0=m_[:, :].rearrange("p (c r) -> p r c", r=4), scalar1=csm_t[:, 5:6], scalar2=None,
                            op0=ALU.mult),
                            reads=[bm_, b_g], writes=[b_big[1]])
                        T.op("dve", lambda: nc.vector.tensor_scalar(
                            out=qmB[:, mp, :].rearrange("p (r c) -> p r c", r=4),
                            in0=m_[:, :].rearrange("p (c r) -> p r c", r=4), scalar1=csm_t[:, 6:7], scalar2=None,
                            op0=ALU.mult),
                            reads=[bm_, b_g], writes=[b_big[1]])
                if own:
                    ring_issue()
                if hooks is not None and ip < len(hooks):
                    hooks[ip]()

        vb_state = dict(n=0)

        def vblock(src_of_pair, dst, b_dst, ones_from_hv):
            hf = vb_state["n"] % 2
            vb_state["n"] += 1
            t_ = tp[:, hf * 512:(hf + 1) * 512]
            bt_ = b_tph[hf]
            for p_ in range(4):
                T.op("pe", lambda: nc.tensor.transpose(t_[:, p_ * 128:(p_ + 1) * 128], src_of_pair(p_), ident[:]),
                     reads=[b_big[1], b_id], writes=[bt_], mark=(p_ == 3))
            d4 = dst.rearrange("p (c s d) -> p c s d", c=4, s=3)
            if hf == 0:
                T.op("act", lambda: nc.scalar.copy(out=d4[:, :, 0::2, :],
                                                   in_=t_.rearrange("p (c s d) -> p c s d", c=4, s=2)),
                     reads=[bt_], writes=[b_dst])
            else:
                T.op("dve", lambda: nc.vector.tensor_copy(out=d4[:, :, 0::2, :],
                                                          in_=t_.rearrange("p (c s d) -> p c s d", c=4, s=2)),
                     reads=[bt_], writes=[b_dst])
            if ones_from_hv:
                T.op("dve", lambda: nc.vector.tensor_copy(out=d4[:, :, 1, :],
                                                          in_=hv_col.unsqueeze(1).to_broadcast([128, 4, 64])),
                     reads=[b_g], writes=[b_dst])
            else:
                T.op("dve", lambda: nc.vector.memset(d4[:, :, 1, :], 1.0), writes=[b_dst])

        def stage_C(j):
            own = j >= NHALO
            s4 = j % KSLOTS
            for r in range(4):
                vblock(lambda p_: VT[:, p_, :].rearrange("p (a s) -> p a s", s=4)[:, :, r],
                       Vd4[:, s4, r, :], b_Vd4[s4], not own)
            if j >= NHALO - 1:
                for b in range(4):
                    g = (4 * j + b) % 5
                    if j == NHALO - 1 and b < 3:
                        continue
                    vblock(lambda p_: VT[:, p_, b * 128:(b + 1) * 128], Vd1[:, g, :], b_Vd1[g], not own)

        def stage_attn(j, side_ops=()):
            batches = []
            cur = j % KSLOTS
            prv = (j - 1) % KSLOTS
            for h in range(8):
                pr, hh = h // 2, h % 2
                Qz = (QA if hh == 0 else QB)[:, pr, :]
                v0 = pr * 192 + 64 * hh
                o_ = ob[h % 2]
                for bt in range(2):
                    qk, pv, pvr = [], [], []
                    for qi in range(2):
                        qb = 2 * bt + qi
                        qv = Qz.rearrange("p (r c) -> p r c", r=4)[:, :, 32 * qb:32 * qb + 32]
                        gs, gp = (4 * j + qb) % 5, (4 * j + qb - 1) % 5
                        if qb == 0:
                            kprev = KT[:, pr, prv * TOK + 384: prv * TOK + 512]
                        else:
                            kprev = KT[:, pr, cur * TOK + (qb - 1) * 128: cur * TOK + qb * 128]
                        ksame = KT[:, pr, cur * TOK + qb * 128: cur * TOK + (qb + 1) * 128]
                        oo = o_[:, :, 32 * qb:32 * qb + 32]
                        qk.append((2 * qi, kprev, qv))
                        qk.append((2 * qi + 1, ksame, qv))
                        pv.append((2 * qi, Vd1[:, gp, v0:v0 + 128], oo))
                        pv.append((2 * qi + 1, Vd1[:, gs, v0:v0 + 128], oo))
                        pvr += [b_Vd1[gp], b_Vd1[gs]]
                    batches.append(dict(qk=qk, qk_reads=[b_KT[cur], b_KT[prv], b_big[0]],
                                        mask=E1[:, h, :, :].unsqueeze(1).to_broadcast([128, 2, 2, 128]), meng="dve",
                                        pv=pv, pv_reads=pvr, obank=h % 2,
                                        first=(bt == 0), last=False, post=None))
                for dl in range(5):
                    slot = (j - dl) % KSLOTS
                    qk, pv = [], []
                    for r in range(4):
                        qv = Qz[:, r * 128:(r + 1) * 128]
                        kv = KT[:, pr, slot * TOK:(slot + 1) * TOK].rearrange("p (a s) -> p a s", s=4)[:, :, r]
                        qk.append((r, kv, qv))
                        pv.append((r, Vd4[:, slot, r, v0:v0 + 128], o_[:, r, :]))
                    batches.append(dict(qk=qk, qk_reads=[b_KT[slot], b_big[0]],
                                        mask=E16[:, h, dl, :].unsqueeze(1).to_broadcast([128, 4, 128]),
                                        meng="dve",
                                        pv=pv, pv_reads=[b_Vd4[slot]], obank=h % 2,
                                        first=False, last=(dl == 4),
                                        post=(h // 2, hh) if dl == 4 else None))
            for mh in range(4):
                mp, hh = mh // 2, mh % 2
                Qz = (qmA if hh == 0 else qmB)[:, mp, :]
                v0 = mp * 192 + 64 * hh
                for kc in range(2):
                    batches.append(dict(qk=[(None, kmT[:, mp, kc * 128:(kc + 1) * 128], Qz)],
                                        qk_reads=[b_km, b_big[1]], mask=None, meng=None,
                                        pv=[(None, vmA[:, kc, v0:v0 + 128], None)],
                                        pv_reads=[b_vm], obank=mh % 2, first=(kc == 0), last=(kc == 1),
                                        post=(6 + mh // 2, hh) if kc == 1 else None))

            def emit_front(bt):
                si = cnt["sc"] % NSC
                cnt["sc"] += 1
                pi = cnt["P"] % NPB
                cnt["P"] += 1
                s_, bs_ = scb[si], b_sc[si]
                P_, bP_ = Pbuf[pi], b_P[pi]
                n = len(bt["qk"])
                for i, (sec, l, r) in enumerate(bt["qk"]):
                    o2 = s_[:, sec, :] if sec is not None else s_.rearrange("p a b -> p (a b)")
                    if len(r.shape) == 3:
                        o2 = o2.rearrange("p (a b) -> p a b", a=r.shape[1])
                    T.op("pe", lambda: nc.tensor.matmul(o2, lhsT=l, rhs=r, start=True, stop=True),
                         reads=bt["qk_reads"], writes=[bs_], mark=(i == n - 1))
                T.op("act", lambda: nc.scalar.activation(out=P_[:], in_=s_, func=AF.Exp), reads=[bs_], writes=[bP_])
                if bt["mask"] is not None:
                    m = bt["mask"]
                    Pv = P_[:].rearrange("p (a b) q -> p a b q", a=2) if len(m.shape) == 4 else P_[:]
                    if bt["meng"] == "pool":
                        T.op("pool", lambda: nc.gpsimd.tensor_tensor(out=Pv, in0=Pv, in1=m, op=ALU.mult),
                             reads=[bP_, b_E], writes=[bP_])
                    else:
                        T.op("dve", lambda: nc.vector.tensor_tensor(out=Pv, in0=Pv, in1=m, op=ALU.mult),
                             reads=[bP_, b_E], writes=[bP_])
                bt["P"] = (P_, bP_)

            def emit_back(bt):
                P_, bP_ = bt["P"]
                o_b = ob[bt["obank"]]
                bo_ = b_ob[bt["obank"]]
                n = len(bt["pv"])
                for i, (sec, l, oo) in enumerate(bt["pv"]):
                    if sec is None:
                        r_ = P_[:].rearrange("p a b -> p (a b)")
                        oo = o_b[:].rearrange("p a b -> p (a b)")
                    else:
                        r_ = P_[:, sec, :]
                        if len(oo.shape) == 3:
                            r_ = r_.rearrange("p (a b) -> p a b", a=oo.shape[1])
                    T.op("pe", lambda: nc.tensor.matmul(oo, lhsT=l, rhs=r_, start=(bt["first"] and i == 0),
                                                        stop=(bt["last"] and i == n - 1), skip_group_check=True),
                         reads=bt["pv_reads"] + [bP_], writes=[bo_], mark=(i == n - 1))
                if bt["post"] is None:
                    return
                ychunk, hh = bt["post"]
                lo, o2_ = 64 * hh, 64 - 64 * hh
                of = o_b[:].rearrange("p a b -> p (a b)")
                ri = cnt["rec"] % 2
                cnt["rec"] += 1
                T.op("act", lambda: nc.scalar.activation(out=rec[ri][lo:lo + 64, :], in_=of[o2_:o2_ + 64, :], func=AF.Ln),
                     reads=[bo_], writes=[b_rec[ri]])
                T.op("act", lambda: nc.scalar.activation(out=rec[ri][lo:lo + 64, :], in_=rec[ri][lo:lo + 64, :], func=AF.Exp,
                                                         scale=-1.0),
                     reads=[b_rec[ri]], writes=[b_rec[ri]])
                T.op("dve", lambda: nc.vector.tensor_tensor(out=yT[lo:lo + 64, ychunk, :], in0=of[lo:lo + 64, :],
                                                            in1=rec[ri][lo:lo + 64, :], op=ALU.mult),
                     reads=[bo_, b_rec[ri]], writes=[b_yT[ychunk]])

            LAG = 4
            side = list(side_ops)
            for i, bt in enumerate(batches):
                emit_front(bt)
                if i >= LAG:
                    emit_back(batches[i - LAG])
                if side and i % 3 == 2:
                    side.pop(0)()
            for bt in batches[-LAG:]:
                emit_back(bt)
            for op_ in side:
                op_()

        def stage_pool(j):
            first_own = (j == NHALO)
            W = 16 + TOK
            ops = []

            def add(out, in0, in1, rd, wr):
                ops.append(lambda out=out, in0=in0, in1=in1, rd=rd, wr=wr: T.op(
                    "dve", lambda: nc.vector.tensor_tensor(out=out, in0=in0, in1=in1, op=ALU.add), reads=rd, writes=wr))

            for ch in range(2):
                Uc = U[:, ch, :]
                add(S1[:, 1:W], Uc[:, 1:W], Uc[:, 0:W - 1], [b_U], [b_S1])
                if ch == 0:
                    add(S2[64:128, 3:W], S1[64:128, 3:W], S1[64:128, 1:W - 2], [b_S1], [b_S2])
                else:
                    add(S2[:, 3:W], S1[:, 3:W], S1[:, 1:W - 2], [b_S1], [b_S2])
                    add(S1[:, 7:W], S2[:, 7:W], S2[:, 3:W - 4], [b_S2], [b_S1])
                    add(S2[64:128, 15:W], S1[64:128, 15:W], S1[64:128, 7:W - 8], [b_S1], [b_S2])
                if first_own:
                    for src_, bsrc_, p0 in ((S1, b_S1, 0), (S2, b_S2, 64)):
                        ops.append(lambda src_=src_, bsrc_=bsrc_, p0=p0, ch=ch: T.op(
                            "dve", lambda: nc.vector.tensor_tensor(out=src_[p0:p0 + 64, 16:32], in0=src_[p0:p0 + 64, 16:32],
                                                                   in1=cpc_t[p0:p0 + 64, ch, :], op=ALU.mult),
                            reads=[b_g], writes=[bsrc_]))
                inv = csm_t[:, 3 + ch:4 + ch]
                for src_, bsrc_, p0 in ((S1, b_S1, 0), (S2, b_S2, 64)):
                    ops.append(lambda src_=src_, bsrc_=bsrc_, p0=p0, ch=ch, Uc=Uc, inv=inv: T.op(
                        "dve", lambda: nc.vector.scalar_tensor_tensor(out=dT[p0:p0 + 64, ch, :], in0=src_[p0:p0 + 64, 16:W],
                                                                      scalar=inv[p0:p0 + 64, :], in1=Uc[p0:p0 + 64, 16:W],
                                                                      op0=ALU.mult, op1=ALU.subtract),
                        reads=[bsrc_, b_U, b_g], writes=[b_dT]))
            ops.append(lambda: T.op("dve", lambda: nc.vector.tensor_copy(out=U[:, :, 0:16], in_=U[:, :, TOK:TOK + 16]),
                                    reads=[b_U], writes=[b_U]))
            return ops

        def stage_pool_mm(j):
            for ch in range(2):
                m_, bm_ = next_mm()
                mm_group(m_[:].rearrange("p (r c) -> p r c", r=4), bm_,
                         [(wpool[:, ch, :], dT[:, ch, :].rearrange("p (c r) -> p r c", r=4))], [b_c, b_dT])
                T.op("act", lambda: nc.scalar.activation(out=yT[:, 4 + ch, :], in_=m_[:], func=AF.Copy,
                                                         scale=csm_t[:, 1 + ch:2 + ch]),
                     reads=[bm_, b_g], writes=[b_yT[4 + ch]])

        def u_margin():
            T.op("dve", lambda: nc.vector.tensor_copy(out=U[:, :, 0:16], in_=U[:, :, TOK:TOK + 16]),
                 reads=[b_U], writes=[b_U])

        def stage_J_chunk(tc):
            h_, bh_ = rms_hb(x1[tc][:], b_x1[tc], gb_ffn[:])
            transpose_to(h_, bh_, h2T[:, :, tc * 128:(tc + 1) * 128], b_h2T)

        def stage_IJ(j):
            pcs = [ring_take(P_WOUT), ring_take(P_WOUT + 1)]
            for tc in range(4):
                for ch in range(2):
                    piece, b_piece = pcs[ch]
                    pk = piece[:].rearrange("p (k c) -> p k c", k=8)
                    m_, bm_ = next_mm()
                    for k in range(8):
                        T.op("pe", lambda: nc.tensor.matmul(m_[:], lhsT=yT[:, k, tc * 128:(tc + 1) * 128], rhs=pk[:, k, :],
                                                            start=(k == 0), stop=(k == 7)),
                             reads=[b_piece, b_yT[k]], writes=[bm_], mark=(k == 7))
                    xs = x1[tc][:, ch * 512:(ch + 1) * 512]
                    T.op("dve", lambda: nc.vector.tensor_tensor(out=xs, in0=xs, in1=m_[:], op=ALU.add),
                         reads=[bm_], writes=[b_x1[tc]])
                if tc >= 2:
                    stage_J_chunk(tc - 2)
            ring_issue()
            ring_issue()
            stage_J_chunk(2)
            stage_J_chunk(3)

        def ff1_quarter(q):
            a_, ba_ = aT[q % 2], b_big[q % 2]
            pcs = [ring_take(P_FF1 + 2 * q), ring_take(P_FF1 + 2 * q + 1)]
            for fl in range(8):
                piece, b_piece = pcs[fl // 4]
                pk = piece[:].rearrange("p (k c) -> p k c", k=8)
                c0 = (fl % 4) * 128
                m_, bm_ = next_mm()
                mm_group(m_[:], bm_, [(pk[:, k, c0:c0 + 128], h2T[:, k, :]) for k in range(8)], [b_piece, b_h2T])
                rt, brt = (S1, b_S1) if fl % 2 == 0 else (S2, b_S2)
                T.op("act", lambda: nc.scalar.activation(out=rt[:, 0:TOK], in_=m_[:], func=AF.Relu),
                     reads=[bm_], writes=[brt])
                T.op("dve", lambda: nc.vector.tensor_tensor(out=a_[:, fl, :], in0=rt[:, 0:TOK], in1=rt[:, 0:TOK],
                                                            op=ALU.mult),
                     reads=[brt], writes=[ba_])
                if fl % 4 == 3:
                    ring_issue()

        def ff2_quarter(q):
            a_, ba_ = aT[q % 2], b_big[q % 2]
            pcs = [ring_take(P_FF2 + 2 * q), ring_take(P_FF2 + 2 * q + 1)]
            for tc in range(4):
                for ch in range(2):
                    m_, bm_ = next_mm()
                    prs = []
                    for fl in range(8):
                        piece, b_piece = pcs[fl // 4]
                        pf = piece[:].rearrange("p (f c) -> p f c", f=4)
                        prs.append((a_[:, fl, tc * 128:(tc + 1) * 128], pf[:, fl % 4, ch * 512:(ch + 1) * 512]))
                    mm_group(m_[:], bm_, prs, [pcs[0][1], pcs[1][1], ba_])
                    xs = x1[tc][:, ch * 512:(ch + 1) * 512]
                    T.op("dve", lambda: nc.vector.tensor_tensor(out=xs, in0=xs, in1=m_[:], op=ALU.add),
                         reads=[bm_], writes=[b_x1[tc]])
            ring_issue()
            ring_issue()

        def stage_K(j, nxt):
            if nxt is not None:
                stage_A_begin(nxt)
            for kind, q in FFN_ORDER:
                if kind == "ff1":
                    if nxt is not None:
                        stage_A_elem(nxt, q)
                    ff1_quarter(q)
                    if nxt is not None:
                        stage_A_pe(nxt, q)
                else:
                    ff2_quarter(q)

        def stage_L(j):
            jo = j - NHALO
            ov = outd[jo * TOK:(jo + 1) * TOK, :].rearrange("(c r) d -> r c d", r=4)
            for tc in range(4):
                i = cnt["stat"] % 8
                cnt["stat"] += 1
                bs = b_stat[i]
                ss, ln, rs_ = stat[:, 0, i:i + 1], stat[:, 1, i:i + 1], stat[:, 2, i:i + 1]
                T.op("act", lambda: nc.scalar.activation(out=junk[:], in_=x1[tc][:], func=AF.Square, accum_out=ss),
                     reads=[b_x1[tc]], writes=[bs])
                T.op("act", lambda: nc.scalar.activation(out=ln, in_=ss, func=AF.Ln, scale=1.0 / D, bias=EPS),
                     reads=[bs], writes=[bs])
                T.op("act", lambda: nc.scalar.activation(out=rs_, in_=ln, func=AF.Exp, scale=-0.5),
                     reads=[bs], writes=[bs])
                T.op("dve", lambda: nc.vector.scalar_tensor_tensor(out=x1[tc][:], in0=x1[tc][:], scalar=rs_, in1=gb_fin[:],
                                                                   op0=ALU.mult, op1=ALU.mult),
                     reads=[bs, b_g], writes=[b_x1[tc]])
                T.dma("sp", out_sem[tc], ov[tc], x1[tc][:], reads=[b_x1[tc]])

        def load_x1(j):
            xv = xh[j * TOK:(j + 1) * TOK, :].rearrange("(c r) d -> r c d", r=4)
            for tc in range(4):
                T.dma("sp", x1_sem[tc], x1[tc][:], xv[tc], writes=[b_x1[tc]])

        for c in range(4):
            stage_A_chunk(0, c)
        for j in range(NT):
            own = j >= NHALO
            nxt = j + 1 if j + 1 < NT else None
            if own:
                load_x1(j)
                stage_B(j)
                stage_C(j)
                stage_attn(j, stage_pool(j))
                stage_pool_mm(j)
                stage_IJ(j)
                stage_K(j, nxt)
                stage_L(j)
            else:
                if nxt is not None:
                    stage_A_begin(nxt)
                    stage_A_elem(nxt, 0)
                    stage_A_elem(nxt, 1)
                    hooks = [lambda: (stage_A_pe(nxt, 0), stage_A_elem(nxt, 2)),
                             lambda: (stage_A_pe(nxt, 1), stage_A_elem(nxt, 3))]
                    stage_B(j, hooks)
                else:
                    stage_B(j)
                stage_C(j)
                if nxt is not None:
                    stage_A_pe(nxt, 2)
                    stage_A_pe(nxt, 3)
                if j == NHALO - 1:
                    u_margin()
                if j == 1:
                    mem_phase()

        for tc in range(4):
            nc.sync.wait_ge(out_sem[tc].sem, out_sem[tc].val)
        build_nc.stats = dict(ninst=dict(T.ninst), nwaits=T.nwaits, sbuf_left=nc.sbuf_bytes_remaining)
    return nc


def _consts():
    slopes = 2.0 ** (-8.0 * np.arange(1, 9, dtype=np.float64) / 8.0)
    a = np.arange(128)[:, None].astype(np.float64)
    c = np.arange(128)[None, :].astype(np.float64)
    E1 = np.zeros((128, 8, 2, 128), np.float64)
    E4 = np.zeros((128, 8, 2, 128), np.float64)
    E16 = np.zeros((128, 8, 5, 128), np.float64)
    for h in range(8):
        for E, d in ((E1, 1), (E4, 4)):
            sd = slopes[h] * d
            E[:, h, 0, :] = np.where(c <= a, np.exp(-sd * (128 + c - a)), 0.0)
            E[:, h, 1, :] = np.where(c >= a, np.exp(-sd * (c - a)), 0.0)
        sd = slopes[h] * 16
        ka, km_ = a // 4, a % 4
        cq, cm = c // 4, c % 4
        for dl in range(5):
            rel = 32 * dl + cq - ka
            ok = (km_ == cm) & (rel >= 0) & (rel <= 128)
            E16[:, h, dl, :] = np.where(ok, np.exp(-sd * np.where(ok, rel, 0.0)), 0.0)
    E16[:, :, 0, :] += E4[:, :, 1, :]
    E16[:, :, 1, :] += E4[:, :, 0, :]
    E1 = E1.reshape(128, 8, 2, 32, 4).transpose(0, 1, 2, 4, 3).reshape(128, 8, 2, 128)
    return np.ascontiguousarray(E1.astype(np.float32)), E4.astype(np.float32), E16.astype(np.float32)


def _prepare_inputs(x, mem, g_mix, w_in, g_mem, w_mem_kv, w_pool, pool_scale, w_out, g_ffn, w_ff1, w_ff2, g_final):
    f = lambda a: np.ascontiguousarray(np.asarray(a, dtype=np.float32))
    x, mem = f(x), f(mem)
    E1, E4, E16 = _consts()
    gbs = np.stack([np.broadcast_to(f(g).reshape(1, D), (128, D)) for g in (g_mix[0], g_ffn[0], g_final, g_mem[0])])
    gbs = np.ascontiguousarray(gbs)
    wp = f(w_pool)[0]
    wpool = np.zeros((128, 2, 128), np.float32)
    for ch in range(2):
        wpool[0:64, ch, 0:64] = wp[2 * ch]
        wpool[64:128, ch, 64:128] = wp[2 * ch + 1]
    ps = f(pool_scale)[0]
    wins = (2, 4, 8, 16)
    shared = dict(w_in=f(w_in)[0], w_mem_kv=f(w_mem_kv)[0], w_out=f(w_out)[0], w_ff1=f(w_ff1)[0], w_ff2=f(w_ff2)[0],
                  gbs=gbs, cE1=E1, cE16=E16, cwpool=wpool, cident=np.eye(128, dtype=np.float32))
    in_maps = []
    for core in range(8):
        b, half = core // 2, core % 2
        xh = np.zeros(((NHALO + NOWN) * TOK, D), np.float32)
        if half == 1:
            xh[:] = x[b, SEQ // 2 - NHALO * TOK:]
        else:
            xh[NHALO * TOK:] = x[b, :SEQ // 2]
        csm = np.zeros((128, 8), np.float32)
        csm[:, 0] = float(half)
        csm[0:64, 5] = 0.125
        csm[64:128, 6] = 0.125
        cpc = np.ones((128, 2, 16), np.float32)
        for ch in range(2):
            csm[:, 1 + ch] = ps[ch * 128:(ch + 1) * 128]
            for hf in range(2):
                w = wins[2 * ch + hf]
                csm[64 * hf:64 * hf + 64, 3 + ch] = 1.0 / w
                if half == 0:
                    t = np.arange(16)
                    cpc[64 * hf:64 * hf + 64, ch, :] = (w / np.minimum(t + 1, w))[None, :]
        m = dict(shared)
        m.update(xh=xh, mem=np.ascontiguousarray(mem[b]), csm=csm, cpc=cpc)
        in_maps.append(m)
    return in_maps


_NC_CACHE = {}


def kernel(x, mem, g_mix, w_in, g_mem, w_mem_kv, w_pool, pool_scale, w_out, g_ffn, w_ff1, w_ff2, g_final):
    in_maps = _prepare_inputs(x, mem, g_mix, w_in, g_mem, w_mem_kv, w_pool, pool_scale, w_out, g_ffn, w_ff1, w_ff2,
                              g_final)
    if "nc" not in _NC_CACHE:
        _NC_CACHE["nc"] = build_nc()
    nc = _NC_CACHE["nc"]
    res = run_bass_kernel_spmd(nc, in_maps, core_ids=list(range(8)))
    out = np.zeros((NB, SEQ, D), np.float32)
    for core in range(8):
        b, half = core // 2, core % 2
        out[b, half * (SEQ // 2):(half + 1) * (SEQ // 2)] = res.results[core]["out"]
    return out
```

```python
import numpy as np
from contextlib import ExitStack

import concourse.bass as bass
import concourse.mybir as mybir
from concourse.bass_utils import run_bass_kernel_spmd

F32 = mybir.dt.float32
BF16 = mybir.dt.bfloat16
AF = mybir.ActivationFunctionType
ALU = mybir.AluOpType

D = 1024
SEQ = 8192
NB = 4
TOK = 512
NHALO = 4
NOWN = 8
EPS = 1e-6
NRING = 3
KSLOTS = 5

P_KVM = 0
P_WIN = 1
P_WOUT = 5
P_FF1 = 7
P_FF2 = 15
NPIECES = 23
FFN_ORDER = [("ff1", 0), ("ff1", 1), ("ff2", 0), ("ff1", 2), ("ff2", 1), ("ff1", 3), ("ff2", 2), ("ff2", 3)]


class Buf:
    __slots__ = ("name", "w", "r")

    def __init__(self, name):
        self.name = name
        self.w = None
        self.r = []


class DmaSem:
    __slots__ = ("sem", "val")

    def __init__(self, sem):
        self.sem = sem
        self.val = 0


class Tracker:
    def __init__(self, nc, es):
        self.nc = nc
        self.eng = {"pe": nc.tensor, "act": nc.scalar, "dve": nc.vector,
                    "pool": nc.gpsimd, "sp": nc.sync}
        self.sem = {k: es.enter_context(nc.semaphore("sem_" + k)) for k in self.eng}
        self.cnt = {k: 0 for k in self.eng}
        self.waited = {}
        self.nwaits = 0
        self.ninst = {k: 0 for k in self.eng}

    def wait(self, e, tok):
        if tok is None:
            return
        sem, val, prod = tok
        if prod == "pe" and e == "pe":
            return
        key = (e, id(sem))
        if self.waited.get(key, 0) >= val:
            return
        self.waited[key] = val
        self.eng[e].wait_ge(sem, val)
        self.nwaits += 1

    def _pre(self, e, reads, writes):
        for b in reads:
            self.wait(e, b.w)
        for b in writes:
            self.wait(e, b.w)
            for t in b.r:
                self.wait(e, t)

    def _post(self, tok, reads, writes):
        for b in reads:
            b.r = [t for t in b.r if t[0] is not tok[0]] + [tok]
        for b in writes:
            b.w = tok
            b.r = []

    def op(self, e, fn, reads=(), writes=(), mark=True):
        self._pre(e, reads, writes)
        ins = fn()
        self.ninst[e] += 1
        if mark:
            self.cnt[e] += 1
            ins.then_inc(self.sem[e], 1)
            tok = (self.sem[e], self.cnt[e], e)
        else:
            tok = (self.sem[e], self.cnt[e] + 1, e)
        self._post(tok, reads, writes)
        return tok

    def dma(self, e, dsem, out, in_, reads=(), writes=(), **kw):
        self._pre(e, reads, writes)
        ins = self.eng[e].dma_start(out=out, in_=in_, **kw)
        self.ninst[e] += 1
        dsem.val += 16
        ins.then_inc(dsem.sem, 16)
        tok = (dsem.sem, dsem.val, "dma")
        self._post(tok, reads, writes)
        return tok


def build_nc(n_own=NOWN, debug=False):
    nc = bass.Bass("TRN2", target_bir_lowering=False)
    NT = NHALO + n_own

    def din(name, shape, dtype=F32):
        return nc.dram_tensor(name, shape, dtype, kind="ExternalInput").ap()

    xh = din("xh", [(NHALO + NOWN) * TOK, D])
    memd = din("mem", [256, D])
    w_in = din("w_in", [D, 2048])
    w_kvm = din("w_mem_kv", [D, 512])
    w_out = din("w_out", [D, D])
    w_ff1 = din("w_ff1", [D, 4096])
    w_ff2 = din("w_ff2", [4096, D])
    gbs = din("gbs", [4, 128, D])
    cE1 = din("cE1", [128, 8, 2, 128])
    cE16 = din("cE16", [128, 8, 5, 128])
    cwpool = din("cwpool", [128, 2, 128])
    csm = din("csm", [128, 8])
    cpc = din("cpc", [128, 2, 16])
    cident = din("cident", [128, 128])
    outd = nc.dram_tensor("out", [NOWN * TOK, D], F32, kind="ExternalOutput").ap()
    sc = nc.dram_tensor("wscratch", [NPIECES, 128, 4096], BF16).ap()
    dbg = {}
    if debug:
        for nm, shp in debug.items():
            dbg[nm] = nc.dram_tensor("dbg_" + nm, shp, F32, kind="ExternalOutput").ap()

    with ExitStack() as es:
        T = Tracker(nc, es)

        def sb(name, shape, dtype):
            return es.enter_context(nc.sbuf_tensor(name, shape, dtype))

        def psum(name, shape, dtype):
            return es.enter_context(nc.psum_tensor(name, shape, dtype))

        def dsem(name):
            return DmaSem(es.enter_context(nc.semaphore(name)))

        ring = [sb(f"ring{i}", [128, 4096], BF16) for i in range(NRING)]
        b_ring = [Buf(f"ring{i}") for i in range(NRING)]
        ring_sem = [dsem(f"ringsem{i}") for i in range(NRING)]
        KT = sb("KT", [128, 4, KSLOTS * TOK], BF16)
        b_KT = [Buf(f"KT{i}") for i in range(KSLOTS)]
        Vd1 = sb("Vd1", [128, 5, 4 * 192], BF16)
        b_Vd1 = [Buf(f"Vd1_{i}") for i in range(5)]
        Vd4 = sb("Vd4", [128, KSLOTS, 4, 4 * 192], BF16)
        b_Vd4 = [Buf(f"Vd4_{i}") for i in range(KSLOTS)]
        big = sb("big", [128, 8192], BF16)
        b_big = [Buf("bigA"), Buf("bigB")]
        aT = [big[:, 0:4096].rearrange("p (f t) -> p f t", f=8),
              big[:, 4096:8192].rearrange("p (f t) -> p f t", f=8)]
        QA = big[:, 0:2048].rearrange("p (c t) -> p c t", c=4)
        QB = big[:, 2048:4096].rearrange("p (c t) -> p c t", c=4)
        qmA = big[:, 4096:5120].rearrange("p (c t) -> p c t", c=2)
        qmB = big[:, 5120:6144].rearrange("p (c t) -> p c t", c=2)
        VT = big[:, 6144:8192].rearrange("p (c t) -> p c t", c=4)
        U = sb("U", [128, 2, 16 + TOK], F32)
        b_U = Buf("U")
        S1 = sb("S1", [128, 16 + TOK], F32)
        S2 = sb("S2", [128, 16 + TOK], F32)
        b_S1, b_S2 = Buf("S1"), Buf("S2")
        dT = sb("dT", [128, 2, TOK], BF16)
        b_dT = Buf("dT")
        yT = sb("yT", [128, 8, TOK], BF16)
        b_yT = [Buf(f"yT{i}") for i in range(8)]
        E1 = sb("E1", [128, 8, 2, 128], BF16)
        E16 = sb("E16", [128, 8, 5, 128], BF16)
        b_E = Buf("E")
        xa = [sb(f"xa{i}", [128, D], F32) for i in range(2)]
        b_xa = [Buf(f"xa{i}") for i in range(2)]
        xa_sem = [dsem(f"xasem{i}") for i in range(2)]
        x1 = [sb(f"x1_{i}", [128, D], F32) for i in range(4)]
        b_x1 = [Buf(f"x1_{i}") for i in range(4)]
        x1_sem = [dsem(f"x1sem{i}") for i in range(4)]
        out_sem = [dsem(f"outsem{i}") for i in range(4)]
        hbs = [sb(f"hb{i}", [128, D], BF16) for i in range(2)]
        b_hbs = [Buf(f"hb{i}") for i in range(2)]
        junk = sb("junk", [128, D], BF16)
        hT = sb("hT", [128, 8, TOK], BF16)
        b_hT = Buf("hT")
        h2T = sb("h2T", [128, 8, TOK], BF16)
        b_h2T = Buf("h2T")
        NPB = 5
        Pbuf = [sb(f"P{i}", [128, 4, 128], BF16) for i in range(NPB)]
        b_P = [Buf(f"P{i}") for i in range(NPB)]
        rec = [sb(f"rec{i}", [128, TOK], F32) for i in range(2)]
        b_rec = [Buf(f"rec{i}") for i in range(2)]
        gb_mix = sb("gb_mix", [128, D], F32)
        gb_ffn = sb("gb_ffn", [128, D], F32)
        gb_fin = sb("gb_fin", [128, D], F32)
        b_g = Buf("g")
        kmT = sb("kmT", [128, 2, 256], BF16)
        vmA = sb("vmA", [128, 2, 2 * 192], BF16)
        b_km, b_vm = Buf("km"), Buf("vm")
        wpool = sb("wpool", [128, 2, 128], BF16)
        csm_t = sb("csm_t", [128, 8], F32)
        cpc_t = sb("cpc_t", [128, 2, 16], F32)
        b_c = Buf("consts")
        ident = sb("ident", [128, 128], BF16)
        b_id = Buf("ident")
        stat = sb("stat", [128, 3, 8], F32)
        b_stat = [Buf(f"stat{i}") for i in range(8)]

        mm = [psum(f"mm{i}", [128, 512], F32) for i in range(2)]
        b_mm = [Buf(f"mm{i}") for i in range(2)]
        scb = [psum(f"sc{i}", [128, 4, 128], F32) for i in range(3)]
        b_sc = [Buf(f"sc{i}") for i in range(3)]
        scb = [t_[:] for t_ in scb] + [m_[:].rearrange("p (a b) -> p a b", a=4) for m_ in mm]
        b_sc = b_sc + b_mm
        NSC = 5
        ob = [psum(f"ob{i}", [128, 4, 128], F32) for i in range(2)]
        b_ob = [Buf(f"ob{i}") for i in range(2)]
        tp = psum("tp", [128, 1024], BF16)
        b_tp = Buf("tp")
        b_tph = [b_tp, b_tp]

        b_sc_piece = [Buf(f"scp{i}") for i in range(NPIECES)]
        conv_sem = [dsem(f"cv{i}") for i in range(NPIECES)]

        def conv(idx, src, pat_out, kw):
            T.dma("pool", conv_sem[idx], sc[idx].rearrange(pat_out, **kw), src, writes=[b_sc_piece[idx]])

        def conv_k(idx, wsrc, c0):
            conv(idx, wsrc[:, c0:c0 + 512].rearrange("(k p) c -> p k c", p=128), "p (k c) -> p k c", dict(k=8))

        hv_col = csm_t[:, 0:1]

        id_sem = dsem("idsem")
        T.dma("pool", id_sem, ident[:], cident[:, :], writes=[b_id])
        T.op("dve", lambda: nc.vector.memset(U[:], 0.0), writes=[b_U])
        T.op("dve", lambda: nc.vector.memset(vmA[:], 1.0), writes=[b_vm])

        g_sem = dsem("gsem")
        T.dma("sp", g_sem, gb_mix[:], gbs[0])
        T.dma("sp", g_sem, csm_t[:], csm[:, :])
        xa_state = dict(n=0)

        def load_xa_rows(src_rows):
            s_ = xa_state["n"] % 2
            xa_state["n"] += 1
            T.dma("sp", xa_sem[s_], xa[s_][:], src_rows, writes=[b_xa[s_]])
            return s_

        def load_xa(j, c):
            r0 = j * TOK + c * 128
            return load_xa_rows(xh[r0:r0 + 128, :])

        a_slots = {}

        def stage_A_begin(j):
            a_slots[j] = [load_xa(j, 0), load_xa(j, 1)]

        stage_A_begin(0)
        T.dma("sp", g_sem, cpc_t[:], cpc[:, :, :])
        T.dma("sp", g_sem, x1[0][:], gbs[3])
        T.dma("sp", g_sem, gb_ffn[:], gbs[1])
        T.dma("sp", g_sem, gb_fin[:], gbs[2])
        totg = (g_sem.sem, g_sem.val, "dma")
        b_g.w = totg
        b_x1[0].w = totg

        def direct_piece(dst2d, b_dst, dsem_, wsrc, c0, ncols=512):
            T.dma("pool", dsem_, dst2d.rearrange("p (k c) -> p k c", k=8),
                  wsrc[:, c0:c0 + ncols].rearrange("(k p) c -> p k c", p=128), writes=b_dst)

        halo_pieces = {}
        halo_sem = [dsem(f"halosem{i}") for i in range(NRING)]
        for pi, slot in ((1, 1), (2, 2), (3, 0)):
            direct_piece(ring[slot][:], [b_ring[slot]], halo_sem[slot], w_in, 512 * pi)
            halo_pieces[pi] = (ring[slot], b_ring[slot])
        kvm_sem = dsem("kvmsem")
        kvm_piece = yT[:].rearrange("p a t -> p (a t)")
        direct_piece(kvm_piece, b_yT, kvm_sem, w_kvm, 0)
        c_sem = dsem("csem")
        T.dma("pool", c_sem, E1[:], cE1[:, :, :, :])
        T.dma("pool", c_sem, E16[:], cE16[:, :, :, :])
        T.dma("pool", c_sem, wpool[:], cwpool[:, :, :])
        tot = (c_sem.sem, c_sem.val, "dma")
        b_E.w = tot
        b_c.w = tot
        for i in (0, 1, 2, 3):
            conv_k(P_WIN + i, w_in, 512 * i)
        for i in range(2):
            conv_k(P_WOUT + i, w_out, 512 * i)
        for kind, q in FFN_ORDER:
            for i in (2 * q, 2 * q + 1):
                if kind == "ff1":
                    conv_k(P_FF1 + i, w_ff1, 512 * i)
                else:
                    conv(P_FF2 + i, w_ff2[512 * i:512 * i + 512, :].rearrange("(f p) c -> p f c", p=128),
                         "p (f c) -> p f c", dict(f=4))

        seq = []
        for j in range(NT):
            if j < NHALO:
                pass
            else:
                seq += [P_WIN + 0, P_WIN + 1, P_WIN + 2, P_WIN + 3, P_WOUT, P_WOUT + 1]
                for kind, q in FFN_ORDER:
                    base = P_FF1 if kind == "ff1" else P_FF2
                    seq += [base + 2 * q, base + 2 * q + 1]
        rs = dict(issued=0, taken=0)

        def ring_issue():
            n = rs["issued"]
            if n >= len(seq):
                return
            s = n % NRING
            idx = seq[n]
            T.dma("sp", ring_sem[s], ring[s][:], sc[idx], reads=[b_sc_piece[idx]], writes=[b_ring[s]])
            rs["issued"] = n + 1

        def ring_take(expect):
            n = rs["taken"]
            assert seq[n] == expect, (n, seq[n], expect)
            while rs["issued"] <= n:
                ring_issue()
            rs["taken"] = n + 1
            s = n % NRING
            return ring[s], b_ring[s]


        cnt = dict(stat=0, mm=0, sc=0, P=0, tp=0, hb=0, rec=0)

        def rms_hb(src, b_src, gb, extra_reads=()):
            hi = cnt["hb"] % 2
            cnt["hb"] += 1
            dst_hb, b_dst = hbs[hi][:], b_hbs[hi]
            i = cnt["stat"] % 8
            cnt["stat"] += 1
            bs = b_stat[i]
            ss, ln, rs_ = stat[:, 0, i:i + 1], stat[:, 1, i:i + 1], stat[:, 2, i:i + 1]
            T.op("act", lambda: nc.scalar.activation(out=junk[:], in_=src, func=AF.Square, accum_out=ss),
                 reads=[b_src], writes=[bs])
            T.op("act", lambda: nc.scalar.activation(out=ln, in_=ss, func=AF.Ln, scale=1.0 / D, bias=EPS),
                 reads=[bs], writes=[bs])
            T.op("act", lambda: nc.scalar.activation(out=rs_, in_=ln, func=AF.Exp, scale=-0.5),
                 reads=[bs], writes=[bs])
            T.op("dve", lambda: nc.vector.scalar_tensor_tensor(out=dst_hb, in0=src, scalar=rs_, in1=gb,
                                                               op0=ALU.mult, op1=ALU.mult),
                 reads=[b_src, bs, b_g] + list(extra_reads), writes=[b_dst])
            return hbs[hi], b_dst

        def transpose_to(hb, b_hb, dst3, b_dst):
            for k in range(8):
                T.op("pe", lambda: nc.tensor.transpose(tp[:, k * 128:(k + 1) * 128], hb[:, k * 128:(k + 1) * 128],
                                                       ident[:]),
                     reads=[b_hb, b_id], writes=[b_tp], mark=(k == 7))
            T.op("dve", lambda: nc.vector.tensor_copy(out=dst3, in_=tp[:].rearrange("p (k t) -> p k t", k=8)),
                 reads=[b_tp], writes=[b_dst])

        def next_mm():
            i = cnt["mm"] % 2
            cnt["mm"] += 1
            return mm[i], b_mm[i]

        def mm_group(out_ap, b_out, pairs, reads):
            n = len(pairs)
            for i, (l, r) in enumerate(pairs):
                T.op("pe", lambda: nc.tensor.matmul(out_ap, lhsT=l, rhs=r, start=(i == 0), stop=(i == n - 1)),
                     reads=reads, writes=[b_out], mark=(i == n - 1))

        def mem_phase():
            pk = kvm_piece.rearrange("p (k c) -> p k c", k=8)
            mT, b_mT = aT[0], b_big[0]
            for c in range(2):
                s_ = load_xa_rows(memd[c * 128:(c + 1) * 128, :])
                h_, bh_ = rms_hb(xa[s_][:], b_xa[s_], x1[0][:], extra_reads=[b_x1[0]])
                transpose_to(h_, bh_, mT[:, :, c * 128:(c + 1) * 128], b_mT)
            for mp in range(2):
                m_, bm_ = next_mm()
                mm_group(m_[:, 0:256], bm_, [(pk[:, k, mp * 128:(mp + 1) * 128], mT[:, k, 0:256]) for k in range(8)],
                         b_yT + [b_mT])
                T.op("dve", lambda: nc.vector.tensor_copy(out=kmT[:, mp, :], in_=m_[:, 0:256]), reads=[bm_], writes=[b_km])
            for kc in range(2):
                m_, bm_ = next_mm()
                mm_group(m_[:, 0:256], bm_, [(mT[:, k, kc * 128:(kc + 1) * 128], pk[:, k, 256:512]) for k in range(8)],
                         b_yT + [b_mT])
                T.op("dve", lambda: nc.vector.tensor_copy(
                    out=vmA[:, kc, :].rearrange("p (c s d) -> p c s d", c=2, s=3)[:, :, 0::2, :],
                    in_=m_[:, 0:256].rearrange("p (c s d) -> p c s d", c=2, s=2)), reads=[bm_], writes=[b_vm])

        a_hb = {}

        def stage_A_elem(j, c):
            slots = a_slots[j]
            s = slots[c]
            a_hb[(j, c)] = rms_hb(xa[s][:], b_xa[s], gb_mix[:])
            if c + 2 < 4:
                slots.append(load_xa(j, c + 2))

        def stage_A_pe(j, c):
            h_, bh_ = a_hb.pop((j, c))
            hd, b_hd = hbuf(j)
            transpose_to(h_, bh_, hd[:, :, c * 128:(c + 1) * 128], b_hd)

        def stage_A_chunk(j, c):
            stage_A_elem(j, c)
            stage_A_pe(j, c)

        def stage_A(j):
            stage_A_begin(j)
            for c in range(4):
                stage_A_chunk(j, c)

        def hbuf(j):
            if j < NHALO and j % 2 == 1:
                return h2T, b_h2T
            return hT, b_hT

        def stage_B(j, hooks=None):
            own = j >= NHALO
            kslot = j % KSLOTS
            hsrc, b_hsrc = hbuf(j)
            plist = [1, 2] if j < NHALO - 1 else ([1, 2, 3] if j == NHALO - 1 else [0, 1, 2, 3])
            for ip, pi in enumerate(plist):
                if own:
                    piece, b_piece = ring_take(P_WIN + pi)
                else:
                    piece, b_piece = halo_pieces[pi]
                pk = piece[:].rearrange("p (k c) -> p k c", k=8)
                for ol in range(4):
                    if pi == 3 and (not own) and ol >= 2:
                        continue
                    m_, bm_ = next_mm()
                    mm_group(m_[:], bm_, [(pk[:, k, ol * 128:(ol + 1) * 128], hsrc[:, k, :]) for k in range(8)],
                             [b_piece, b_hsrc])
                    if pi == 0:
                        T.op("dve", lambda: nc.vector.tensor_scalar(
                            out=QA[:, ol, :].rearrange("p (r c) -> p r c", r=4),
                            in0=m_[:, :].rearrange("p (c r) -> p r c", r=4), scalar1=csm_t[:, 5:6], scalar2=None,
                            op0=ALU.mult),
                            reads=[bm_, b_g], writes=[b_big[0]])
                        T.op("dve", lambda: nc.vector.tensor_scalar(
                            out=QB[:, ol, :].rearrange("p (r c) -> p r c", r=4),
                            in0=m_[:, :].rearrange("p (c r) -> p r c", r=4), scalar1=csm_t[:, 6:7], scalar2=None,
                            op0=ALU.mult),
                            reads=[bm_, b_g], writes=[b_big[0]])
                    elif pi == 1:
                        T.op("dve", lambda: nc.vector.tensor_copy(out=KT[:, ol, kslot * TOK:(kslot + 1) * TOK], in_=m_[:]),
                             reads=[bm_], writes=[b_KT[kslot]])
                    elif pi == 2:
                        T.op("act", lambda: nc.scalar.copy(out=VT[:, ol, :], in_=m_[:]), reads=[bm_], writes=[b_big[1]])
                    elif ol < 2:
                        T.op("dve", lambda: nc.vector.tensor_copy(out=U[:, ol, 16:16 + TOK], in_=m_[:]),
                             reads=[bm_], writes=[b_U])
                    else:
                        mp = ol - 2
                        T.op("dve", lambda: nc.vector.tensor_scalar(
                            out=qmA[:, mp, :].rearrange("p (r c) -> p r c", r=4),
                            in0=m_[:, :].rearrange("p (c r) -> p r c", r=4), scalar1=csm_t[:, 5:6], scalar2=None,
                            op0=ALU.mult),
                            reads=[bm_, b_g], writes=[b_big[1]])
                        T.op("dve", lambda: nc.vector.tensor_scalar(
                            out=qmB[:, mp, :].rearrange("p (r c) -> p r c", r=4),
                            in0=m_[:, :].rearrange("p (c r) -> p r c", r=4), scalar1=csm_t[:, 6:7], scalar2=None,
                            op0=ALU.mult),
                            reads=[bm_, b_g], writes=[b_big[1]])
                if own:
                    ring_issue()
                if hooks is not None and ip < len(hooks):
                    hooks[ip]()

        vb_state = dict(n=0)

        def vblock(src_of_pair, dst, b_dst, ones_from_hv):
            hf = vb_state["n"] % 2
            vb_state["n"] += 1
            t_ = tp[:, hf * 512:(hf + 1) * 512]
            bt_ = b_tph[hf]
            for p_ in range(4):
                T.op("pe", lambda: nc.tensor.transpose(t_[:, p_ * 128:(p_ + 1) * 128], src_of_pair(p_), ident[:]),
                     reads=[b_big[1], b_id], writes=[bt_], mark=(p_ == 3))
            d4 = dst.rearrange("p (c s d) -> p c s d", c=4, s=3)
            if hf == 0:
                T.op("act", lambda: nc.scalar.copy(out=d4[:, :, 0::2, :],
                                                   in_=t_.rearrange("p (c s d) -> p c s d", c=4, s=2)),
                     reads=[bt_], writes=[b_dst])
            else:
                T.op("dve", lambda: nc.vector.tensor_copy(out=d4[:, :, 0::2, :],
                                                          in_=t_.rearrange("p (c s d) -> p c s d", c=4, s=2)),
                     reads=[bt_], writes=[b_dst])
            if ones_from_hv:
                T.op("dve", lambda: nc.vector.tensor_copy(out=d4[:, :, 1, :],
                                                          in_=hv_col.unsqueeze(1).to_broadcast([128, 4, 64])),
                     reads=[b_g], writes=[b_dst])
            else:
                T.op("dve", lambda: nc.vector.memset(d4[:, :, 1, :], 1.0), writes=[b_dst])

        def stage_C(j):
            own = j >= NHALO
            s4 = j % KSLOTS
            for r in range(4):
                vblock(lambda p_: VT[:, p_, :].rearrange("p (a s) -> p a s", s=4)[:, :, r],
                       Vd4[:, s4, r, :], b_Vd4[s4], not own)
            if j >= NHALO - 1:
                for b in range(4):
                    g = (4 * j + b) % 5
                    if j == NHALO - 1 and b < 3:
                        continue
                    vblock(lambda p_: VT[:, p_, b * 128:(b + 1) * 128], Vd1[:, g, :], b_Vd1[g], not own)

        def stage_attn(j, side_ops=()):
            batches = []
            cur = j % KSLOTS
            prv = (j - 1) % KSLOTS
            for h in range(8):
                pr, hh = h // 2, h % 2
                Qz = (QA if hh == 0 else QB)[:, pr, :]
                v0 = pr * 192 + 64 * hh
                o_ = ob[h % 2]
                for bt in range(2):
                    qk, pv, pvr = [], [], []
                    for qi in range(2):
                        qb = 2 * bt + qi
                        qv = Qz.rearrange("p (r c) -> p r c", r=4)[:, :, 32 * qb:32 * qb + 32]
                        gs, gp = (4 * j + qb) % 5, (4 * j + qb - 1) % 5
                        if qb == 0:
                            kprev = KT[:, pr, prv * TOK + 384: prv * TOK + 512]
                        else:
                            kprev = KT[:, pr, cur * TOK + (qb - 1) * 128: cur * TOK + qb * 128]
                        ksame = KT[:, pr, cur * TOK + qb * 128: cur * TOK + (qb + 1) * 128]
                        oo = o_[:, :, 32 * qb:32 * qb + 32]
                        qk.append((2 * qi, kprev, qv))
                        qk.append((2 * qi + 1, ksame, qv))
                        pv.append((2 * qi, Vd1[:, gp, v0:v0 + 128], oo))
                        pv.append((2 * qi + 1, Vd1[:, gs, v0:v0 + 128], oo))
                        pvr += [b_Vd1[gp], b_Vd1[gs]]
                    batches.append(dict(qk=qk, qk_reads=[b_KT[cur], b_KT[prv], b_big[0]],
                                        mask=E1[:, h, :, :].unsqueeze(1).to_broadcast([128, 2, 2, 128]), meng="dve",
                                        pv=pv, pv_reads=pvr, obank=h % 2,
                                        first=(bt == 0), last=False, post=None))
                for dl in range(5):
                    slot = (j - dl) % KSLOTS
                    qk, pv = [], []
                    for r in range(4):
                        qv = Qz[:, r * 128:(r + 1) * 128]
                        kv = KT[:, pr, slot * TOK:(slot + 1) * TOK].rearrange("p (a s) -> p a s", s=4)[:, :, r]
                        qk.append((r, kv, qv))
                        pv.append((r, Vd4[:, slot, r, v0:v0 + 128], o_[:, r, :]))
                    batches.append(dict(qk=qk, qk_reads=[b_KT[slot], b_big[0]],
                                        mask=E16[:, h, dl, :].unsqueeze(1).to_broadcast([128, 4, 128]),
                                        meng="dve",
                                        pv=pv, pv_reads=[b_Vd4[slot]], obank=h % 2,
                                        first=False, last=(dl == 4),
                                        post=(h // 2, hh) if dl == 4 else None))
            for mh in range(4):
                mp, hh = mh // 2, mh % 2
                Qz = (qmA if hh == 0 else qmB)[:, mp, :]
                v0 = mp * 192 + 64 * hh
                for kc in range(2):
                    batches.append(dict(qk=[(None, kmT[:, mp, kc * 128:(kc + 1) * 128], Qz)],
                                        qk_reads=[b_km, b_big[1]], mask=None, meng=None,
                                        pv=[(None, vmA[:, kc, v0:v0 + 128], None)],
                                        pv_reads=[b_vm], obank=mh % 2, first=(kc == 0), last=(kc == 1),
                                        post=(6 + mh // 2, hh) if kc == 1 else None))

            def emit_front(bt):
                si = cnt["sc"] % NSC
                cnt["sc"] += 1
                pi = cnt["P"] % NPB
                cnt["P"] += 1
                s_, bs_ = scb[si], b_sc[si]
                P_, bP_ = Pbuf[pi], b_P[pi]
                n = len(bt["qk"])
                for i, (sec, l, r) in enumerate(bt["qk"]):
                    o2 = s_[:, sec, :] if sec is not None else s_.rearrange("p a b -> p (a b)")
                    if len(r.shape) == 3:
                        o2 = o2.rearrange("p (a b) -> p a b", a=r.shape[1])
                    T.op("pe", lambda: nc.tensor.matmul(o2, lhsT=l, rhs=r, start=True, stop=True),
                         reads=bt["qk_reads"], writes=[bs_], mark=(i == n - 1))
                T.op("act", lambda: nc.scalar.activation(out=P_[:], in_=s_, func=AF.Exp), reads=[bs_], writes=[bP_])
                if bt["mask"] is not None:
                    m = bt["mask"]
                    Pv = P_[:].rearrange("p (a b) q -> p a b q", a=2) if len(m.shape) == 4 else P_[:]
                    if bt["meng"] == "pool":
                        T.op("pool", lambda: nc.gpsimd.tensor_tensor(out=Pv, in0=Pv, in1=m, op=ALU.mult),
                             reads=[bP_, b_E], writes=[bP_])
                    else:
                        T.op("dve", lambda: nc.vector.tensor_tensor(out=Pv, in0=Pv, in1=m, op=ALU.mult),
                             reads=[bP_, b_E], writes=[bP_])
                bt["P"] = (P_, bP_)

            def emit_back(bt):
                P_, bP_ = bt["P"]
                o_b = ob[bt["obank"]]
                bo_ = b_ob[bt["obank"]]
                n = len(bt["pv"])
                for i, (sec, l, oo) in enumerate(bt["pv"]):
                    if sec is None:
                        r_ = P_[:].rearrange("p a b -> p (a b)")
                        oo = o_b[:].rearrange("p a b -> p (a b)")
                    else:
                        r_ = P_[:, sec, :]
                        if len(oo.shape) == 3:
                            r_ = r_.rearrange("p (a b) -> p a b", a=oo.shape[1])
                    T.op("pe", lambda: nc.tensor.matmul(oo, lhsT=l, rhs=r_, start=(bt["first"] and i == 0),
                                                        stop=(bt["last"] and i == n - 1), skip_group_check=True),
                         reads=bt["pv_reads"] + [bP_], writes=[bo_], mark=(i == n - 1))
                if bt["post"] is None:
                    return
                ychunk, hh = bt["post"]
                lo, o2_ = 64 * hh, 64 - 64 * hh
                of = o_b[:].rearrange("p a b -> p (a b)")
                ri = cnt["rec"] % 2
                cnt["rec"] += 1
                T.op("act", lambda: nc.scalar.activation(out=rec[ri][lo:lo + 64, :], in_=of[o2_:o2_ + 64, :], func=AF.Ln),
                     reads=[bo_], writes=[b_rec[ri]])
                T.op("act", lambda: nc.scalar.activation(out=rec[ri][lo:lo + 64, :], in_=rec[ri][lo:lo + 64, :], func=AF.Exp,
                                                         scale=-1.0),
                     reads=[b_rec[ri]], writes=[b_rec[ri]])
                T.op("dve", lambda: nc.vector.tensor_tensor(out=yT[lo:lo + 64, ychunk, :], in0=of[lo:lo + 64, :],
                                                            in1=rec[ri][lo:lo + 64, :], op=ALU.mult),
                     reads=[bo_, b_rec[ri]], writes=[b_yT[ychunk]])

            LAG = 4
            side = list(side_ops)
            for i, bt in enumerate(batches):
                emit_front(bt)
                if i >= LAG:
                    emit_back(batches[i - LAG])
                if side and i % 3 == 2:
                    side.pop(0)()
            for bt in batches[-LAG:]:
                emit_back(bt)
            for op_ in side:
                op_()

        def stage_pool(j):
            first_own = (j == NHALO)
            W = 16 + TOK
            ops = []

            def add(out, in0, in1, rd, wr):
                ops.append(lambda out=out, in0=in0, in1=in1, rd=rd, wr=wr: T.op(
                    "dve", lambda: nc.vector.tensor_tensor(out=out, in0=in0, in1=in1, op=ALU.add), reads=rd, writes=wr))

            for ch in range(2):
                Uc = U[:, ch, :]
                add(S1[:, 1:W], Uc[:, 1:W], Uc[:, 0:W - 1], [b_U], [b_S1])
                if ch == 0:
                    add(S2[64:128, 3:W], S1[64:128, 3:W], S1[64:128, 1:W - 2], [b_S1], [b_S2])
                else:
                    add(S2[:, 3:W], S1[:, 3:W], S1[:, 1:W - 2], [b_S1], [b_S2])
                    add(S1[:, 7:W], S2[:, 7:W], S2[:, 3:W - 4], [b_S2], [b_S1])
                    add(S2[64:128, 15:W], S1[64:128, 15:W], S1[64:128, 7:W - 8], [b_S1], [b_S2])
                if first_own:
                    for src_, bsrc_, p0 in ((S1, b_S1, 0), (S2, b_S2, 64)):
                        ops.append(lambda src_=src_, bsrc_=bsrc_, p0=p0, ch=ch: T.op(
                            "dve", lambda: nc.vector.tensor_tensor(out=src_[p0:p0 + 64, 16:32], in0=src_[p0:p0 + 64, 16:32],
                                                                   in1=cpc_t[p0:p0 + 64, ch, :], op=ALU.mult),
                            reads=[b_g], writes=[bsrc_]))
                inv = csm_t[:, 3 + ch:4 + ch]
                for src_, bsrc_, p0 in ((S1, b_S1, 0), (S2, b_S2, 64)):
                    ops.append(lambda src_=src_, bsrc_=bsrc_, p0=p0, ch=ch, Uc=Uc, inv=inv: T.op(
                        "dve", lambda: nc.vector.scalar_tensor_tensor(out=dT[p0:p0 + 64, ch, :], in0=src_[p0:p0 + 64, 16:W],
                                                                      scalar=inv[p0:p0 + 64, :], in1=Uc[p0:p0 + 64, 16:W],
                                                                      op0=ALU.mult, op1=ALU.subtract),
                        reads=[bsrc_, b_U, b_g], writes=[b_dT]))
            ops.append(lambda: T.op("dve", lambda: nc.vector.tensor_copy(out=U[:, :, 0:16], in_=U[:, :, TOK:TOK + 16]),
                                    reads=[b_U], writes=[b_U]))
            return ops

        def stage_pool_mm(j):
            for ch in range(2):
                m_, bm_ = next_mm()
                mm_group(m_[:].rearrange("p (r c) -> p r c", r=4), bm_,
                         [(wpool[:, ch, :], dT[:, ch, :].rearrange("p (c r) -> p r c", r=4))], [b_c, b_dT])
                T.op("act", lambda: nc.scalar.activation(out=yT[:, 4 + ch, :], in_=m_[:], func=AF.Copy,
                                                         scale=csm_t[:, 1 + ch:2 + ch]),
                     reads=[bm_, b_g], writes=[b_yT[4 + ch]])

        def u_margin():
            T.op("dve", lambda: nc.vector.tensor_copy(out=U[:, :, 0:16], in_=U[:, :, TOK:TOK + 16]),
                 reads=[b_U], writes=[b_U])

        j_hb = {}

        def stage_J_elem(tc):
            j_hb[tc] = rms_hb(x1[tc][:], b_x1[tc], gb_ffn[:])

        def stage_J_pe(tc):
            h_, bh_ = j_hb.pop(tc)
            transpose_to(h_, bh_, h2T[:, :, tc * 128:(tc + 1) * 128], b_h2T)

        def stage_IJ(j):
            pcs = [ring_take(P_WOUT), ring_take(P_WOUT + 1)]
            for tc in range(4):
                for ch in range(2):
                    piece, b_piece = pcs[ch]
                    pk = piece[:].rearrange("p (k c) -> p k c", k=8)
                    m_, bm_ = next_mm()
                    for k in range(8):
                        T.op("pe", lambda: nc.tensor.matmul(m_[:], lhsT=yT[:, k, tc * 128:(tc + 1) * 128], rhs=pk[:, k, :],
                                                            start=(k == 0), stop=(k == 7)),
                             reads=[b_piece, b_yT[k]], writes=[bm_], mark=(k == 7))
                    xs = x1[tc][:, ch * 512:(ch + 1) * 512]
                    T.op("dve", lambda: nc.vector.tensor_tensor(out=xs, in0=xs, in1=m_[:], op=ALU.add),
                         reads=[bm_], writes=[b_x1[tc]])
                stage_J_elem(tc)
                if tc >= 1:
                    stage_J_pe(tc - 1)
            ring_issue()
            ring_issue()
            stage_J_pe(3)

        def ff1_quarter(q):
            a_, ba_ = aT[q % 2], b_big[q % 2]
            pcs = [ring_take(P_FF1 + 2 * q), ring_take(P_FF1 + 2 * q + 1)]
            for fl in range(8):
                piece, b_piece = pcs[fl // 4]
                pk = piece[:].rearrange("p (k c) -> p k c", k=8)
                c0 = (fl % 4) * 128
                m_, bm_ = next_mm()
                mm_group(m_[:], bm_, [(pk[:, k, c0:c0 + 128], h2T[:, k, :]) for k in range(8)], [b_piece, b_h2T])
                rt, brt = (S1, b_S1) if fl % 2 == 0 else (S2, b_S2)
                T.op("act", lambda: nc.scalar.activation(out=rt[:, 0:TOK], in_=m_[:], func=AF.Relu),
                     reads=[bm_], writes=[brt])
                T.op("dve", lambda: nc.vector.tensor_tensor(out=a_[:, fl, :], in0=rt[:, 0:TOK], in1=rt[:, 0:TOK],
                                                            op=ALU.mult),
                     reads=[brt], writes=[ba_])
                if fl % 4 == 3:
                    ring_issue()

        def ff2_quarter(q):
            a_, ba_ = aT[q % 2], b_big[q % 2]
            pcs = [ring_take(P_FF2 + 2 * q), ring_take(P_FF2 + 2 * q + 1)]
            for tc in range(4):
                for ch in range(2):
                    m_, bm_ = next_mm()
                    prs = []
                    for fl in range(8):
                        piece, b_piece = pcs[fl // 4]
                        pf = piece[:].rearrange("p (f c) -> p f c", f=4)
                        prs.append((a_[:, fl, tc * 128:(tc + 1) * 128], pf[:, fl % 4, ch * 512:(ch + 1) * 512]))
                    mm_group(m_[:], bm_, prs, [pcs[0][1], pcs[1][1], ba_])
                    xs = x1[tc][:, ch * 512:(ch + 1) * 512]
                    T.op("dve", lambda: nc.vector.tensor_tensor(out=xs, in0=xs, in1=m_[:], op=ALU.add),
                         reads=[bm_], writes=[b_x1[tc]])
            ring_issue()
            ring_issue()

        def stage_K(j, nxt):
            if nxt is not None:
                stage_A_begin(nxt)
            for kind, q in FFN_ORDER:
                if kind == "ff1":
                    if nxt is not None:
                        stage_A_elem(nxt, q)
                    ff1_quarter(q)
                    if nxt is not None:
                        stage_A_pe(nxt, q)
                else:
                    ff2_quarter(q)

        def stage_L(j):
            jo = j - NHALO
            ov = outd[jo * TOK:(jo + 1) * TOK, :].rearrange("(c r) d -> r c d", r=4)
            for tc in range(4):
                i = cnt["stat"] % 8
                cnt["stat"] += 1
                bs = b_stat[i]
                ss, ln, rs_ = stat[:, 0, i:i + 1], stat[:, 1, i:i + 1], stat[:, 2, i:i + 1]
                T.op("act", lambda: nc.scalar.activation(out=junk[:], in_=x1[tc][:], func=AF.Square, accum_out=ss),
                     reads=[b_x1[tc]], writes=[bs])
                T.op("act", lambda: nc.scalar.activation(out=ln, in_=ss, func=AF.Ln, scale=1.0 / D, bias=EPS),
                     reads=[bs], writes=[bs])
                T.op("act", lambda: nc.scalar.activation(out=rs_, in_=ln, func=AF.Exp, scale=-0.5),
                     reads=[bs], writes=[bs])
                T.op("dve", lambda: nc.vector.scalar_tensor_tensor(out=x1[tc][:], in0=x1[tc][:], scalar=rs_, in1=gb_fin[:],
                                                                   op0=ALU.mult, op1=ALU.mult),
                     reads=[bs, b_g], writes=[b_x1[tc]])
                T.dma("sp", out_sem[tc], ov[tc], x1[tc][:], reads=[b_x1[tc]])

        def load_x1(j):
            xv = xh[j * TOK:(j + 1) * TOK, :].rearrange("(c r) d -> r c d", r=4)
            for tc in range(4):
                T.dma("sp", x1_sem[tc], x1[tc][:], xv[tc], writes=[b_x1[tc]])

        for c in range(4):
            stage_A_chunk(0, c)
        for j in range(NT):
            own = j >= NHALO
            nxt = j + 1 if j + 1 < NT else None
            if own:
                load_x1(j)
                stage_B(j)
                stage_C(j)
                stage_attn(j, stage_pool(j))
                stage_pool_mm(j)
                stage_IJ(j)
                stage_K(j, nxt)
                stage_L(j)
            else:
                if nxt is not None:
                    stage_A_begin(nxt)
                    stage_A_elem(nxt, 0)
                    stage_A_elem(nxt, 1)
                    hooks = [lambda: (stage_A_pe(nxt, 0), stage_A_elem(nxt, 2)),
                             lambda: (stage_A_pe(nxt, 1), stage_A_elem(nxt, 3))]
                    stage_B(j, hooks)
                else:
                    stage_B(j)
                stage_C(j)
                if nxt is not None:
                    stage_A_pe(nxt, 2)
                    stage_A_pe(nxt, 3)
                if j == NHALO - 1:
                    u_margin()
                if j == 1:
                    mem_phase()

        for tc in range(4):
            nc.sync.wait_ge(out_sem[tc].sem, out_sem[tc].val)
        build_nc.stats = dict(ninst=dict(T.ninst), nwaits=T.nwaits, sbuf_left=nc.sbuf_bytes_remaining)
    return nc


def _consts():
    slopes = 2.0 ** (-8.0 * np.arange(1, 9, dtype=np.float64) / 8.0)
    a = np.arange(128)[:, None].astype(np.float64)
    c = np.arange(128)[None, :].astype(np.float64)
    E1 = np.zeros((128, 8, 2, 128), np.float64)
    E4 = np.zeros((128, 8, 2, 128), np.float64)
    E16 = np.zeros((128, 8, 5, 128), np.float64)
    for h in range(8):
        for E, d in ((E1, 1), (E4, 4)):
            sd = slopes[h] * d
            E[:, h, 0, :] = np.where(c <= a, np.exp(-sd * (128 + c - a)), 0.0)
            E[:, h, 1, :] = np.where(c >= a, np.exp(-sd * (c - a)), 0.0)
        sd = slopes[h] * 16
        ka, km_ = a // 4, a % 4
        cq, cm = c // 4, c % 4
        for dl in range(5):
            rel = 32 * dl + cq - ka
            ok = (km_ == cm) & (rel >= 0) & (rel <= 128)
            E16[:, h, dl, :] = np.where(ok, np.exp(-sd * np.where(ok, rel, 0.0)), 0.0)
    E16[:, :, 0, :] += E4[:, :, 1, :]
    E16[:, :, 1, :] += E4[:, :, 0, :]
    E1 = E1.reshape(128, 8, 2, 32, 4).transpose(0, 1, 2, 4, 3).reshape(128, 8, 2, 128)
    return np.ascontiguousarray(E1.astype(np.float32)), E4.astype(np.float32), E16.astype(np.float32)


def _prepare_inputs(x, mem, g_mix, w_in, g_mem, w_mem_kv, w_pool, pool_scale, w_out, g_ffn, w_ff1, w_ff2, g_final):
    f = lambda a: np.ascontiguousarray(np.asarray(a, dtype=np.float32))
    x, mem = f(x), f(mem)
    E1, E4, E16 = _consts()
    gbs = np.stack([np.broadcast_to(f(g).reshape(1, D), (128, D)) for g in (g_mix[0], g_ffn[0], g_final, g_mem[0])])
    gbs = np.ascontiguousarray(gbs)
    wp = f(w_pool)[0]
    wpool = np.zeros((128, 2, 128), np.float32)
    for ch in range(2):
        wpool[0:64, ch, 0:64] = wp[2 * ch]
        wpool[64:128, ch, 64:128] = wp[2 * ch + 1]
    ps = f(pool_scale)[0]
    wins = (2, 4, 8, 16)
    shared = dict(w_in=f(w_in)[0], w_mem_kv=f(w_mem_kv)[0], w_out=f(w_out)[0], w_ff1=f(w_ff1)[0], w_ff2=f(w_ff2)[0],
                  gbs=gbs, cE1=E1, cE16=E16, cwpool=wpool, cident=np.eye(128, dtype=np.float32))
    in_maps = []
    for core in range(8):
        b, half = core // 2, core % 2
        xh = np.zeros(((NHALO + NOWN) * TOK, D), np.float32)
        if half == 1:
            xh[:] = x[b, SEQ // 2 - NHALO * TOK:]
        else:
            xh[NHALO * TOK:] = x[b, :SEQ // 2]
        csm = np.zeros((128, 8), np.float32)
        csm[:, 0] = float(half)
        csm[0:64, 5] = 0.125
        csm[64:128, 6] = 0.125
        cpc = np.ones((128, 2, 16), np.float32)
        for ch in range(2):
            csm[:, 1 + ch] = ps[ch * 128:(ch + 1) * 128]
            for hf in range(2):
                w = wins[2 * ch + hf]
                csm[64 * hf:64 * hf + 64, 3 + ch] = 1.0 / w
                if half == 0:
                    t = np.arange(16)
                    cpc[64 * hf:64 * hf + 64, ch, :] = (w / np.minimum(t + 1, w))[None, :]
        m = dict(shared)
        m.update(xh=xh, mem=np.ascontiguousarray(mem[b]), csm=csm, cpc=cpc)
        in_maps.append(m)
    return in_maps


_NC_CACHE = {}


def kernel(x, mem, g_mix, w_in, g_mem, w_mem_kv, w_pool, pool_scale, w_out, g_ffn, w_ff1, w_ff2, g_final):
    in_maps = _prepare_inputs(x, mem, g_mix, w_in, g_mem, w_mem_kv, w_pool, pool_scale, w_out, g_ffn, w_ff1, w_ff2,
                              g_final)
    if "nc" not in _NC_CACHE:
        _NC_CACHE["nc"] = build_nc()
    nc = _NC_CACHE["nc"]
    res = run_bass_kernel_spmd(nc, in_maps, core_ids=list(range(8)))
    out = np.zeros((NB, SEQ, D), np.float32)
    for core in range(8):
        b, half = core // 2, core % 2
        out[b, half * (SEQ // 2):(half + 1) * (SEQ // 2)] = res.results[core]["out"]
    return out
```

```python
import numpy as np
from contextlib import ExitStack

import concourse.bass as bass
import concourse.mybir as mybir
from concourse.bass_utils import run_bass_kernel_spmd

F32 = mybir.dt.float32
BF16 = mybir.dt.bfloat16
AF = mybir.ActivationFunctionType
ALU = mybir.AluOpType

D = 1024
SEQ = 8192
NB = 4
TOK = 512
NHALO = 4
NOWN = 8
EPS = 1e-6
NRING = 3
KSLOTS = 5

P_KVM = 0
P_WIN = 1
P_WOUT = 5
P_FF1 = 7
P_FF2 = 15
NPIECES = 23
FFN_ORDER = [("ff1", 0), ("ff1", 1), ("ff2", 0), ("ff1", 2), ("ff2", 1), ("ff1", 3), ("ff2", 2), ("ff2", 3)]


class Buf:
    __slots__ = ("name", "w", "r")

    def __init__(self, name):
        self.name = name
        self.w = None
        self.r = []


class DmaSem:
    __slots__ = ("sem", "val")

    def __init__(self, sem):
        self.sem = sem
        self.val = 0


class Tracker:
    def __init__(self, nc, es):
        self.nc = nc
        self.eng = {"pe": nc.tensor, "act": nc.scalar, "dve": nc.vector,
                    "pool": nc.gpsimd, "sp": nc.sync}
        self.sem = {k: es.enter_context(nc.semaphore("sem_" + k)) for k in self.eng}
        self.cnt = {k: 0 for k in self.eng}
        self.waited = {}
        self.nwaits = 0
        self.ninst = {k: 0 for k in self.eng}

    def wait(self, e, tok):
        if tok is None:
            return
        sem, val, prod = tok
        if prod == "pe" and e == "pe":
            return
        key = (e, id(sem))
        if self.waited.get(key, 0) >= val:
            return
        self.waited[key] = val
        self.eng[e].wait_ge(sem, val)
        self.nwaits += 1

    def _pre(self, e, reads, writes):
        for b in reads:
            self.wait(e, b.w)
        for b in writes:
            self.wait(e, b.w)
            for t in b.r:
                self.wait(e, t)

    def _post(self, tok, reads, writes):
        for b in reads:
            b.r = [t for t in b.r if t[0] is not tok[0]] + [tok]
        for b in writes:
            b.w = tok
            b.r = []

    def op(self, e, fn, reads=(), writes=(), mark=True):
        self._pre(e, reads, writes)
        ins = fn()
        self.ninst[e] += 1
        if mark:
            self.cnt[e] += 1
            ins.then_inc(self.sem[e], 1)
            tok = (self.sem[e], self.cnt[e], e)
        else:
            tok = (self.sem[e], self.cnt[e] + 1, e)
        self._post(tok, reads, writes)
        return tok

    def dma(self, e, dsem, out, in_, reads=(), writes=(), **kw):
        self._pre(e, reads, writes)
        ins = self.eng[e].dma_start(out=out, in_=in_, **kw)
        self.ninst[e] += 1
        dsem.val += 16
        ins.then_inc(dsem.sem, 16)
        tok = (dsem.sem, dsem.val, "dma")
        self._post(tok, reads, writes)
        return tok


def build_nc(n_own=NOWN, debug=False):
    nc = bass.Bass("TRN2", target_bir_lowering=False)
    NT = NHALO + n_own

    def din(name, shape, dtype=F32):
        return nc.dram_tensor(name, shape, dtype, kind="ExternalInput").ap()

    xh = din("xh", [(NHALO + NOWN) * TOK, D])
    memd = din("mem", [256, D])
    w_in = din("w_in", [D, 2048])
    w_kvm = din("w_mem_kv", [D, 512])
    w_out = din("w_out", [D, D])
    w_ff1 = din("w_ff1", [D, 4096])
    w_ff2 = din("w_ff2", [4096, D])
    gbs = din("gbs", [4, 128, D])
    cE1 = din("cE1", [128, 8, 2, 128])
    cE16 = din("cE16", [128, 8, 5, 128])
    cwpool = din("cwpool", [128, 2, 128])
    csm = din("csm", [128, 8])
    cpc = din("cpc", [128, 2, 16])
    cident = din("cident", [128, 128])
    outd = nc.dram_tensor("out", [NOWN * TOK, D], F32, kind="ExternalOutput").ap()
    sc = nc.dram_tensor("wscratch", [NPIECES, 128, 4096], BF16).ap()
    dbg = {}
    if debug:
        for nm, shp in debug.items():
            dbg[nm] = nc.dram_tensor("dbg_" + nm, shp, F32, kind="ExternalOutput").ap()

    with ExitStack() as es:
        T = Tracker(nc, es)

        def sb(name, shape, dtype):
            return es.enter_context(nc.sbuf_tensor(name, shape, dtype))

        def psum(name, shape, dtype):
            return es.enter_context(nc.psum_tensor(name, shape, dtype))

        def dsem(name):
            return DmaSem(es.enter_context(nc.semaphore(name)))

        ring = [sb(f"ring{i}", [128, 4096], BF16) for i in range(NRING)]
        b_ring = [Buf(f"ring{i}") for i in range(NRING)]
        ring_sem = [dsem(f"ringsem{i}") for i in range(NRING)]
        KT = sb("KT", [128, 4, KSLOTS * TOK], BF16)
        b_KT = [Buf(f"KT{i}") for i in range(KSLOTS)]
        Vd1 = sb("Vd1", [128, 5, 4 * 192], BF16)
        b_Vd1 = [Buf(f"Vd1_{i}") for i in range(5)]
        Vd4 = sb("Vd4", [128, KSLOTS, 4, 4 * 192], BF16)
        b_Vd4 = [Buf(f"Vd4_{i}") for i in range(KSLOTS)]
        big = sb("big", [128, 8192], BF16)
        b_big = [Buf("bigA"), Buf("bigB")]
        aT = [big[:, 0:4096].rearrange("p (f t) -> p f t", f=8),
              big[:, 4096:8192].rearrange("p (f t) -> p f t", f=8)]
        QA = big[:, 0:2048].rearrange("p (c t) -> p c t", c=4)
        QB = big[:, 2048:4096].rearrange("p (c t) -> p c t", c=4)
        qmA = big[:, 4096:5120].rearrange("p (c t) -> p c t", c=2)
        qmB = big[:, 5120:6144].rearrange("p (c t) -> p c t", c=2)
        VT = big[:, 6144:8192].rearrange("p (c t) -> p c t", c=4)
        U = sb("U", [128, 2, 16 + TOK], F32)
        b_U = Buf("U")
        S1 = sb("S1", [128, 16 + TOK], F32)
        S2 = sb("S2", [128, 16 + TOK], F32)
        b_S1, b_S2 = Buf("S1"), Buf("S2")
        dT = sb("dT", [128, 2, TOK], BF16)
        b_dT = Buf("dT")
        yT = sb("yT", [128, 8, TOK], BF16)
        b_yT = [Buf(f"yT{i}") for i in range(8)]
        E1 = sb("E1", [128, 8, 2, 128], BF16)
        E16 = sb("E16", [128, 8, 5, 128], BF16)
        b_E = Buf("E")
        xa = [sb(f"xa{i}", [128, D], F32) for i in range(2)]
        b_xa = [Buf(f"xa{i}") for i in range(2)]
        xa_sem = [dsem(f"xasem{i}") for i in range(2)]
        x1 = [sb(f"x1_{i}", [128, D], F32) for i in range(4)]
        b_x1 = [Buf(f"x1_{i}") for i in range(4)]
        x1_sem = [dsem(f"x1sem{i}") for i in range(4)]
        out_sem = [dsem(f"outsem{i}") for i in range(4)]
        hbs = [sb(f"hb{i}", [128, D], BF16) for i in range(2)]
        b_hbs = [Buf(f"hb{i}") for i in range(2)]
        junk = sb("junk", [128, D], BF16)
        hT = sb("hT", [128, 8, TOK], BF16)
        b_hT = Buf("hT")
        h2T = sb("h2T", [128, 8, TOK], BF16)
        b_h2T = Buf("h2T")
        NPB = 5
        Pbuf = [sb(f"P{i}", [128, 4, 128], BF16) for i in range(NPB)]
        b_P = [Buf(f"P{i}") for i in range(NPB)]
        rec = [sb(f"rec{i}", [128, TOK], F32) for i in range(2)]
        b_rec = [Buf(f"rec{i}") for i in range(2)]
        gb_mix = sb("gb_mix", [128, D], F32)
        gb_ffn = sb("gb_ffn", [128, D], F32)
        gb_fin = sb("gb_fin", [128, D], F32)
        b_g = Buf("g")
        kmT = sb("kmT", [128, 2, 256], BF16)
        vmA = sb("vmA", [128, 2, 2 * 192], BF16)
        b_km, b_vm = Buf("km"), Buf("vm")
        wpool = sb("wpool", [128, 2, 128], BF16)
        csm_t = sb("csm_t", [128, 8], F32)
        cpc_t = sb("cpc_t", [128, 2, 16], F32)
        b_c = Buf("consts")
        ident = sb("ident", [128, 128], BF16)
        b_id = Buf("ident")
        stat = sb("stat", [128, 3, 8], F32)
        b_stat = [Buf(f"stat{i}") for i in range(8)]

        mm = [psum(f"mm{i}", [128, 512], F32) for i in range(2)]
        b_mm = [Buf(f"mm{i}") for i in range(2)]
        scb = [psum(f"sc{i}", [128, 4, 128], F32) for i in range(3)]
        b_sc = [Buf(f"sc{i}") for i in range(3)]
        scb = [t_[:] for t_ in scb] + [m_[:].rearrange("p (a b) -> p a b", a=4) for m_ in mm]
        b_sc = b_sc + b_mm
        NSC = 5
        ob = [psum(f"ob{i}", [128, 4, 128], F32) for i in range(2)]
        b_ob = [Buf(f"ob{i}") for i in range(2)]
        tp = psum("tp", [128, 1024], BF16)
        b_tp = Buf("tp")
        b_tph = [b_tp, b_tp]

        b_sc_piece = [Buf(f"scp{i}") for i in range(NPIECES)]
        conv_sem = [dsem(f"cv{i}") for i in range(NPIECES)]

        def conv(idx, src, pat_out, kw):
            T.dma("pool", conv_sem[idx], sc[idx].rearrange(pat_out, **kw), src, writes=[b_sc_piece[idx]])

        def conv_k(idx, wsrc, c0):
            conv(idx, wsrc[:, c0:c0 + 512].rearrange("(k p) c -> p k c", p=128), "p (k c) -> p k c", dict(k=8))

        hv_col = csm_t[:, 0:1]

        id_sem = dsem("idsem")
        T.dma("pool", id_sem, ident[:], cident[:, :], writes=[b_id])
        T.op("dve", lambda: nc.vector.memset(U[:], 0.0), writes=[b_U])
        T.op("dve", lambda: nc.vector.memset(vmA[:], 1.0), writes=[b_vm])

        g_sem = dsem("gsem")
        T.dma("sp", g_sem, gb_mix[:], gbs[0])
        T.dma("sp", g_sem, csm_t[:], csm[:, :])
        xa_state = dict(n=0)

        def load_xa_rows(src_rows):
            s_ = xa_state["n"] % 2
            xa_state["n"] += 1
            T.dma("sp", xa_sem[s_], xa[s_][:], src_rows, writes=[b_xa[s_]])
            return s_

        def load_xa(j, c):
            r0 = j * TOK + c * 128
            return load_xa_rows(xh[r0:r0 + 128, :])

        a_slots = {}

        def stage_A_begin(j):
            a_slots[j] = [load_xa(j, 0), load_xa(j, 1)]

        stage_A_begin(0)
        T.dma("sp", g_sem, cpc_t[:], cpc[:, :, :])
        T.dma("sp", g_sem, x1[0][:], gbs[3])
        T.dma("sp", g_sem, gb_ffn[:], gbs[1])
        T.dma("sp", g_sem, gb_fin[:], gbs[2])
        totg = (g_sem.sem, g_sem.val, "dma")
        b_g.w = totg
        b_x1[0].w = totg

        def direct_piece(dst2d, b_dst, dsem_, wsrc, c0, ncols=512):
            T.dma("pool", dsem_, dst2d.rearrange("p (k c) -> p k c", k=8),
                  wsrc[:, c0:c0 + ncols].rearrange("(k p) c -> p k c", p=128), writes=b_dst)

        halo_pieces = {}
        halo_sem = [dsem(f"halosem{i}") for i in range(NRING)]
        for pi, slot in ((1, 1), (2, 2), (3, 0)):
            direct_piece(ring[slot][:], [b_ring[slot]], halo_sem[slot], w_in, 512 * pi)
            halo_pieces[pi] = (ring[slot], b_ring[slot])
        kvm_sem = dsem("kvmsem")
        kvm_piece = yT[:].rearrange("p a t -> p (a t)")
        direct_piece(kvm_piece, b_yT, kvm_sem, w_kvm, 0)
        c_sem = dsem("csem")
        T.dma("pool", c_sem, E1[:], cE1[:, :, :, :])
        T.dma("pool", c_sem, E16[:], cE16[:, :, :, :])
        T.dma("pool", c_sem, wpool[:], cwpool[:, :, :])
        tot = (c_sem.sem, c_sem.val, "dma")
        b_E.w = tot
        b_c.w = tot
        for i in (0, 1, 2, 3):
            conv_k(P_WIN + i, w_in, 512 * i)
        for i in range(2):
            conv_k(P_WOUT + i, w_out, 512 * i)
        for kind, q in FFN_ORDER:
            for i in (2 * q, 2 * q + 1):
                if kind == "ff1":
                    conv_k(P_FF1 + i, w_ff1, 512 * i)
                else:
                    conv(P_FF2 + i, w_ff2[512 * i:512 * i + 512, :].rearrange("(f p) c -> p f c", p=128),
                         "p (f c) -> p f c", dict(f=4))

        seq = []
        for j in range(NT):
            if j < NHALO:
                pass
            else:
                seq += [P_WIN + 0, P_WIN + 1, P_WIN + 2, P_WIN + 3, P_WOUT, P_WOUT + 1]
                for kind, q in FFN_ORDER:
                    base = P_FF1 if kind == "ff1" else P_FF2
                    seq += [base + 2 * q, base + 2 * q + 1]
        rs = dict(issued=0, taken=0)

        def ring_issue():
            n = rs["issued"]
            if n >= len(seq):
                return
            s = n % NRING
            idx = seq[n]
            T.dma("sp", ring_sem[s], ring[s][:], sc[idx], reads=[b_sc_piece[idx]], writes=[b_ring[s]])
            rs["issued"] = n + 1

        def ring_take(expect):
            n = rs["taken"]
            assert seq[n] == expect, (n, seq[n], expect)
            while rs["issued"] <= n:
                ring_issue()
            rs["taken"] = n + 1
            s = n % NRING
            return ring[s], b_ring[s]


        cnt = dict(stat=0, mm=0, sc=0, P=0, tp=0, hb=0, rec=0)

        def rms_hb(src, b_src, gb, extra_reads=()):
            hi = cnt["hb"] % 2
            cnt["hb"] += 1
            dst_hb, b_dst = hbs[hi][:], b_hbs[hi]
            i = cnt["stat"] % 8
            cnt["stat"] += 1
            bs = b_stat[i]
            ss, ln, rs_ = stat[:, 0, i:i + 1], stat[:, 1, i:i + 1], stat[:, 2, i:i + 1]
            T.op("act", lambda: nc.scalar.activation(out=junk[:], in_=src, func=AF.Square, accum_out=ss),
                 reads=[b_src], writes=[bs])
            T.op("act", lambda: nc.scalar.activation(out=ln, in_=ss, func=AF.Ln, scale=1.0 / D, bias=EPS),
                 reads=[bs], writes=[bs])
            T.op("act", lambda: nc.scalar.activation(out=rs_, in_=ln, func=AF.Exp, scale=-0.5),
                 reads=[bs], writes=[bs])
            T.op("dve", lambda: nc.vector.scalar_tensor_tensor(out=dst_hb, in0=src, scalar=rs_, in1=gb,
                                                               op0=ALU.mult, op1=ALU.mult),
                 reads=[b_src, bs, b_g] + list(extra_reads), writes=[b_dst])
            return hbs[hi], b_dst

        def transpose_to(hb, b_hb, dst3, b_dst):
            for k in range(8):
                T.op("pe", lambda: nc.tensor.transpose(tp[:, k * 128:(k + 1) * 128], hb[:, k * 128:(k + 1) * 128],
                                                       ident[:]),
                     reads=[b_hb, b_id], writes=[b_tp], mark=(k == 7))
            T.op("dve", lambda: nc.vector.tensor_copy(out=dst3, in_=tp[:].rearrange("p (k t) -> p k t", k=8)),
                 reads=[b_tp], writes=[b_dst])

        def next_mm():
            i = cnt["mm"] % 2
            cnt["mm"] += 1
            return mm[i], b_mm[i]

        def mm_group(out_ap, b_out, pairs, reads):
            n = len(pairs)
            for i, (l, r) in enumerate(pairs):
                T.op("pe", lambda: nc.tensor.matmul(out_ap, lhsT=l, rhs=r, start=(i == 0), stop=(i == n - 1)),
                     reads=reads, writes=[b_out], mark=(i == n - 1))

        def mem_phase():
            pk = kvm_piece.rearrange("p (k c) -> p k c", k=8)
            mT, b_mT = aT[0], b_big[0]
            for c in range(2):
                s_ = load_xa_rows(memd[c * 128:(c + 1) * 128, :])
                h_, bh_ = rms_hb(xa[s_][:], b_xa[s_], x1[0][:], extra_reads=[b_x1[0]])
                transpose_to(h_, bh_, mT[:, :, c * 128:(c + 1) * 128], b_mT)
            for mp in range(2):
                m_, bm_ = next_mm()
                mm_group(m_[:, 0:256], bm_, [(pk[:, k, mp * 128:(mp + 1) * 128], mT[:, k, 0:256]) for k in range(8)],
                         b_yT + [b_mT])
                T.op("dve", lambda: nc.vector.tensor_copy(out=kmT[:, mp, :], in_=m_[:, 0:256]), reads=[bm_], writes=[b_km])
            for kc in range(2):
                m_, bm_ = next_mm()
                mm_group(m_[:, 0:256], bm_, [(mT[:, k, kc * 128:(kc + 1) * 128], pk[:, k, 256:512]) for k in range(8)],
                         b_yT + [b_mT])
                T.op("dve", lambda: nc.vector.tensor_copy(
                    out=vmA[:, kc, :].rearrange("p (c s d) -> p c s d", c=2, s=3)[:, :, 0::2, :],
                    in_=m_[:, 0:256].rearrange("p (c s d) -> p c s d", c=2, s=2)), reads=[bm_], writes=[b_vm])

        a_hb = {}

        def stage_A_elem(j, c):
            slots = a_slots[j]
            s = slots[c]
            a_hb[(j, c)] = rms_hb(xa[s][:], b_xa[s], gb_mix[:])
            if c + 2 < 4:
                slots.append(load_xa(j, c + 2))

        def stage_A_pe(j, c):
            h_, bh_ = a_hb.pop((j, c))
            hd, b_hd = hbuf(j)
            transpose_to(h_, bh_, hd[:, :, c * 128:(c + 1) * 128], b_hd)

        def stage_A_chunk(j, c):
            stage_A_elem(j, c)
            stage_A_pe(j, c)

        def stage_A(j):
            stage_A_begin(j)
            for c in range(4):
                stage_A_chunk(j, c)

        def hbuf(j):
            if j < NHALO and j % 2 == 1:
                return h2T, b_h2T
            return hT, b_hT

        def stage_B(j, hooks=None):
            own = j >= NHALO
            kslot = j % KSLOTS
            hsrc, b_hsrc = hbuf(j)
            plist = [1, 2] if j < NHALO - 1 else ([1, 2, 3] if j == NHALO - 1 else [0, 1, 2, 3])
            for ip, pi in enumerate(plist):
                if own:
                    piece, b_piece = ring_take(P_WIN + pi)
                else:
                    piece, b_piece = halo_pieces[pi]
                pk = piece[:].rearrange("p (k c) -> p k c", k=8)
                for ol in range(4):
                    if pi == 3 and (not own) and ol >= 2:
                        continue
                    m_, bm_ = next_mm()
                    mm_group(m_[:], bm_, [(pk[:, k, ol * 128:(ol + 1) * 128], hsrc[:, k, :]) for k in range(8)],
                             [b_piece, b_hsrc])
                    if pi == 0:
                        T.op("dve", lambda: nc.vector.tensor_scalar(
                            out=QA[:, ol, :].rearrange("p (r c) -> p r c", r=4),
                            in0=m_[:, :].rearrange("p (c r) -> p r c", r=4), scalar1=csm_t[:, 5:6], scalar2=None,
                            op0=ALU.mult),
                            reads=[bm_, b_g], writes=[b_big[0]])
                        T.op("dve", lambda: nc.vector.tensor_scalar(
                            out=QB[:, ol, :].rearrange("p (r c) -> p r c", r=4),
                            in0=m_[:, :].rearrange("p (c r) -> p r c", r=4), scalar1=csm_t[:, 6:7], scalar2=None,
                            op0=ALU.mult),
                            reads=[bm_, b_g], writes=[b_big[0]])
                    elif pi == 1:
                        T.op("dve", lambda: nc.vector.tensor_copy(out=KT[:, ol, kslot * TOK:(kslot + 1) * TOK], in_=m_[:]),
                             reads=[bm_], writes=[b_KT[kslot]])
                    elif pi == 2:
                        T.op("act", lambda: nc.scalar.copy(out=VT[:, ol, :], in_=m_[:]), reads=[bm_], writes=[b_big[1]])
                    elif ol < 2:
                        T.op("dve", lambda: nc.vector.tensor_copy(out=U[:, ol, 16:16 + TOK], in_=m_[:]),
                             reads=[bm_], writes=[b_U])
                    else:
                        mp = ol - 2
                        T.op("dve", lambda: nc.vector.tensor_scalar(
                            out=qmA[:, mp, :].rearrange("p (r c) -> p r c", r=4),
                            in0=m_[:, :].rearrange("p (c r) -> p r c", r=4), scalar1=csm_t[:, 5:6], scalar2=None,
                            op0=ALU.mult),
                            reads=[bm_, b_g], writes=[b_big[1]])
                        T.op("dve", lambda: nc.vector.tensor_scalar(
                            out=qmB[:, mp, :].rearrange("p (r c) -> p r c", r=4),
                            in0=m_[:, :].rearrange("p (c r) -> p r c", r=4), scalar1=csm_t[:, 6:7], scalar2=None,
                            op0=ALU.mult),
                            reads=[bm_, b_g], writes=[b_big[1]])
                if own:
                    ring_issue()
                if hooks is not None and ip < len(hooks):
                    hooks[ip]()

        vb_state = dict(n=0)

        def vblock(src_of_pair, dst, b_dst, ones_from_hv):
            hf = vb_state["n"] % 2
            vb_state["n"] += 1
            t_ = tp[:, hf * 512:(hf + 1) * 512]
            bt_ = b_tph[hf]
            for p_ in range(4):
                T.op("pe", lambda: nc.tensor.transpose(t_[:, p_ * 128:(p_ + 1) * 128], src_of_pair(p_), ident[:]),
                     reads=[b_big[1], b_id], writes=[bt_], mark=(p_ == 3))
            d4 = dst.rearrange("p (c s d) -> p c s d", c=4, s=3)
            if hf == 0:
                T.op("act", lambda: nc.scalar.copy(out=d4[:, :, 0::2, :],
                                                   in_=t_.rearrange("p (c s d) -> p c s d", c=4, s=2)),
                     reads=[bt_], writes=[b_dst])
            else:
                T.op("dve", lambda: nc.vector.tensor_copy(out=d4[:, :, 0::2, :],
                                                          in_=t_.rearrange("p (c s d) -> p c s d", c=4, s=2)),
                     reads=[bt_], writes=[b_dst])
            if ones_from_hv:
                T.op("dve", lambda: nc.vector.tensor_copy(out=d4[:, :, 1, :],
                                                          in_=hv_col.unsqueeze(1).to_broadcast([128, 4, 64])),
                     reads=[b_g], writes=[b_dst])
            else:
                T.op("dve", lambda: nc.vector.memset(d4[:, :, 1, :], 1.0), writes=[b_dst])

        def stage_C(j):
            own = j >= NHALO
            s4 = j % KSLOTS
            for r in range(4):
                vblock(lambda p_: VT[:, p_, :].rearrange("p (a s) -> p a s", s=4)[:, :, r],
                       Vd4[:, s4, r, :], b_Vd4[s4], not own)
            if j >= NHALO - 1:
                for b in range(4):
                    g = (4 * j + b) % 5
                    if j == NHALO - 1 and b < 3:
                        continue
                    vblock(lambda p_: VT[:, p_, b * 128:(b + 1) * 128], Vd1[:, g, :], b_Vd1[g], not own)

        def stage_attn(j, side_ops=()):
            batches = []
            cur = j % KSLOTS
            prv = (j - 1) % KSLOTS
            for h in range(8):
                pr, hh = h // 2, h % 2
                Qz = (QA if hh == 0 else QB)[:, pr, :]
                v0 = pr * 192 + 64 * hh
                o_ = ob[h % 2]
                for bt in range(2):
                    qk, pv, pvr = [], [], []
                    for qi in range(2):
                        qb = 2 * bt + qi
                        qv = Qz.rearrange("p (r c) -> p r c", r=4)[:, :, 32 * qb:32 * qb + 32]
                        gs, gp = (4 * j + qb) % 5, (4 * j + qb - 1) % 5
                        if qb == 0:
                            kprev = KT[:, pr, prv * TOK + 384: prv * TOK + 512]
                        else:
                            kprev = KT[:, pr, cur * TOK + (qb - 1) * 128: cur * TOK + qb * 128]
                        ksame = KT[:, pr, cur * TOK + qb * 128: cur * TOK + (qb + 1) * 128]
                        oo = o_[:, :, 32 * qb:32 * qb + 32]
                        qk.append((2 * qi, kprev, qv))
                        qk.append((2 * qi + 1, ksame, qv))
                        pv.append((2 * qi, Vd1[:, gp, v0:v0 + 128], oo))
                        pv.append((2 * qi + 1, Vd1[:, gs, v0:v0 + 128], oo))
                        pvr += [b_Vd1[gp], b_Vd1[gs]]
                    batches.append(dict(qk=qk, qk_reads=[b_KT[cur], b_KT[prv], b_big[0]],
                                        mask=E1[:, h, :, :].unsqueeze(1).to_broadcast([128, 2, 2, 128]), meng="dve",
                                        pv=pv, pv_reads=pvr, obank=h % 2,
                                        first=(bt == 0), last=False, post=None))
                for dl in range(5):
                    slot = (j - dl) % KSLOTS
                    qk, pv = [], []
                    for r in range(4):
                        qv = Qz[:, r * 128:(r + 1) * 128]
                        kv = KT[:, pr, slot * TOK:(slot + 1) * TOK].rearrange("p (a s) -> p a s", s=4)[:, :, r]
                        qk.append((r, kv, qv))
                        pv.append((r, Vd4[:, slot, r, v0:v0 + 128], o_[:, r, :]))
                    batches.append(dict(qk=qk, qk_reads=[b_KT[slot], b_big[0]],
                                        mask=E16[:, h, dl, :].unsqueeze(1).to_broadcast([128, 4, 128]),
                                        meng="dve",
                                        pv=pv, pv_reads=[b_Vd4[slot]], obank=h % 2,
                                        first=False, last=(dl == 4),
                                        post=(h // 2, hh) if dl == 4 else None))
            for mh in range(4):
                mp, hh = mh // 2, mh % 2
                Qz = (qmA if hh == 0 else qmB)[:, mp, :]
                v0 = mp * 192 + 64 * hh
                for kc in range(2):
                    batches.append(dict(qk=[(None, kmT[:, mp, kc * 128:(kc + 1) * 128], Qz)],
                                        qk_reads=[b_km, b_big[1]], mask=None, meng=None,
                                        pv=[(None, vmA[:, kc, v0:v0 + 128], None)],
                                        pv_reads=[b_vm], obank=mh % 2, first=(kc == 0), last=(kc == 1),
                                        post=(6 + mh // 2, hh) if kc == 1 else None))

            def emit_front(bt):
                si = cnt["sc"] % NSC
                cnt["sc"] += 1
                pi = cnt["P"] % NPB
                cnt["P"] += 1
                s_, bs_ = scb[si], b_sc[si]
                P_, bP_ = Pbuf[pi], b_P[pi]
                n = len(bt["qk"])
                for i, (sec, l, r) in enumerate(bt["qk"]):
                    o2 = s_[:, sec, :] if sec is not None else s_.rearrange("p a b -> p (a b)")
                    if len(r.shape) == 3:
                        o2 = o2.rearrange("p (a b) -> p a b", a=r.shape[1])
                    T.op("pe", lambda: nc.tensor.matmul(o2, lhsT=l, rhs=r, start=True, stop=True),
                         reads=bt["qk_reads"], writes=[bs_], mark=(i == n - 1))
                T.op("act", lambda: nc.scalar.activation(out=P_[:], in_=s_, func=AF.Exp), reads=[bs_], writes=[bP_])
                if bt["mask"] is not None:
                    m = bt["mask"]
                    Pv = P_[:].rearrange("p (a b) q -> p a b q", a=2) if len(m.shape) == 4 else P_[:]
                    if bt["meng"] == "pool":
                        T.op("pool", lambda: nc.gpsimd.tensor_tensor(out=Pv, in0=Pv, in1=m, op=ALU.mult),
                             reads=[bP_, b_E], writes=[bP_])
                    else:
                        T.op("dve", lambda: nc.vector.tensor_tensor(out=Pv, in0=Pv, in1=m, op=ALU.mult),
                             reads=[bP_, b_E], writes=[bP_])
                bt["P"] = (P_, bP_)

            def emit_back(bt):
                P_, bP_ = bt["P"]
                o_b = ob[bt["obank"]]
                bo_ = b_ob[bt["obank"]]
                n = len(bt["pv"])
                for i, (sec, l, oo) in enumerate(bt["pv"]):
                    if sec is None:
                        r_ = P_[:].rearrange("p a b -> p (a b)")
                        oo = o_b[:].rearrange("p a b -> p (a b)")
                    else:
                        r_ = P_[:, sec, :]
                        if len(oo.shape) == 3:
                            r_ = r_.rearrange("p (a b) -> p a b", a=oo.shape[1])
                    T.op("pe", lambda: nc.tensor.matmul(oo, lhsT=l, rhs=r_, start=(bt["first"] and i == 0),
                                                        stop=(bt["last"] and i == n - 1), skip_group_check=True),
                         reads=bt["pv_reads"] + [bP_], writes=[bo_], mark=(i == n - 1))
                if bt["post"] is None:
                    return
                ychunk, hh = bt["post"]
                lo, o2_ = 64 * hh, 64 - 64 * hh
                of = o_b[:].rearrange("p a b -> p (a b)")
                ri = cnt["rec"] % 2
                cnt["rec"] += 1
                T.op("act", lambda: nc.scalar.activation(out=rec[ri][lo:lo + 64, :], in_=of[o2_:o2_ + 64, :], func=AF.Ln),
                     reads=[bo_], writes=[b_rec[ri]])
                T.op("act", lambda: nc.scalar.activation(out=rec[ri][lo:lo + 64, :], in_=rec[ri][lo:lo + 64, :], func=AF.Exp,
                                                         scale=-1.0),
                     reads=[b_rec[ri]], writes=[b_rec[ri]])
                T.op("dve", lambda: nc.vector.tensor_tensor(out=yT[lo:lo + 64, ychunk, :], in0=of[lo:lo + 64, :],
                                                            in1=rec[ri][lo:lo + 64, :], op=ALU.mult),
                     reads=[bo_, b_rec[ri]], writes=[b_yT[ychunk]])

            LAG = 4
            side = list(side_ops)
            for i, bt in enumerate(batches):
                emit_front(bt)
                if i >= LAG:
                    emit_back(batches[i - LAG])
                if side and i % 3 == 2:
                    side.pop(0)()
            for bt in batches[-LAG:]:
                emit_back(bt)
            for op_ in side:
                op_()

        def stage_pool(j):
            first_own = (j == NHALO)
            W = 16 + TOK
            ops = []

            def add(out, in0, in1, rd, wr):
                ops.append(lambda out=out, in0=in0, in1=in1, rd=rd, wr=wr: T.op(
                    "dve", lambda: nc.vector.tensor_tensor(out=out, in0=in0, in1=in1, op=ALU.add), reads=rd, writes=wr))

            for ch in range(2):
                Uc = U[:, ch, :]
                add(S1[:, 1:W], Uc[:, 1:W], Uc[:, 0:W - 1], [b_U], [b_S1])
                if ch == 0:
                    add(S2[64:128, 3:W], S1[64:128, 3:W], S1[64:128, 1:W - 2], [b_S1], [b_S2])
                else:
                    add(S2[:, 3:W], S1[:, 3:W], S1[:, 1:W - 2], [b_S1], [b_S2])
                    add(S1[:, 7:W], S2[:, 7:W], S2[:, 3:W - 4], [b_S2], [b_S1])
                    add(S2[64:128, 15:W], S1[64:128, 15:W], S1[64:128, 7:W - 8], [b_S1], [b_S2])
                if first_own:
                    for src_, bsrc_, p0 in ((S1, b_S1, 0), (S2, b_S2, 64)):
                        ops.append(lambda src_=src_, bsrc_=bsrc_, p0=p0, ch=ch: T.op(
                            "dve", lambda: nc.vector.tensor_tensor(out=src_[p0:p0 + 64, 16:32], in0=src_[p0:p0 + 64, 16:32],
                                                                   in1=cpc_t[p0:p0 + 64, ch, :], op=ALU.mult),
                            reads=[b_g], writes=[bsrc_]))
                inv = csm_t[:, 3 + ch:4 + ch]
                for src_, bsrc_, p0 in ((S1, b_S1, 0), (S2, b_S2, 64)):
                    ops.append(lambda src_=src_, bsrc_=bsrc_, p0=p0, ch=ch, Uc=Uc, inv=inv: T.op(
                        "dve", lambda: nc.vector.scalar_tensor_tensor(out=dT[p0:p0 + 64, ch, :], in0=src_[p0:p0 + 64, 16:W],
                                                                      scalar=inv[p0:p0 + 64, :], in1=Uc[p0:p0 + 64, 16:W],
                                                                      op0=ALU.mult, op1=ALU.subtract),
                        reads=[bsrc_, b_U, b_g], writes=[b_dT]))
            ops.append(lambda: T.op("dve", lambda: nc.vector.tensor_copy(out=U[:, :, 0:16], in_=U[:, :, TOK:TOK + 16]),
                                    reads=[b_U], writes=[b_U]))
            return ops

        def stage_pool_mm(j):
            for ch in range(2):
                m_, bm_ = next_mm()
                mm_group(m_[:].rearrange("p (r c) -> p r c", r=4), bm_,
                         [(wpool[:, ch, :], dT[:, ch, :].rearrange("p (c r) -> p r c", r=4))], [b_c, b_dT])
                T.op("act", lambda: nc.scalar.activation(out=yT[:, 4 + ch, :], in_=m_[:], func=AF.Copy,
                                                         scale=csm_t[:, 1 + ch:2 + ch]),
                     reads=[bm_, b_g], writes=[b_yT[4 + ch]])

        def u_margin():
            T.op("dve", lambda: nc.vector.tensor_copy(out=U[:, :, 0:16], in_=U[:, :, TOK:TOK + 16]),
                 reads=[b_U], writes=[b_U])

        j_hb = {}

        def stage_J_elem(tc):
            j_hb[tc] = rms_hb(x1[tc][:], b_x1[tc], gb_ffn[:])

        def stage_J_pe(tc):
            h_, bh_ = j_hb.pop(tc)
            transpose_to(h_, bh_, h2T[:, :, tc * 128:(tc + 1) * 128], b_h2T)

        def stage_IJ(j):
            pcs = [ring_take(P_WOUT), ring_take(P_WOUT + 1)]
            for tc in range(4):
                for ch in range(2):
                    piece, b_piece = pcs[ch]
                    pk = piece[:].rearrange("p (k c) -> p k c", k=8)
                    m_, bm_ = next_mm()
                    for k in range(8):
                        T.op("pe", lambda: nc.tensor.matmul(m_[:], lhsT=yT[:, k, tc * 128:(tc + 1) * 128], rhs=pk[:, k, :],
                                                            start=(k == 0), stop=(k == 7)),
                             reads=[b_piece, b_yT[k]], writes=[bm_], mark=(k == 7))
                    xs = x1[tc][:, ch * 512:(ch + 1) * 512]
                    T.op("dve", lambda: nc.vector.tensor_tensor(out=xs, in0=xs, in1=m_[:], op=ALU.add),
                         reads=[bm_], writes=[b_x1[tc]])
                stage_J_elem(tc)
                if tc >= 1:
                    stage_J_pe(tc - 1)
            ring_issue()
            ring_issue()
            stage_J_pe(3)

        def ff1_quarter(q):
            a_, ba_ = aT[q % 2], b_big[q % 2]
            pcs = [ring_take(P_FF1 + 2 * q), ring_take(P_FF1 + 2 * q + 1)]
            for fl in range(8):
                piece, b_piece = pcs[fl // 4]
                pk = piece[:].rearrange("p (k c) -> p k c", k=8)
                c0 = (fl % 4) * 128
                m_, bm_ = next_mm()
                mm_group(m_[:], bm_, [(pk[:, k, c0:c0 + 128], h2T[:, k, :]) for k in range(8)], [b_piece, b_h2T])
                rt, brt = (S1, b_S1) if fl % 2 == 0 else (S2, b_S2)
                T.op("act", lambda: nc.scalar.activation(out=rt[:, 0:TOK], in_=m_[:], func=AF.Relu),
                     reads=[bm_], writes=[brt])
                T.op("dve", lambda: nc.vector.tensor_tensor(out=a_[:, fl, :], in0=rt[:, 0:TOK], in1=rt[:, 0:TOK],
                                                            op=ALU.mult),
                     reads=[brt], writes=[ba_])
                if fl % 4 == 3:
                    ring_issue()

        def ff2_quarter(q):
            a_, ba_ = aT[q % 2], b_big[q % 2]
            pcs = [ring_take(P_FF2 + 2 * q), ring_take(P_FF2 + 2 * q + 1)]
            for tc in range(4):
                for ch in range(2):
                    m_, bm_ = next_mm()
                    prs = []
                    for fl in range(8):
                        piece, b_piece = pcs[fl // 4]
                        pf = piece[:].rearrange("p (f c) -> p f c", f=4)
                        prs.append((a_[:, fl, tc * 128:(tc + 1) * 128], pf[:, fl % 4, ch * 512:(ch + 1) * 512]))
                    mm_group(m_[:], bm_, prs, [pcs[0][1], pcs[1][1], ba_])
                    xs = x1[tc][:, ch * 512:(ch + 1) * 512]
                    T.op("dve", lambda: nc.vector.tensor_tensor(out=xs, in0=xs, in1=m_[:], op=ALU.add),
                         reads=[bm_], writes=[b_x1[tc]])
            ring_issue()
            ring_issue()

        def stage_K(j, nxt):
            if nxt is not None:
                stage_A_begin(nxt)
            for kind, q in FFN_ORDER:
                if kind == "ff1":
                    if nxt is not None:
                        stage_A_elem(nxt, q)
                    ff1_quarter(q)
                    if nxt is not None:
                        stage_A_pe(nxt, q)
                else:
                    ff2_quarter(q)

        def stage_L(j):
            jo = j - NHALO
            ov = outd[jo * TOK:(jo + 1) * TOK, :].rearrange("(c r) d -> r c d", r=4)
            for tc in range(4):
                i = cnt["stat"] % 8
                cnt["stat"] += 1
                bs = b_stat[i]
                ss, ln, rs_ = stat[:, 0, i:i + 1], stat[:, 1, i:i + 1], stat[:, 2, i:i + 1]
                T.op("act", lambda: nc.scalar.activation(out=junk[:], in_=x1[tc][:], func=AF.Square, accum_out=ss),
                     reads=[b_x1[tc]], writes=[bs])
                T.op("act", lambda: nc.scalar.activation(out=ln, in_=ss, func=AF.Ln, scale=1.0 / D, bias=EPS),
                     reads=[bs], writes=[bs])
                T.op("act", lambda: nc.scalar.activation(out=rs_, in_=ln, func=AF.Exp, scale=-0.5),
                     reads=[bs], writes=[bs])
                T.op("dve", lambda: nc.vector.scalar_tensor_tensor(out=x1[tc][:], in0=x1[tc][:], scalar=rs_, in1=gb_fin[:],
                                                                   op0=ALU.mult, op1=ALU.mult),
                     reads=[bs, b_g], writes=[b_x1[tc]])
                T.dma("sp", out_sem[tc], ov[tc], x1[tc][:], reads=[b_x1[tc]])

        def load_x1(j):
            xv = xh[j * TOK:(j + 1) * TOK, :].rearrange("(c r) d -> r c d", r=4)
            for tc in range(4):
                T.dma("sp", x1_sem[tc], x1[tc][:], xv[tc], writes=[b_x1[tc]])

        for c in range(4):
            stage_A_chunk(0, c)
        for j in range(NT):
            own = j >= NHALO
            nxt = j + 1 if j + 1 < NT else None
            if own:
                stage_B(j)
                if j > NHALO:
                    stage_L(j - 1)
                load_x1(j)
                stage_C(j)
                stage_attn(j, stage_pool(j))
                stage_pool_mm(j)
                stage_IJ(j)
                stage_K(j, nxt)
                if nxt is None:
                    stage_L(j)
            else:
                if nxt is not None:
                    stage_A_begin(nxt)
                    stage_A_elem(nxt, 0)
                    stage_A_elem(nxt, 1)
                    hooks = [lambda: (stage_A_pe(nxt, 0), stage_A_elem(nxt, 2)),
                             lambda: (stage_A_pe(nxt, 1), stage_A_elem(nxt, 3))]
                    stage_B(j, hooks)
                else:
                    stage_B(j)
                stage_C(j)
                if nxt is not None:
                    stage_A_pe(nxt, 2)
                    stage_A_pe(nxt, 3)
                if j == NHALO - 1:
                    u_margin()
                if j == 1:
                    mem_phase()

        for tc in range(4):
            nc.sync.wait_ge(out_sem[tc].sem, out_sem[tc].val)
        build_nc.stats = dict(ninst=dict(T.ninst), nwaits=T.nwaits, sbuf_left=nc.sbuf_bytes_remaining)
    return nc


def _consts():
    slopes = 2.0 ** (-8.0 * np.arange(1, 9, dtype=np.float64) / 8.0)
    a = np.arange(128)[:, None].astype(np.float64)
    c = np.arange(128)[None, :].astype(np.float64)
    E1 = np.zeros((128, 8, 2, 128), np.float64)
    E4 = np.zeros((128, 8, 2, 128), np.float64)
    E16 = np.zeros((128, 8, 5, 128), np.float64)
    for h in range(8):
        for E, d in ((E1, 1), (E4, 4)):
            sd = slopes[h] * d
            E[:, h, 0, :] = np.where(c <= a, np.exp(-sd * (128 + c - a)), 0.0)
            E[:, h, 1, :] = np.where(c >= a, np.exp(-sd * (c - a)), 0.0)
        sd = slopes[h] * 16
        ka, km_ = a // 4, a % 4
        cq, cm = c // 4, c % 4
        for dl in range(5):
            rel = 32 * dl + cq - ka
            ok = (km_ == cm) & (rel >= 0) & (rel <= 128)
            E16[:, h, dl, :] = np.where(ok, np.exp(-sd * np.where(ok, rel, 0.0)), 0.0)
    E16[:, :, 0, :] += E4[:, :, 1, :]
    E16[:, :, 1, :] += E4[:, :, 0, :]
    E1 = E1.reshape(128, 8, 2, 32, 4).transpose(0, 1, 2, 4, 3).reshape(128, 8, 2, 128)
    return np.ascontiguousarray(E1.astype(np.float32)), E4.astype(np.float32), E16.astype(np.float32)


def _prepare_inputs(x, mem, g_mix, w_in, g_mem, w_mem_kv, w_pool, pool_scale, w_out, g_ffn, w_ff1, w_ff2, g_final):
    f = lambda a: np.ascontiguousarray(np.asarray(a, dtype=np.float32))
    x, mem = f(x), f(mem)
    E1, E4, E16 = _consts()
    gbs = np.stack([np.broadcast_to(f(g).reshape(1, D), (128, D)) for g in (g_mix[0], g_ffn[0], g_final, g_mem[0])])
    gbs = np.ascontiguousarray(gbs)
    wp = f(w_pool)[0]
    wpool = np.zeros((128, 2, 128), np.float32)
    for ch in range(2):
        wpool[0:64, ch, 0:64] = wp[2 * ch]
        wpool[64:128, ch, 64:128] = wp[2 * ch + 1]
    ps = f(pool_scale)[0]
    wins = (2, 4, 8, 16)
    shared = dict(w_in=f(w_in)[0], w_mem_kv=f(w_mem_kv)[0], w_out=f(w_out)[0], w_ff1=f(w_ff1)[0], w_ff2=f(w_ff2)[0],
                  gbs=gbs, cE1=E1, cE16=E16, cwpool=wpool, cident=np.eye(128, dtype=np.float32))
    in_maps = []
    for core in range(8):
        b, half = core // 2, core % 2
        xh = np.zeros(((NHALO + NOWN) * TOK, D), np.float32)
        if half == 1:
            xh[:] = x[b, SEQ // 2 - NHALO * TOK:]
        else:
            xh[NHALO * TOK:] = x[b, :SEQ // 2]
        csm = np.zeros((128, 8), np.float32)
        csm[:, 0] = float(half)
        csm[0:64, 5] = 0.125
        csm[64:128, 6] = 0.125
        cpc = np.ones((128, 2, 16), np.float32)
        for ch in range(2):
            csm[:, 1 + ch] = ps[ch * 128:(ch + 1) * 128]
            for hf in range(2):
                w = wins[2 * ch + hf]
                csm[64 * hf:64 * hf + 64, 3 + ch] = 1.0 / w
                if half == 0:
                    t = np.arange(16)
                    cpc[64 * hf:64 * hf + 64, ch, :] = (w / np.minimum(t + 1, w))[None, :]
        m = dict(shared)
        m.update(xh=xh, mem=np.ascontiguousarray(mem[b]), csm=csm, cpc=cpc)
        in_maps.append(m)
    return in_maps


_NC_CACHE = {}


def kernel(x, mem, g_mix, w_in, g_mem, w_mem_kv, w_pool, pool_scale, w_out, g_ffn, w_ff1, w_ff2, g_final):
    in_maps = _prepare_inputs(x, mem, g_mix, w_in, g_mem, w_mem_kv, w_pool, pool_scale, w_out, g_ffn, w_ff1, w_ff2,
                              g_final)
    if "nc" not in _NC_CACHE:
        _NC_CACHE["nc"] = build_nc()
    nc = _NC_CACHE["nc"]
    res = run_bass_kernel_spmd(nc, in_maps, core_ids=list(range(8)))
    out = np.zeros((NB, SEQ, D), np.float32)
    for core in range(8):
        b, half = core // 2, core % 2
        out[b, half * (SEQ // 2):(half + 1) * (SEQ // 2)] = res.results[core]["out"]
    return out
```
